# Optimizing a Trainium2 kernel written in Bass

```python
import jax, jax.numpy as jnp
from jax import lax
import numpy as np

D_MODEL = 4096
BATCH = 4
SEQ = 2048
DEPTH = 4

N_MIXERS = 2
MLA_HEADS = 32
Q_LORA = 1024
KV_LORA = 512
NOPE_DIM = 128
ROPE_DIM = 64
V_DIM = 128
QK_DIM = NOPE_DIM + ROPE_DIM
MLA_WIDTH = MLA_HEADS * V_DIM
MLA_IN = Q_LORA + KV_LORA + ROPE_DIM + MLA_WIDTH
ROPE_THETA = 10000.0
SB_HEADS = 32
SB_DIM = 128
SB_WIDTH = SB_HEADS * SB_DIM
SB_IN = 4 * SB_WIDTH
Q_BLOCK = 128
EPS = 1e-6
N_MLA = (DEPTH + 1) // 2
N_SB = DEPTH // 2

kernel_name = "hybrid_mla_stickbreaking_gated"


def rms_norm(x, g):
    xf = x.astype(jnp.float32)
    y = xf * lax.rsqrt(jnp.mean(xf * xf, axis=-1, keepdims=True) + EPS)
    return (y * g.astype(jnp.float32)).astype(x.dtype)


def rope_tables(seq):
    inv = 1.0 / (ROPE_THETA ** (jnp.arange(0, ROPE_DIM, 2, dtype=jnp.float32) / ROPE_DIM))
    ang = jnp.arange(seq, dtype=jnp.float32)[:, None] * inv[None, :]
    return jnp.cos(ang), jnp.sin(ang)


def apply_rope(x, cos, sin):
    x1, x2 = jnp.split(x, 2, axis=-1)
    c = cos.astype(x.dtype)
    s = sin.astype(x.dtype)
    return jnp.concatenate([x1 * c - x2 * s, x1 * s + x2 * c], axis=-1)


def to_blocks(t):
    b, s, h, d = t.shape
    return jnp.moveaxis(t.reshape(b, s // Q_BLOCK, Q_BLOCK, h, d), 1, 0)


def from_blocks(t):
    nb, b, qb, h, d = t.shape
    return jnp.moveaxis(t, 0, 1).reshape(b, nb * qb, h * d)


def softmax_attend(q, k, v):
    seq = q.shape[1]
    nb = seq // Q_BLOCK
    scale = QK_DIM ** -0.5
    kpos = jnp.arange(seq)

    def block(args):
        qblk, i = args
        qpos = i * Q_BLOCK + jnp.arange(Q_BLOCK)
        s = jnp.einsum('bqhd,bkhd->bhqk', qblk, k).astype(jnp.float32) * scale
        s = jnp.where(kpos[None, :] <= qpos[:, None], s, -jnp.inf)
        p = jax.nn.softmax(s, axis=-1).astype(v.dtype)
        return jnp.einsum('bhqk,bkhd->bqhd', p, v)

    out = lax.map(block, (to_blocks(q), jnp.arange(nb)))
    return from_blocks(out)


def stick_breaking_attend(q, k, v):
    seq = q.shape[1]
    nb = seq // Q_BLOCK
    scale = SB_DIM ** -0.5
    kpos = jnp.arange(seq)

    def block(args):
        qblk, i = args
        qpos = i * Q_BLOCK + jnp.arange(Q_BLOCK)
        z = jnp.einsum('bqhd,bkhd->bhqk', qblk, k).astype(jnp.float32) * scale
        mask = kpos[None, :] < qpos[:, None]
        log_beta = jax.nn.log_sigmoid(z)
        log_1m = jnp.where(mask, jax.nn.log_sigmoid(-z), 0.0)
        suffix = lax.cumsum(log_1m, axis=3, reverse=True) - log_1m
        w = jnp.where(mask, jnp.exp(log_beta + suffix), 0.0).astype(v.dtype)
        return jnp.einsum('bhqk,bkhd->bqhd', w, v)

    out = lax.map(block, (to_blocks(q), jnp.arange(nb)))
    return from_blocks(out)


def mla_layer(x, norm_g, w_in, q_a_g, w_qb, kv_a_g, w_kvb, q_norm_g, k_norm_g, w_out, cos, sin):
    b, s, _ = x.shape
    h = rms_norm(x, norm_g)
    proj = h @ w_in
    q_lat, kv_lat, k_pe, gate = jnp.split(
        proj, [Q_LORA, Q_LORA + KV_LORA, Q_LORA + KV_LORA + ROPE_DIM], axis=-1)
    q = (rms_norm(q_lat, q_a_g) @ w_qb).reshape(b, s, MLA_HEADS, QK_DIM)
    kv = (rms_norm(kv_lat, kv_a_g) @ w_kvb).reshape(b, s, MLA_HEADS, NOPE_DIM + V_DIM)
    k_nope, v = jnp.split(kv, [NOPE_DIM], axis=-1)
    k_pe_h = jnp.broadcast_to(k_pe[:, :, None, :], (b, s, MLA_HEADS, ROPE_DIM))
    k = jnp.concatenate([k_nope, k_pe_h], axis=-1)
    q = rms_norm(q, q_norm_g)
    k = rms_norm(k, k_norm_g)
    c, sn = cos[:, None, :], sin[:, None, :]
    q = jnp.concatenate([q[..., :NOPE_DIM], apply_rope(q[..., NOPE_DIM:], c, sn)], axis=-1)
    k = jnp.concatenate([k[..., :NOPE_DIM], apply_rope(k[..., NOPE_DIM:], c, sn)], axis=-1)
    o = softmax_attend(q, k, v)
    return x + (o * jax.nn.silu(gate)) @ w_out


def sb_layer(x, norm_g, w_in, w_out):
    b, s, _ = x.shape
    h = rms_norm(x, norm_g)
    q, k, v, gate = jnp.split(h @ w_in, 4, axis=-1)
    q = q.reshape(b, s, SB_HEADS, SB_DIM)
    k = k.reshape(b, s, SB_HEADS, SB_DIM)
    v = v.reshape(b, s, SB_HEADS, SB_DIM)
    o = stick_breaking_attend(q, k, v)
    return x + (o * jax.nn.silu(gate)) @ w_out


def setup_inputs(seed: int = 0) -> dict:
    key = jax.random.key(seed)
    ks = jax.random.split(key, 16)

    def w(k, shape, fan_in):
        return jax.random.normal(k, shape, jnp.float32) * (fan_in ** -0.5)

    def gain(k, shape):
        return 1.0 + 0.02 * jax.random.normal(k, shape, jnp.float32)

    return {
        "x": jax.random.normal(ks[0], (BATCH, SEQ, D_MODEL), jnp.float32),
        "mla_norm_g": gain(ks[1], (N_MLA, D_MODEL)),
        "mla_w_in": w(ks[2], (N_MLA, D_MODEL, MLA_IN), D_MODEL),
        "mla_q_a_g": gain(ks[3], (N_MLA, Q_LORA)),
        "mla_w_qb": w(ks[4], (N_MLA, Q_LORA, MLA_HEADS * QK_DIM), Q_LORA),
        "mla_kv_a_g": gain(ks[5], (N_MLA, KV_LORA)),
        "mla_w_kvb": w(ks[6], (N_MLA, KV_LORA, MLA_HEADS * (NOPE_DIM + V_DIM)), KV_LORA),
        "mla_q_norm_g": gain(ks[7], (N_MLA, QK_DIM)),
        "mla_k_norm_g": gain(ks[8], (N_MLA, QK_DIM)),
        "mla_w_out": w(ks[9], (N_MLA, MLA_WIDTH, D_MODEL), MLA_WIDTH),
        "sb_norm_g": gain(ks[10], (N_SB, D_MODEL)),
        "sb_w_in": w(ks[11], (N_SB, D_MODEL, SB_IN), D_MODEL),
        "sb_w_out": w(ks[12], (N_SB, SB_WIDTH, D_MODEL), SB_WIDTH),
    }


def reference(x, mla_norm_g, mla_w_in, mla_q_a_g, mla_w_qb, mla_kv_a_g, mla_w_kvb,
              mla_q_norm_g, mla_k_norm_g, mla_w_out, sb_norm_g, sb_w_in, sb_w_out):
    cos, sin = rope_tables(x.shape[1])
    h = x
    for i in range(DEPTH):
        j = i // N_MIXERS
        if i % N_MIXERS == 0:
            h = mla_layer(h, mla_norm_g[j], mla_w_in[j], mla_q_a_g[j], mla_w_qb[j],
                          mla_kv_a_g[j], mla_w_kvb[j], mla_q_norm_g[j], mla_k_norm_g[j],
                          mla_w_out[j], cos, sin)
        else:
            h = sb_layer(h, sb_norm_g[j], sb_w_in[j], sb_w_out[j])
    return h
```

```python
import math
from contextlib import ExitStack

import numpy as np
import ml_dtypes
import concourse.bass as bass
import concourse.mybir as mybir
from concourse.bass_utils import run_bass_kernel_spmd

F32 = mybir.dt.float32
BF16 = mybir.dt.bfloat16
F32R = mybir.dt.float32r
AF = mybir.ActivationFunctionType
ALU = mybir.AluOpType

D = 4096
T = 1024
S_FULL = 2048
NH = 32
EPS = 1e-6
NEG = -30000.0
RG = [[0, 1], [2, 3], [4, 5], [6, 7]]
GPOS = [0, 2, 3, 1]
WELEMS = 8192
NWBUF = 3

CF = {}
_off = 0
for _name, _w in [("g0", 32), ("g1", 32), ("g2", 32), ("g3", 32),
                  ("qa0", 8), ("qa1", 8), ("kva0", 4), ("kva1", 4),
                  ("qn0", 2), ("qn1", 2), ("kn0", 2), ("kn1", 2),
                  ("tri", 128), ("ones", 128), ("rT", 64),
                  ("cos", 1024), ("sin", 1024)]:
    CF[_name] = (_off, _w)
    _off += _w
CFW = _off
CB = {"ident": (0, 128), "nident": (128, 128), "onesb": (256, 128)}
CBW = 384


class Tok:
    __slots__ = ("sem", "val", "eng", "inc")

    def __init__(self, sem, val, eng, inc):
        self.sem, self.val, self.eng, self.inc = sem, val, eng, inc


class Res:
    __slots__ = ("name", "last_w", "readers")

    def __init__(self, name):
        self.name = name
        self.last_w = None
        self.readers = {}


class Sched:
    ENGS = ("pe", "act", "dve", "pool", "sp")

    def __init__(self, nc, stack):
        self.nc = nc
        self.stack = stack
        self.streams = {e: [] for e in self.ENGS}
        self.esem = {e: stack.enter_context(nc.semaphore("s_" + e)) for e in self.ENGS}
        self.ecount = {e: 0 for e in self.ENGS}
        self.waited = {e: {} for e in self.ENGS}
        self.semcount = {}
        self.final = []
        self.nsem = 0

    def new_sem(self, name):
        self.nsem += 1
        return self.stack.enter_context(self.nc.semaphore(f"{name}_{self.nsem}"))

    def op(self, eng, fn, reads=(), writes=(), dsem=None, inc=16):
        deps = []
        for r in reads:
            if r.last_w is not None:
                deps.append((r.last_w, True))
        for w in writes:
            if w.last_w is not None:
                deps.append((w.last_w, False))
            for t in w.readers.values():
                deps.append((t, False))
        waits = {}
        for t, raw in deps:
            if t.eng == eng and t.inc == 1 and t.sem is self.esem[eng]:
                if not (raw and eng in ("act", "dve", "pool")):
                    continue
            k = id(t.sem)
            if k not in waits or waits[k][1] < t.val:
                waits[k] = (t.sem, t.val)
        wl = []
        for k, (s, v) in waits.items():
            if self.waited[eng].get(k, 0) < v:
                self.waited[eng][k] = v
                wl.append((s, v))
        if dsem is not None:
            k = id(dsem)
            self.semcount[k] = self.semcount.get(k, 0) + inc
            tok = Tok(dsem, self.semcount[k], eng, inc)
        else:
            self.ecount[eng] += 1
            tok = Tok(self.esem[eng], self.ecount[eng], eng, 1)
        self.streams[eng].append((wl, fn, tok))
        for r in reads:
            k = id(tok.sem)
            r.readers[k] = tok
        for w in writes:
            w.last_w = tok
            w.readers = {}
        return tok

    def replay(self):
        nc = self.nc
        with nc.Block() as block:
            def run(e, name):
                for wl, fn, tok in self.streams[name]:
                    for s, v in wl:
                        e.wait_ge(s, v)
                    ins = fn(e)
                    if tok.inc == 1 and tok.sem is not self.esem[name]:
                        ins.then_inc(tok.sem)
                    else:
                        ins.then_inc(tok.sem, tok.inc)
                if name == "sp":
                    done = {}
                    for t in self.final:
                        k = id(t.sem)
                        if k not in done or done[k][1] < t.val:
                            done[k] = (t.sem, t.val)
                    for s, v in done.values():
                        e.wait_ge(s, v)

            @block.tensor
            def _(e):
                run(e, "pe")

            @block.scalar
            def _(e):
                run(e, "act")

            @block.vector
            def _(e):
                run(e, "dve")

            @block.gpsimd
            def _(e):
                run(e, "pool")

            @block.sync
            def _(e):
                run(e, "sp")


class Buf:
    def __init__(self, S, name, aps):
        self.aps = aps
        self.res = [Res(f"{name}{i}") for i in range(len(aps))]
        self.sems = [S.new_sem(name) for _ in aps]
        self.i = -1

    def next(self):
        self.i = (self.i + 1) % len(self.aps)
        return self.aps[self.i], self.res[self.i], self.sems[self.i]


def r32(ap):
    return ap.bitcast(F32R)


class Builder:
    def __init__(self, kinds, first_is_input=True):
        self.kinds = kinds
        nc = self.nc = bass.Bass("TRN2", target_bir_lowering=False)
        self.stack = ExitStack()
        st = self.stack
        S = self.S = Sched(nc, st)
        dt = nc.dram_tensor
        self.x_in = dt("x", [T, D], F32, kind="ExternalInput").ap()
        self.out = dt("out", [T, D], F32, kind="ExternalOutput").ap()
        self.cf_d = dt("cf", [128, CFW], F32, kind="ExternalInput").ap()
        self.cb_d = dt("cb", [128, CBW], BF16, kind="ExternalInput").ap()
        self.mask_d = dt("masks", [2, 128, 16 * 512], BF16, kind="ExternalInput").ap()
        self.w = {}
        nm = sum(1 for k in kinds if k[0] == "mla")
        nsb = sum(1 for k in kinds if k[0] == "sb")
        if nm:
            self.w["mla_w_in"] = dt("mla_w_in", [nm, D, 5696], F32, kind="ExternalInput").ap()
            self.w["mla_w_qb"] = dt("mla_w_qb", [nm, 1024, 6144], F32, kind="ExternalInput").ap()
            self.w["mla_w_kvb"] = dt("mla_w_kvb", [nm, 512, 8192], F32, kind="ExternalInput").ap()
            self.w["mla_w_out"] = dt("mla_w_out", [nm, D, D], F32, kind="ExternalInput").ap()
        if nsb:
            self.w["sb_w_in"] = dt("sb_w_in", [nsb, D, 4 * D], F32, kind="ExternalInput").ap()
            self.w["sb_w_out"] = dt("sb_w_out", [nsb, D, D], F32, kind="ExternalInput").ap()
        self.xres = dt("xres", [T, D], F32).ap()
        self.qT = dt("qT", [6144, T], BF16).ap()
        self.gateT = dt("gateT", [D, T], BF16).ap()
        self.sendK = [dt(f"sendK{c}", [1024, T], BF16).ap() for c in range(4)]
        self.gathK = [dt(f"gathK{c}", [2048, T], BF16).ap() for c in range(4)]
        self.sendKr = [dt(f"sendKr{c}", [1024, T], BF16).ap() for c in range(2)]
        self.gathKr = [dt(f"gathKr{c}", [2048, T], BF16).ap() for c in range(2)]
        self.sendV = [dt(f"sendV{c}", [T, 1024], BF16).ap() for c in range(4)]
        self.gathV = [dt(f"gathV{c}", [2 * T, 1024], BF16).ap() for c in range(4)]
        self.r_xcol = [Res(f"xcol{i}") for i in range(16)]
        self.r_qT = [Res(f"qT{h}") for h in range(NH)]
        self.r_gate = [Res(f"gate{h}") for h in range(NH)]
        self.r_sendK = [Res(f"sendK{h}") for h in range(NH)]
        self.r_sendKr = [Res(f"sendKr{h}") for h in range(NH)]
        self.r_sendV = [Res(f"sendV{i}") for i in range(NH)]
        self.r_gathK = [Res(f"gathK{c}") for c in range(4)]
        self.r_gathKr = [Res(f"gathKr{c}") for c in range(2)]
        self.r_gathV = [Res(f"gathV{c}") for c in range(4)]
        self.cc_sems = [S.new_sem("cc") for _ in range(10)]
        sb = lambda name, shape, dtp: st.enter_context(nc.sbuf_tensor(name, shape, dtp))
        self.actT = sb("actT", [128, 32, T], BF16)
        self.r_actT = [Res(f"actT{t}") for t in range(8)]
        self.r_og = [Res(f"og{h}") for h in range(NH)]
        wts = [sb(f"wbuf{i}", [128, WELEMS], BF16) for i in range(NWBUF)]
        self.wbuf = Buf(S, "wb", [w_[:] for w_ in wts])
        self.cf = sb("cf_sb", [128, CFW], F32)
        self.cb = sb("cb_sb", [128, CBW], BF16)
        self.r_const = Res("const")
        self.cfr = sb("cfr_sb", [128, 320], F32)
        stg = [sb(f"stg{i}", [128, T], BF16) for i in range(3)]
        self.stg = Buf(S, "stg", [s_[:] for s_ in stg])
        self.small = sb("small", [128, 64], F32)
        self.r_small = Res("small")
        AR = 32256
        self.zone = sb("zone", [128, 3584], F32)
        carveZ = lambda off, n: self.zone[:, off:off + n]
        self.arena = sb("arena", [128, AR], BF16)
        A = self.arena
        self.AR = AR

        def carve(off, n, dtp=BF16):
            v = A[:, off:off + n]
            return v.bitcast(F32) if dtp == F32 else v
        self.carve = carve
        self.phase_res = {k: [] for k in ("norm", "outp", "attn", "kv", "q")}

        def reg(ph, buf):
            self.phase_res[ph].extend(buf.res if isinstance(buf, Buf) else [buf])
            return buf
        self.xrowb = reg("norm", Buf(S, "xrow", [carve(0, 8192, F32), carve(8192, 8192, F32)]))
        self.hn = reg("norm", Buf(S, "hn", [carve(16384, 4096), carve(20480, 4096)]))
        self.xold = reg("outp", Buf(S, "xold", [carve(0, 4096, F32), carve(4096, 4096, F32)]))
        self.stgtm = Buf(S, "stgtm", [carve(8192, 2048), carve(10240, 2048)])
        for ph in ("outp", "kv"):
            reg(ph, self.stgtm)
        self.esp = reg("attn", Buf(S, "esp", [carveZ(i * 512, 512) for i in range(3)]))
        self.lsum = reg("attn", Buf(S, "lsum", [carveZ(1536 + i * 512, 512) for i in range(3)]))
        o = 0
        hs = []
        for i in range(2):
            d = {}
            d["q"] = carve(o, 1024); o += 1024
            d["q2"] = carve(o, 1024); o += 1024
            d["k"] = carve(o, 2048); o += 2048
            d["k2"] = carve(o, 2048); o += 2048
            d["v"] = carve(o, 2048); o += 2048
            d["g"] = carve(o, 1024); o += 1024
            hs.append(d)
        self.hset = hs
        self.r_hset = [{k: reg("attn", Res(f"hs{i}{k}")) for k in hs[i]} for i in range(2)]
        self.hsem = [{k: S.new_sem(f"hs{k}") for k in hs[i]} for i in range(2)]
        self.maskt = carve(o, 8192); o += 8192
        self.r_mask = reg("attn", Res("mask"))
        self.mask_sem = S.new_sem("mask")
        self.ebuf = reg("attn", Buf(S, "ebuf", [carve(o + i * 1024, 1024, F32) for i in range(2)])); o += 2048
        self.wt = reg("attn", Buf(S, "wt", [carve(o + i * 512, 512) for i in range(3)])); o += 1536
        self.fin = reg("attn", Buf(S, "fin", [carve(o + i * 1024, 1024, F32) for i in range(2)])); o += 2048
        assert o <= AR, o
        self.ebp = reg("attn", Buf(S, "ebp", [carve(26624, 2048, F32), carve(30208, 2048, F32)]))
        self.spp = reg("attn", Buf(S, "spp", [carveZ(0, 1024), carveZ(1024, 1024)]))
        self.lp = reg("attn", Buf(S, "lp", [carveZ(2048 + i * 512, 512) for i in range(3)]))
        self.wp = reg("attn", Buf(S, "wp", [carve(28672, 1024), hs[0]["k2"][:, 0:1024]]))
        self.prep_f = [carveZ(0, 1024), carve(0, 2048, F32), carveZ(1024, 1024),
                       carve(2048, 2048, F32), carve(4096, 2048, F32)]
        self.r_prep = [Res(f"prep{i}") for i in range(5)]
        self.sqr = carveZ(2048, 1024)
        self.r_sqr = Res("sqr")
        for ph in ("kv", "q"):
            for r in self.r_prep + [self.r_sqr]:
                reg(ph, r)
        self.lat = carve(12288, 10240, F32).rearrange("p (c t) -> p c t", t=T)
        self.r_lat = reg("kv", Res("lat"))
        self.kvn = carve(22528, 4096).rearrange("p (c t) -> p c t", t=T)
        self.r_kvn = reg("kv", Res("kvn"))
        self.qlat = carve(6144, 16384, F32).rearrange("p (c t) -> p c t", t=T)
        self.r_qlat = reg("q", Res("qlat"))
        self.qn = carve(22528, 8192).rearrange("p (c t) -> p c t", t=T)
        self.r_qn = reg("q", Res("qn"))
        assert 22528 + 8192 <= AR
        self.prep_alt = {"kv": [carve(12288, 2048, F32), carve(14336, 2048, F32), carve(16384, 2048, F32)],
                         "q": [carve(6144, 2048, F32), carve(8192, 2048, F32), carve(10240, 2048, F32)]}
        self.r_prep_alt = [Res(f"prepalt{i}") for i in range(3)]
        for ph in ("kv", "q"):
            for r in self.r_prep_alt:
                reg(ph, r)
        self.r_gstg = Res("gstg")
        self.gstg_sem = S.new_sem("gstg")
        for ph in ("kv", "q"):
            reg(ph, self.r_gstg)
        self.cur_phase = None
        self.pbig = [st.enter_context(nc.psum_tensor(f"pb{i}", [128, 1024], F32)) for i in range(4)]
        self.pb = [self.pbig[i // 2][:, (i % 2) * 512:(i % 2 + 1) * 512] for i in range(8)]
        self.r_pb = [Res(f"pb{i}") for i in range(8)]
        self.bank_i = -1
        self.held = set()
        self.plan = []
        self.plan_i = 0
        self.plan_loaded = 0
        self.wbusy = set()
        self.wlast = -1

    def c(self, name, rows=128):
        o, w = CF[name]
        return self.cf[0:rows, o:o + w]

    def cr(self, name, rows=128):
        o = {"tri": 0, "ones": 128, "rT": 256}[name]
        w = CF[name][1]
        return self.cfr[0:rows, o:o + w].bitcast(F32R)

    def cbv(self, name, rows=128):
        o, w = CB[name]
        return self.cb[0:rows, o:o + w]

    def next_bank(self):
        for _ in range(8):
            self.bank_i = (self.bank_i + 1) % 8
            if self.bank_i not in self.held:
                return self.bank_i
        raise RuntimeError("no free PSUM bank")

    def wnext(self):
        self._wprefetch()
        ent = self.plan[self.plan_i]
        assert "view" in ent, ("weight piece not loadable: no free buffer", self.plan_i, ent["lane"], self.wbusy)
        self.plan_i += 1
        return ent["view"], ent["res"], ent["KC"], ent["ncols"]

    def wfree(self, res):
        self.wbusy.discard(self.wv_res.index(res))
        self._wprefetch()

    def _wsetup(self):
        wb = self.wbuf
        h0 = wb.aps[0][:, 0:WELEMS // 2]
        h1 = wb.aps[0][:, WELEMS // 2:WELEMS]
        self.wv_ap = list(wb.aps) + [h0, h1]
        self.wv_res = list(wb.res) + [Res("wb0a"), Res("wb0b")]
        self.wv_sem = list(wb.sems) + [self.S.new_sem("wb0a"), self.S.new_sem("wb0b")]
        self.wv_alias = {0: [3, 4], 3: [0], 4: [0], 1: [], 2: []}

    def _wprefetch(self):
        S = self.S
        if not hasattr(self, "wv_ap"):
            self._wsetup()
        while self.plan_loaded < len(self.plan) and self.plan_loaded < self.plan_i + NWBUF:
            ent = self.plan[self.plan_loaded]
            cand = {"g": [1, 2], "s": [3, 4], "a": [1, 2, 0]}[ent["lane"]]
            free = [c for c in cand if c not in self.wbusy and not any(a_ in self.wbusy for a_ in self.wv_alias[c])]
            if not free:
                return
            if ent["lane"] == "a":
                free.sort(key=lambda c: (c - self.wlast - 1) % 3)
            bi = free[0]
            if bi < 3:
                self.wlast = bi
            self.wbusy.add(bi)
            ap, res, sem = self.wv_ap[bi], self.wv_res[bi], self.wv_sem[bi]
            KC, ncols = ent["KC"], ent["ncols"]
            assert KC * ncols <= (WELEMS if bi < 3 else WELEMS // 2)
            view = ap[:, 0:KC * ncols].rearrange("p (k n) -> p k n", n=ncols)
            src = ent["w"].rearrange("(k p) n -> p k n", p=128)
            S.op("pool", lambda e, view=view, src=src: e.dma_start(out=view, in_=src),
                 writes=[res] + [self.wv_res[a_] for a_ in self.wv_alias[bi]], dsem=sem)
            ent["view"], ent["res"] = view, res
            self.plan_loaded += 1

    def build_plan(self):
        jm = jsb = 0
        for kind, _ in self.kinds:
            if kind == "mla":
                w_in = self.w["mla_w_in"][jm]
                for c0, n in [(1024, 256), (1280, 256), (1536, 64)]:
                    self.plan.append(dict(w=w_in[:, c0:c0 + n], KC=32, ncols=n, lane="a"))
                gi = 0
                for h in range(NH):
                    self.plan.append(dict(w=self.w["mla_w_kvb"][jm][:, h * 256:(h + 1) * 256], KC=4, ncols=256, lane="s"))
                    if h % 4 == 0:
                        self.plan.append(dict(w=w_in[:, 1600 + gi * 256:1600 + (gi + 1) * 256], KC=32, ncols=256, lane="g"))
                        gi += 1
                for c0 in range(0, 1024, 256):
                    self.plan.append(dict(w=w_in[:, c0:c0 + 256], KC=32, ncols=256, lane="a"))
                for h in range(NH):
                    self.plan.append(dict(w=self.w["mla_w_qb"][jm][:, h * 192:(h + 1) * 192], KC=8, ncols=192, lane="s"))
                    if h % 4 == 0:
                        self.plan.append(dict(w=w_in[:, 1600 + gi * 256:1600 + (gi + 1) * 256], KC=32, ncols=256, lane="g"))
                        gi += 1
                for c0 in range(0, D, 256):
                    self.plan.append(dict(w=self.w["mla_w_out"][jm][:, c0:c0 + 256], KC=32, ncols=256, lane="a"))
                jm += 1
            else:
                w_in = self.w["sb_w_in"][jsb]
                for base in (D, 2 * D, 0, 3 * D):
                    for c0 in range(base, base + D, 256):
                        self.plan.append(dict(w=w_in[:, c0:c0 + 256], KC=32, ncols=256, lane="a"))
                for c0 in range(0, D, 256):
                    self.plan.append(dict(w=self.w["sb_w_out"][jsb][:, c0:c0 + 256], KC=32, ncols=256, lane="a"))
                jsb += 1

    def load_consts(self):
        S = self.S
        sem = S.new_sem("const")
        S.op("sp", lambda e: e.dma_start(out=self.cf[:], in_=self.cf_d[:, :]), writes=[self.r_const], dsem=sem)
        S.op("sp", lambda e: e.dma_start(out=self.cb[:], in_=self.cb_d[:, :]), writes=[self.r_const], dsem=sem)
        o0 = CF["tri"][0]
        S.op("dve", lambda e: e.tensor_copy(out=self.cfr[:, :].bitcast(F32R), in_=self.cf[:, o0:o0 + 320]),
             reads=[self.r_const], writes=[self.r_const])

    def norm_phase(self, li, x_src, x_res):
        S = self.S
        g = self.c(f"g{li}")
        ident = self.cbv("ident")
        for t in range(8):
            xrow, r_xrow, xsem = self.xrowb.next()
            S.op("sp", lambda e, t=t, xrow=xrow: e.dma_start(out=xrow, in_=x_src[t * 128:(t + 1) * 128, :]),
                 reads=x_res, writes=[r_xrow], dsem=xsem)
            hn, r_hn, _ = self.hn.next()
            ssq = self.small[:, t:t + 1]
            rstd = self.small[:, 8 + t:9 + t]
            rinv = self.small[:, 16 + t:17 + t]
            S.op("act", lambda e, hn=hn, ssq=ssq, xrow=xrow: e.activation(out=hn, in_=xrow, func=AF.Square, accum_out=ssq),
                 reads=[r_xrow], writes=[r_hn, self.r_small])
            S.op("act", lambda e, ssq=ssq, rstd=rstd: e.activation(out=rstd, in_=ssq, func=AF.Sqrt, scale=1.0 / D, bias=EPS),
                 reads=[self.r_small, self.r_const], writes=[self.r_small])
            S.op("dve", lambda e, rstd=rstd, rinv=rinv: e.reciprocal(out=rinv, in_=rstd),
                 reads=[self.r_small], writes=[self.r_small])
            S.op("dve", lambda e, hn=hn, rinv=rinv, xrow=xrow: e.tensor_scalar(out=hn, in0=xrow, scalar1=rinv, scalar2=None, op0=ALU.mult),
                 reads=[r_xrow, self.r_small], writes=[r_hn])
            for q in range(4):
                b = self.next_bank()
                pv = self.pb[b][:].bitcast(BF16)

                def ftr(e, pv=pv, hn=hn, q=q):
                    for j in range(8):
                        cix = 8 * q + j
                        ins = e.transpose(out=pv[:, j * 128:(j + 1) * 128], in_=hn[:, cix * 128:(cix + 1) * 128], identity=ident)
                    return ins
                S.op("pe", ftr, reads=[r_hn, self.r_const], writes=[self.r_pb[b]])

                def fev(e, pv=pv, q=q, t=t):
                    for j in range(8):
                        cix = 8 * q + j
                        ins = e.tensor_scalar(out=self.actT[:, cix, t * 128:(t + 1) * 128], in0=pv[:, j * 128:(j + 1) * 128],
                                              scalar1=g[:, cix:cix + 1], scalar2=None, op0=ALU.mult)
                    return ins
                S.op("dve", fev, reads=[self.r_pb[b], self.r_const], writes=[self.r_actT[t]])

    def gemm_fm(self, npieces, act_fn, act_res, epi, nslots=2):
        S = self.S
        for pi in range(npieces):
            wt, wres, KC, ncols = self.wnext()
            nm = (ncols + 127) // 128
            banks = [[None] * nslots for _ in range(nm)]
            for slot in range(nslots):
                ares = act_res[4 * slot:4 * slot + 4] if (len(act_res) == 8 and nslots == 2) else act_res
                for mc in range(nm):
                    M = min(128, ncols - mc * 128)
                    b = self.next_bank()

                    def f(e, b=b, wt=wt, KC=KC, mc=mc, M=M, slot=slot):
                        for kc in range(KC):
                            ins = e.matmul(self.pb[b][0:M, :], lhsT=wt[:, kc, mc * 128:mc * 128 + M], rhs=act_fn(kc, slot),
                                           start=(kc == 0), stop=(kc == KC - 1))
                        return ins
                    S.op("pe", f, reads=[wres] + ares, writes=[self.r_pb[b]])
                    banks[mc][slot] = b
            for mc in range(nm):
                epi(pi, mc, min(128, ncols - mc * 128), banks[mc])
            self.wfree(wres)

    def gemm_tm(self, npieces, lhs_fn, lhs_res, epi, col0=0, ncols_use=None, ntt=8):
        S = self.S
        for pi in range(npieces):
            wt, wres, KC, ncols = self.wnext()
            self.gemm_tm_piece(pi, wt, wres, KC, col0, ncols_use or ncols, lhs_fn, lhs_res, epi, ntt)
            self.wfree(wres)

    def gemm_tm_piece(self, pi, wt, wres, KC, col0, n, lhs_fn, lhs_res, epi, ntt=8):
        S = self.S
        per = 512 // n if n <= 256 else 1
        per = min(per, 2) if n == 256 else per
        banks = []
        t = 0
        while t < ntt:
            b = self.next_bank()
            tl = list(range(t, min(ntt, t + per)))

            def f(e, b=b, tl=tl, wt=wt, KC=KC):
                for i, tt in enumerate(tl):
                    for kc in range(KC):
                        ins = e.matmul(self.pb[b][:, i * n:(i + 1) * n], lhsT=lhs_fn(kc, tt), rhs=wt[:, kc, col0:col0 + n],
                                       start=(kc == 0), stop=(kc == KC - 1))
                return ins
            S.op("pe", f, reads=[wres] + lhs_res, writes=[self.r_pb[b]])
            banks.append((b, tl))
            t += per
        epi(pi, banks)

    def epi_fm_dram(self, dst, dst_res_fn, row_fn, func=None, scale=1.0):
        S = self.S

        def epi(pi, mc, M, banks):
            stg, r_stg, sem = self.stg.next()
            for slot, b in enumerate(banks):
                if func is None:
                    S.op("dve", lambda e, b=b, slot=slot, stg=stg, M=M: e.tensor_scalar(
                        out=stg[0:M, slot * 512:(slot + 1) * 512], in0=self.pb[b][0:M, :], scalar1=float(scale), scalar2=None, op0=ALU.mult),
                        reads=[self.r_pb[b]], writes=[r_stg])
                else:
                    S.op("act", lambda e, b=b, slot=slot, stg=stg, M=M: e.activation(
                        out=stg[0:M, slot * 512:(slot + 1) * 512], in_=self.pb[b][0:M, :], func=func),
                        reads=[self.r_pb[b]], writes=[r_stg])
            r0 = row_fn(pi, mc)
            dd = dst(pi, mc) if callable(dst) else dst
            S.op("sp", lambda e, stg=stg, r0=r0, M=M, dd=dd: e.dma_start(out=dd[r0:r0 + M, :], in_=stg[0:M, :]),
                 reads=[r_stg], writes=[dst_res_fn(pi, mc)], dsem=sem)
        return epi

    def epi_fm_sbuf(self, dst_fn, dst_res):
        S = self.S

        def epi(pi, mc, M, banks):
            for slot, b in enumerate(banks):
                S.op("dve", lambda e, b=b, slot=slot, M=M, pi=pi, mc=mc: e.tensor_copy(
                    out=dst_fn(pi, mc, M, slot), in_=self.pb[b][0:M, :]),
                    reads=[self.r_pb[b]], writes=[dst_res])
        return epi

    def epi_tm_dram(self, dst, dst_res_fn, col_fn, n=256):
        S = self.S

        def epi(pi, banks):
            stg, r_stg, sem = self.stgtm.next()
            sv = stg.rearrange("p (t n) -> p t n", n=256)
            for b, tl in banks:
                k = len(tl)
                S.op("dve", lambda e, b=b, tl=tl, k=k, sv=sv: e.tensor_copy(
                    out=sv[:, tl[0]:tl[0] + k, 0:n], in_=self.pb[b][:, 0:k * n].rearrange("p (t n) -> p t n", n=n)),
                    reads=[self.r_pb[b]], writes=[r_stg])
            c0 = col_fn(pi)
            dd = dst(pi) if callable(dst) else dst
            dv = dd[:, c0:c0 + n].rearrange("(t p) n -> p t n", p=128)
            S.op("sp", lambda e, sv=sv, dv=dv: e.dma_start(out=dv, in_=sv[:, :, 0:n]),
                 reads=[r_stg], writes=[dst_res_fn(pi)], dsem=sem)
        return epi

    def out_proj(self, x_src, x_src_res_fn, x_dst, final):
        S = self.S
        pend = {}

        def prefetch(pi):
            xo, r_xo, sem = self.xold.next()
            xv = xo.rearrange("p (t n) -> p t n", n=256)
            sv = x_src[:, pi * 256:(pi + 1) * 256].rearrange("(t p) n -> p t n", p=128)
            S.op("sp", lambda e, xv=xv, sv=sv: e.dma_start(out=xv, in_=sv),
                 reads=x_src_res_fn(pi), writes=[r_xo], dsem=sem)
            pend[pi] = (xv, r_xo, sem)

        def epi(pi, banks):
            xv, r_xo, sem = pend.pop(pi)
            for b, tl in banks:
                k = len(tl)
                S.op("dve", lambda e, b=b, tl=tl, k=k, xv=xv: e.tensor_tensor(
                    out=xv[:, tl[0]:tl[0] + k, :], in0=self.pb[b][:, 0:k * 256].rearrange("p (t n) -> p t n", n=256),
                    in1=xv[:, tl[0]:tl[0] + k, :], op=ALU.add),
                    reads=[self.r_pb[b], r_xo], writes=[r_xo])
            dv = x_dst[:, pi * 256:(pi + 1) * 256].rearrange("(t p) n -> p t n", p=128)
            tok = S.op("sp", lambda e, xv=xv, dv=dv: e.dma_start(out=dv, in_=xv),
                       reads=[r_xo], writes=[self.r_xcol[pi]], dsem=sem)
            if final:
                S.final.append(tok)
            if pi + 2 < 16:
                prefetch(pi + 2)
        prefetch(0)
        prefetch(1)
        self.gemm_tm(16, lambda kc, tt: self.actT[:, kc, tt * 128:(tt + 1) * 128], self.r_og, epi)

    def allgather(self, send, send_res, gath, gath_res, sem):
        S = self.S
        S.op("pool", lambda e: e.collective_compute("AllGather", ALU.bypass, replica_groups=RG,
                                                    ins=[send.opt()], outs=[gath.opt()]),
             reads=send_res, writes=[gath_res], dsem=sem, inc=1)

    def exchange(self, mla, do_k=True, do_v=True):
        if do_k:
            for c in range(4):
                self.allgather(self.sendK[c], self.r_sendK[8 * c:8 * c + 8], self.gathK[c], self.r_gathK[c], self.cc_sems[c])
        if not do_v:
            return
        if mla:
            for c in range(2):
                self.allgather(self.sendKr[c], self.r_sendKr[16 * c:16 * c + 16], self.gathKr[c], self.r_gathKr[c], self.cc_sems[4 + c])
        for c in range(4):
            rs = self.r_sendV[8 * c:8 * c + 8] if mla else self.r_sendV[4 * c:4 * c + 4]
            self.allgather(self.sendV[c], rs, self.gathV[c], self.r_gathV[c], self.cc_sems[6 + c])

    def load_masks(self, which):
        S = self.S
        S.op("sp", lambda e: e.dma_start(out=self.maskt, in_=self.mask_d[which]),
             writes=[self.r_mask], dsem=self.mask_sem)

    def attention(self, kind):
        S = self.S
        sb_mode = kind == "sb"
        krows = 128 if sb_mode else 192
        ident = self.cbv("ident")
        nident = self.cbv("nident")
        onesb = self.cbv("onesb")
        tri = self.cr("tri")
        ones = self.cr("ones")
        items = []
        for h in range(NH):
            for s in range(2):
                nkb = 2 if s == 0 else 4
                tl = []
                for j in reversed(range(nkb)):
                    for i in reversed(range(4)):
                        gt = 4 * GPOS[j] + i
                        if s == 0:
                            mi = j * 4 + i
                        elif j >= 2:
                            mi = 8 + (j - 2) * 4 + i
                        else:
                            mi = None
                        tl.append((gt, mi))
                for n_, (gt, mi) in enumerate(tl):
                    items.append(dict(h=h, s=s, gt=gt, mi=mi, first=(n_ == 0), last=(n_ == len(tl) - 1)))
        n = len(items)

        def load_head(h):
            hs = self.hset[h % 2]
            rs = self.r_hset[h % 2]
            sm = self.hsem[h % 2]
            kv = hs["k"].rearrange("p (r t) -> p r t", r=2)
            ksrc = self.gathK[h // 8].rearrange("(r f) t -> f r t", r=2)[(h % 8) * 128:(h % 8 + 1) * 128]
            S.op("sp", lambda e: e.dma_start(out=kv, in_=ksrc), reads=[self.r_gathK[h // 8]], writes=[rs["k"]], dsem=sm["k"])
            if sb_mode:
                S.op("sp", lambda e: e.dma_start(out=hs["q"], in_=self.qT[h * 128:(h + 1) * 128, :]),
                     reads=[self.r_qT[h]], writes=[rs["q"]], dsem=sm["q"])
                S.op("dve", lambda e: e.tensor_scalar(out=hs["q2"], in0=hs["q"], scalar1=-1.0, scalar2=None, op0=ALU.mult),
                     reads=[rs["q"]], writes=[rs["q2"]])
            else:
                S.op("sp", lambda e: e.dma_start(out=hs["q"], in_=self.qT[h * 192:h * 192 + 128, :]),
                     reads=[self.r_qT[h]], writes=[rs["q"]], dsem=sm["q"])
                S.op("sp", lambda e: e.dma_start(out=hs["q2"][0:64, :], in_=self.qT[h * 192 + 128:h * 192 + 192, :]),
                     reads=[self.r_qT[h]], writes=[rs["q2"]], dsem=sm["q2"])
                kv2 = hs["k2"].rearrange("p (r t) -> p r t", r=2)
                k2src = self.gathKr[h // 16].rearrange("(r f) t -> f r t", r=2)[(h % 16) * 64:(h % 16 + 1) * 64]
                S.op("sp", lambda e: e.dma_start(out=kv2[0:64], in_=k2src),
                     reads=[self.r_gathKr[h // 16]], writes=[rs["k2"]], dsem=sm["k2"])
            vv = hs["v"].rearrange("p (t d) -> p t d", d=128)
            vsrc = self.gathV[h // 8][:, (h % 8) * 128:(h % 8 + 1) * 128].rearrange("(t p) d -> p t d", p=128)
            S.op("sp", lambda e: e.dma_start(out=vv, in_=vsrc), reads=[self.r_gathV[h // 8]], writes=[rs["v"]], dsem=sm["v"])
            S.op("sp", lambda e: e.dma_start(out=hs["g"], in_=self.gateT[h * 128:(h + 1) * 128, :]),
                 reads=[self.r_gate[h]], writes=[rs["g"]], dsem=sm["g"])

        ZB = [0, 1]
        CBK = [2, 3]
        OT = [4, 5]
        DEN = [6, 7]
        st = {}

        def stageA(ix):
            it = items[ix]
            h, s, gt, mi = it["h"], it["s"], it["gt"], it["mi"]
            hs, rs = self.hset[h % 2], self.r_hset[h % 2]
            zb = ZB[ix % 2]
            qs = hs["q"][:, s * 512:(s + 1) * 512]
            kt = hs["k"][:, gt * 128:(gt + 1) * 128]
            mk = None if mi is None else self.maskt[:, mi * 512:(mi + 1) * 512]

            def f1(e):
                if sb_mode:
                    ins = e.matmul(self.pb[zb][:, :], lhsT=kt, rhs=qs, start=True, stop=(mk is None))
                    if mk is not None:
                        ins = e.matmul(self.pb[zb][:, :], lhsT=ident, rhs=mk, start=False, stop=True)
                else:
                    e.matmul(self.pb[zb][:, :], lhsT=kt, rhs=qs, start=True, stop=False)
                    ins = e.matmul(self.pb[zb][:, :], lhsT=hs["k2"][:, gt * 128:(gt + 1) * 128],
                                   rhs=hs["q2"][:, s * 512:(s + 1) * 512], start=False, stop=True)
                return ins
            rd = [rs["q"], rs["k"], self.r_const] + ([] if sb_mode else [rs["q2"], rs["k2"]]) + ([self.r_mask] if (mk is not None and sb_mode) else [])
            S.op("pe", f1, reads=rd, writes=[self.r_pb[zb]])
            if sb_mode:
                esp, r_esp, _ = self.esp.next()
                eb, r_eb, _ = self.ebuf.next()
                S.op("act", lambda e: e.activation(out=eb, in_=self.pb[zb][:, :], func=AF.Exp),
                     reads=[self.r_pb[zb]], writes=[r_eb])
                S.op("act", lambda e: e.activation(out=r32(esp), in_=eb, func=AF.Ln, bias=1.0),
                     reads=[r_eb], writes=[r_esp])
                prev = st.get("lsum")
                if it["first"]:
                    ls, r_ls, _ = self.lsum.next()
                    S.op("dve", lambda e: e.tensor_copy(out=r32(ls), in_=esp), reads=[r_esp], writes=[r_ls])
                else:
                    pls, r_pls = prev
                    ls, r_ls, _ = self.lsum.next()
                    S.op("dve", lambda e: e.tensor_tensor(out=r32(ls), in0=pls, in1=esp, op=ALU.add),
                         reads=[r_esp, r_pls], writes=[r_ls])
                st["lsum"] = (ls, r_ls)
                it["esp"] = (esp, r_esp)
                it["lprev"] = prev if not it["first"] else None
            else:
                w, r_w, _ = self.wt.next()
                S.op("act", lambda e: e.activation(out=w, in_=self.pb[zb][:, :], func=AF.Exp),
                     reads=[self.r_pb[zb]], writes=[r_w])
                if mk is not None:
                    S.op("dve", lambda e: e.tensor_tensor(out=w, in0=w, in1=mk, op=ALU.mult),
                         reads=[r_w, self.r_mask], writes=[r_w])
                it["w"] = (w, r_w)

        def stageB(ix):
            if not sb_mode:
                return
            it = items[ix]
            h, s, gt, mi = it["h"], it["s"], it["gt"], it["mi"]
            hs, rs = self.hset[h % 2], self.r_hset[h % 2]
            cbk = CBK[ix % 2]
            esp, r_esp = it["esp"]
            kt = hs["k"][:, gt * 128:(gt + 1) * 128]
            nq = hs["q2"][:, s * 512:(s + 1) * 512]
            mk = None if mi is None else self.maskt[:, mi * 512:(mi + 1) * 512]
            lp = it["lprev"]

            def f2(e):
                e.matmul(self.pb[cbk][:, :], lhsT=tri, rhs=r32(esp), start=True, stop=False)
                if lp is not None:
                    e.matmul(self.pb[cbk][:, :], lhsT=ones, rhs=r32(lp[0]), start=False, stop=False)
                ins = e.matmul(self.pb[cbk][:, :], lhsT=kt, rhs=nq, start=False, stop=(mk is None))
                if mk is not None:
                    ins = e.matmul(self.pb[cbk][:, :], lhsT=nident, rhs=mk, start=False, stop=True)
                return ins
            rd = [r_esp, rs["k"], rs["q2"], self.r_const] + ([lp[1]] if lp is not None else []) + ([self.r_mask] if mk is not None else [])
            S.op("pe", f2, reads=rd, writes=[self.r_pb[cbk]])
            w, r_w, _ = self.wt.next()
            S.op("act", lambda e: e.activation(out=w, in_=self.pb[cbk][:, :], func=AF.Exp, scale=-1.0),
                 reads=[self.r_pb[cbk]], writes=[r_w])
            it["w"] = (w, r_w)

        def stageC(ix):
            it = items[ix]
            h, s, gt = it["h"], it["s"], it["gt"]
            hs, rs = self.hset[h % 2], self.r_hset[h % 2]
            par = (2 * h + s) % 2
            ot = OT[par]
            w, r_w = it["w"]
            vt = hs["v"][:, gt * 128:(gt + 1) * 128]

            def f3(e):
                ins = e.matmul(self.pb[ot][:, :], lhsT=vt, rhs=w, start=it["first"], stop=it["last"])
                if not sb_mode:
                    ins = e.matmul(self.pb[DEN[par]][:, :], lhsT=onesb, rhs=w, start=it["first"], stop=it["last"])
                return ins
            wr = [self.r_pb[ot]] + ([] if sb_mode else [self.r_pb[DEN[par]]])
            S.op("pe", f3, reads=[r_w, rs["v"], self.r_const], writes=wr)
            if it["last"]:
                dst = self.actT[:, h, s * 512:(s + 1) * 512]
                gs = hs["g"][:, s * 512:(s + 1) * 512]
                if sb_mode:
                    S.op("dve", lambda e: e.tensor_tensor(out=dst, in0=self.pb[ot][:, :], in1=gs, op=ALU.mult),
                         reads=[self.r_pb[ot], rs["g"]], writes=[self.r_og[h]])
                else:
                    fb, r_fb, _ = self.fin.next()
                    S.op("act", lambda e: e.activation(out=fb, in_=self.pb[DEN[par]][:, :], func=AF.Ln),
                         reads=[self.r_pb[DEN[par]]], writes=[r_fb])
                    S.op("act", lambda e: e.activation(out=fb, in_=fb, func=AF.Exp, scale=-1.0),
                         reads=[r_fb], writes=[r_fb])
                    S.op("dve", lambda e: e.tensor_tensor(out=fb, in0=self.pb[ot][:, :], in1=fb, op=ALU.mult),
                         reads=[self.r_pb[ot], r_fb], writes=[r_fb])
                    S.op("dve", lambda e: e.tensor_tensor(out=dst, in0=fb, in1=gs, op=ALU.mult),
                         reads=[r_fb, rs["g"]], writes=[self.r_og[h]])
            if it["last"] and s == 1 and h + 2 < NH:
                load_head(h + 2)

        if not sb_mode:
            for i in range(2):
                for kk in ("q2", "k2"):
                    S.op("pool", lambda e, i=i, kk=kk: e.memset(self.hset[i][kk][64:128, :], 0.0),
                         writes=[self.r_hset[i][kk]])
        if sb_mode:
            pairs = [(items[2 * i], items[2 * i + 1]) for i in range(n // 2)]
            npairs = len(pairs)
            ZP = [self.pbig[0], self.pbig[3]]
            CP = self.pbig[1]
            r_ZP = [[self.r_pb[0], self.r_pb[1]], [self.r_pb[6], self.r_pb[7]]]
            r_CP = [self.r_pb[2], self.r_pb[3]]
            half = lambda ap, x: ap[:, x * 512:(x + 1) * 512]
            pst = {"L": None}

            def mask_of(it):
                return None if it["mi"] is None else self.maskt[:, it["mi"] * 512:(it["mi"] + 1) * 512]

            def pA(pi):
                pr = pairs[pi]
                h, s = pr[0]["h"], pr[0]["s"]
                hs, rs = self.hset[h % 2], self.r_hset[h % 2]
                zp, rz = ZP[pi % 2], r_ZP[pi % 2]
                qs = hs["q"][:, s * 512:(s + 1) * 512]

                def f1(e):
                    for x in range(2):
                        gt = pr[x]["gt"]
                        ins = e.matmul(half(zp, x), lhsT=hs["k"][:, gt * 128:(gt + 1) * 128], rhs=qs, start=True, stop=True)
                    return ins
                S.op("pe", f1, reads=[rs["q"], rs["k"]], writes=rz)
                eb, r_eb, _ = self.ebp.next()
                S.op("act", lambda e: e.activation(out=eb, in_=zp[:, :], func=AF.Exp), reads=rz, writes=[r_eb])
                for x in range(2):
                    mk = mask_of(pr[x])
                    if mk is not None:
                        S.op("dve", lambda e, x=x, mk=mk: e.tensor_tensor(out=half(eb, x), in0=half(eb, x), in1=mk, op=ALU.mult),
                             reads=[r_eb, self.r_mask], writes=[r_eb])
                pr[0]["eb"] = (eb, r_eb)

            def pLn(pi):
                pr = pairs[pi]
                eb, r_eb = pr[0]["eb"]
                sp, r_sp, _ = self.spp.next()
                S.op("act", lambda e: e.activation(out=r32(sp), in_=eb, func=AF.Ln, bias=1.0), reads=[r_eb], writes=[r_sp])
                lprev = None if pr[0]["first"] else pst["L"]
                ln_, r_ln, _ = self.lp.next()
                if lprev is None:
                    S.op("dve", lambda e: e.tensor_tensor(out=r32(ln_), in0=half(sp, 0), in1=half(sp, 1), op=ALU.add),
                         reads=[r_sp], writes=[r_ln])
                else:
                    S.op("dve", lambda e: e.tensor_tensor(out=r32(ln_), in0=lprev[0], in1=half(sp, 0), op=ALU.add),
                         reads=[r_sp, lprev[1]], writes=[r_ln])
                    S.op("dve", lambda e: e.tensor_tensor(out=r32(ln_), in0=ln_, in1=half(sp, 1), op=ALU.add),
                         reads=[r_sp, r_ln], writes=[r_ln])
                pst["L"] = (ln_, r_ln)
                pr[0]["sp"] = (sp, r_sp)
                pr[0]["lprev"] = lprev

            def pB(pi):
                pr = pairs[pi]
                h, s = pr[0]["h"], pr[0]["s"]
                hs, rs = self.hset[h % 2], self.r_hset[h % 2]
                sp, r_sp = pr[0]["sp"]
                lprev = pr[0]["lprev"]
                nq = hs["q2"][:, s * 512:(s + 1) * 512]

                def f2(e):
                    for x in range(2):
                        gt = pr[x]["gt"]
                        e.matmul(half(CP, x), lhsT=tri, rhs=r32(half(sp, x)), start=True, stop=False)
                        if x == 1:
                            e.matmul(half(CP, x), lhsT=ones, rhs=r32(half(sp, 0)), start=False, stop=False)
                        if lprev is not None:
                            e.matmul(half(CP, x), lhsT=ones, rhs=r32(lprev[0]), start=False, stop=False)
                        ins = e.matmul(half(CP, x), lhsT=hs["k"][:, gt * 128:(gt + 1) * 128], rhs=nq, start=False, stop=True)
                    return ins
                rd = [r_sp, rs["k"], rs["q2"], self.r_const] + ([lprev[1]] if lprev is not None else [])
                S.op("pe", f2, reads=rd, writes=r_CP)
                w, r_w, _ = self.wp.next()
                S.op("act", lambda e: e.activation(out=w, in_=CP[:, :], func=AF.Exp, scale=-1.0), reads=r_CP, writes=[r_w])
                for x in range(2):
                    mk = mask_of(pr[x])
                    if mk is not None:
                        S.op("dve", lambda e, x=x, mk=mk: e.tensor_tensor(out=half(w, x), in0=half(w, x), in1=mk, op=ALU.mult),
                             reads=[r_w, self.r_mask], writes=[r_w])
                pr[0]["w"] = (w, r_w)

            def pC(pi):
                pr = pairs[pi]
                h, s = pr[0]["h"], pr[0]["s"]
                hs, rs = self.hset[h % 2], self.r_hset[h % 2]
                ot = OT[(2 * h + s) % 2]
                w, r_w = pr[0]["w"]

                def f3(e):
                    for x in range(2):
                        gt = pr[x]["gt"]
                        ins = e.matmul(self.pb[ot][:, :], lhsT=hs["v"][:, gt * 128:(gt + 1) * 128], rhs=half(w, x),
                                       start=(x == 0 and pr[0]["first"]), stop=(x == 1 and pr[1]["last"]))
                    return ins
                S.op("pe", f3, reads=[r_w, rs["v"]], writes=[self.r_pb[ot]])
                if pr[1]["last"]:
                    dst = self.actT[:, h, s * 512:(s + 1) * 512]
                    gs = hs["g"][:, s * 512:(s + 1) * 512]
                    S.op("dve", lambda e: e.tensor_tensor(out=dst, in0=self.pb[ot][:, :], in1=gs, op=ALU.mult),
                         reads=[self.r_pb[ot], rs["g"]], writes=[self.r_og[h]])
                    if s == 1 and h + 2 < NH:
                        load_head(h + 2)

            load_head(0)
            load_head(1)
            for step in range(npairs + 3):
                if step < npairs:
                    pA(step)
                if 1 <= step <= npairs:
                    pLn(step - 1)
                if 2 <= step <= npairs + 1:
                    pB(step - 2)
                if 3 <= step <= npairs + 2:
                    pC(step - 3)
            return
        load_head(0)
        load_head(1)
        for step in range(n + 2):
            if step < n:
                stageA(step)
            if 1 <= step <= n:
                stageB(step - 1)
            if 2 <= step <= n + 1:
                stageC(step - 2)

    def build(self):
        self.build_plan()
        self.load_consts()
        nl = len(self.kinds)
        for li, (kind, j) in enumerate(self.kinds):
            x_src = self.x_in if li == 0 else self.xres
            x_res = [] if li == 0 else self.r_xcol
            final = li == nl - 1
            x_dst = self.out if final else self.xres
            if kind == "sb":
                self.sb_layer(li, j, x_src, x_res, x_dst, final)
            else:
                self.mla_layer(li, j, x_src, x_res, x_dst, final)
        self.S.replay()
        return self.nc


def _phase(self, new):
    old = self.cur_phase
    self.cur_phase = new
    if old is None or old == new:
        return
    toks = {}
    for r in self.phase_res[old]:
        cand = list(r.readers.values()) + ([r.last_w] if r.last_w is not None else [])
        for t in cand:
            k = id(t.sem)
            if k not in toks or toks[k].val < t.val:
                toks[k] = t
    for r in self.phase_res[new]:
        for k, t in toks.items():
            if k not in r.readers or r.readers[k].val < t.val:
                r.readers[k] = t


def _alias(self, old_list, new_list):
    toks = {}
    for r in old_list:
        cand = list(r.readers.values()) + ([r.last_w] if r.last_w is not None else [])
        for t in cand:
            k = id(t.sem)
            if k not in toks or toks[k].val < t.val:
                toks[k] = t
    for r in new_list:
        for k, t in toks.items():
            if k not in r.readers or r.readers[k].val < t.val:
                r.readers[k] = t


def _latent_norm(self, src_fn, nch, gname, dst, dst_res, src_res, inv_n):
    S = self.S
    ones = self.cr("ones")
    g = self.c(gname)
    P = self.prep_f
    RP = self.r_prep
    bs = [self.next_bank(), self.next_bank()]
    for c in range(nch):
        sq, r_sq = (P[0], RP[0]) if c % 2 == 0 else (P[2], RP[2])
        S.op("act", lambda e, c=c, sq=sq: e.activation(out=r32(sq), in_=src_fn(c), func=AF.Square),
             reads=[src_res], writes=[r_sq])
        for s in range(2):
            S.op("pe", lambda e, c=c, s=s, sq=sq: e.matmul(self.pb[bs[s]][:, :], lhsT=ones, rhs=r32(sq[:, s * 512:(s + 1) * 512]),
                                                          start=(c == 0), stop=(c == nch - 1)),
                 reads=[r_sq, self.r_const], writes=[self.r_pb[bs[s]]])
    for s in range(2):
        S.op("act", lambda e, s=s: e.activation(out=P[1][:, s * 512:(s + 1) * 512], in_=self.pb[bs[s]][:, :], func=AF.Ln,
                                                scale=float(inv_n), bias=EPS),
             reads=[self.r_pb[bs[s]]], writes=[RP[1]])
    S.op("act", lambda e: e.activation(out=P[1], in_=P[1], func=AF.Exp, scale=-0.5), reads=[RP[1]], writes=[RP[1]])
    for c in range(nch):
        S.op("dve", lambda e, c=c: e.scalar_tensor_tensor(out=dst[:, c, :], in0=src_fn(c), scalar=g[:, c:c + 1], in1=P[1],
                                                         op0=ALU.mult, op1=ALU.mult),
             reads=[src_res, RP[1], self.r_const], writes=[dst_res])


def _prep(self, bn, rope, gname, lnscale, dn, dr, fill1=None, fill2=None, alt=None):
    S = self.S
    ones = self.cr("ones")
    ones64 = self.cr("ones", 64)
    rT = self.cr("rT", 64)
    cos = self.c("cos", 64)
    sin = self.c("sin", 64)
    g = self.c(gname)
    gn = g[:, 0:1]
    gr = g[0:64, 1:2]
    P, RP = list(self.prep_f), list(self.r_prep)
    if alt is not None:
        for k_, i_ in enumerate((1, 3, 4)):
            P[i_] = self.prep_alt[alt][k_]
            RP[i_] = self.r_prep_alt[k_]
    sqr, r_sqr = self.sqr, self.r_sqr
    H = lambda ap, s: ap[:, s * 512:(s + 1) * 512]
    self.held.update(bn)
    for s in range(2):
        S.op("act", lambda e, s=s: e.activation(out=r32(H(P[0], s)), in_=self.pb[bn[s]][:, :], func=AF.Square),
             reads=[self.r_pb[bn[s]]], writes=[RP[0]])
    if rope[0] == "psum":
        for s in range(2):
            S.op("act", lambda e, s=s: e.activation(out=r32(H(sqr[0:64], s)), in_=self.pb[rope[1][s]][0:64, :], func=AF.Square),
                 reads=[self.r_pb[rope[1][s]]], writes=[r_sqr])
        rres = [self.r_pb[b] for b in rope[1]]
        rsrc = lambda s: self.pb[rope[1][s]][0:64, :]
    else:
        S.op("act", lambda e: e.activation(out=r32(sqr[0:64]), in_=rope[1], func=AF.Square), reads=[rope[2]], writes=[r_sqr])
        rres = [rope[2]]
        rsrc = lambda s: H(rope[1], s)
    for s in range(2):
        S.op("dve", lambda e, s=s: e.tensor_scalar(out=r32(H(P[2][0:64], s)), in0=rsrc(s), scalar1=gr, scalar2=None, op0=ALU.mult),
             reads=rres + [self.r_const, r_sqr], writes=[RP[2]])
    if fill1 is not None:
        fill1()
    bs = [self.next_bank(), self.next_bank()]
    for s in range(2):
        def fss(e, s=s):
            e.matmul(self.pb[bs[s]][:, :], lhsT=ones, rhs=r32(H(P[0], s)), start=True, stop=False)
            return e.matmul(self.pb[bs[s]][:, :], lhsT=ones64, rhs=r32(H(sqr[0:64], s)), start=False, stop=True)
        S.op("pe", fss, reads=[RP[0], r_sqr, self.r_const], writes=[self.r_pb[bs[s]]])
    if fill2 is not None:
        fill2()
    br = [self.next_bank(), self.next_bank()]
    for s in range(2):
        S.op("pe", lambda e, s=s: e.matmul(self.pb[br[s]][0:64, :], lhsT=rT, rhs=r32(H(P[2][0:64], s)), start=True, stop=True),
             reads=[RP[2], self.r_const], writes=[self.r_pb[br[s]]])
    for s in range(2):
        S.op("act", lambda e, s=s: e.activation(out=H(P[1], s), in_=self.pb[bs[s]][:, :], func=AF.Ln, scale=1.0 / 192, bias=EPS),
             reads=[self.r_pb[bs[s]]], writes=[RP[1]])
    S.op("act", lambda e: e.activation(out=P[1], in_=P[1], func=AF.Exp, scale=-0.5, bias=float(lnscale)),
         reads=[RP[1]], writes=[RP[1]])
    S.op("dve", lambda e: e.tensor_tensor(out=P[3][0:64], in0=P[2][0:64], in1=cos, op=ALU.mult),
         reads=[RP[2], self.r_const], writes=[RP[3]])
    for s in range(2):
        S.op("dve", lambda e, s=s: e.tensor_tensor(out=H(P[4][0:64], s), in0=self.pb[br[s]][0:64, :], in1=H(sin, s), op=ALU.mult),
             reads=[self.r_pb[br[s]], self.r_const], writes=[RP[4]])
    S.op("dve", lambda e: e.tensor_tensor(out=P[3][0:64], in0=P[3][0:64], in1=P[4][0:64], op=ALU.add),
         reads=[RP[3], RP[4]], writes=[RP[3]])
    stg, r_stg, sem = self.stg.next()
    for s in range(2):
        S.op("dve", lambda e, s=s: e.scalar_tensor_tensor(out=H(stg, s), in0=self.pb[bn[s]][:, :], scalar=gn, in1=H(P[1], s),
                                                         op0=ALU.mult, op1=ALU.mult),
             reads=[self.r_pb[bn[s]], RP[1], self.r_const], writes=[r_stg])
    S.op("sp", lambda e: e.dma_start(out=dn[0][dn[1]:dn[1] + 128, :], in_=stg), reads=[r_stg], writes=[dn[2]], dsem=sem)
    self.held.difference_update(bn)
    stg2, r_stg2, sem2 = self.stg.next()
    S.op("dve", lambda e: e.tensor_tensor(out=stg2[0:64], in0=P[3][0:64], in1=P[1][0:64], op=ALU.mult),
         reads=[RP[3], RP[1]], writes=[r_stg2])
    S.op("sp", lambda e: e.dma_start(out=dr[0][dr[1]:dr[1] + 64, :], in_=stg2[0:64]), reads=[r_stg2], writes=[dr[2]], dsem=sem2)


def _mla_layer(self, li, j, x_src, x_res, x_dst, final):
    S = self.S
    self.phase("norm")
    self.alias(self.r_og, self.r_actT)
    self.norm_phase(li, x_src, x_res)
    act = lambda kc, slot: self.actT[:, kc, slot * 512:(slot + 1) * 512]
    self.phase("kv")
    lat = self.lat
    gstate = [0]

    def gate_gen():
        gstg = self.carve(30720, 1024)
        for gi in range(16):
            wt, wres, KC, ncols = self.wnext()
            for mc in range(2):
                for slot in range(2):
                    b = self.next_bank()

                    def f(e, b=b, wt=wt, mc=mc, slot=slot):
                        for kc in range(32):
                            ins = e.matmul(self.pb[b][:, :], lhsT=wt[:, kc, mc * 128:(mc + 1) * 128], rhs=act(kc, slot),
                                           start=(kc == 0), stop=(kc == 31))
                        return ins
                    S.op("pe", f, reads=[wres] + self.r_actT[4 * slot:4 * slot + 4], writes=[self.r_pb[b]])
                    S.op("act", lambda e, b=b, slot=slot: e.activation(
                        out=gstg[:, slot * 512:(slot + 1) * 512], in_=self.pb[b][:, :], func=AF.Silu),
                        reads=[self.r_pb[b]], writes=[self.r_gstg])
                    if slot == 1:
                        r0 = (2 * gi + mc) * 128
                        S.op("sp", lambda e, r0=r0: e.dma_start(out=self.gateT[r0:r0 + 128, :], in_=gstg),
                             reads=[self.r_gstg], writes=[self.r_gate[2 * gi + mc]], dsem=self.gstg_sem)
                    if mc == 1 and slot == 1:
                        self.wfree(wres)
                    yield
    gg = gate_gen()

    def gate_group():
        next(gg, None)
    self.gemm_fm(3, act, self.r_actT, self.epi_fm_sbuf(
        lambda pi, mc, M, slot: lat[0:M, 2 * pi + mc, slot * 512:(slot + 1) * 512], self.r_lat))
    self.latent_norm(lambda c: lat[:, c, :], 4, f"kva{j}", self.kvn, self.r_kvn, self.r_lat, 1.0 / 512)
    self.alias([self.r_lat], self.r_prep_alt)
    kvn = self.kvn
    for h in range(NH):
        wt, wres, KC, ncols = self.wnext()
        bn = []
        for s in range(2):
            b = self.next_bank()

            def fk(e, b=b, s=s, wt=wt):
                for kc in range(4):
                    ins = e.matmul(self.pb[b][:, :], lhsT=wt[:, kc, 0:128], rhs=kvn[:, kc, s * 512:(s + 1) * 512],
                                   start=(kc == 0), stop=(kc == 3))
                return ins
            S.op("pe", fk, reads=[wres, self.r_kvn], writes=[self.r_pb[b]])
            bn.append(b)

        def epi_v(pi, banks, h=h):
            stg, r_stg, sem = self.stgtm.next()
            sv = stg.rearrange("p (t n) -> p t n", n=256)
            for b, tl in banks:
                k = len(tl)
                S.op("dve", lambda e, b=b, tl=tl, k=k, sv=sv: e.tensor_copy(
                    out=sv[:, tl[0]:tl[0] + k, 0:128], in_=self.pb[b][:, 0:k * 128].rearrange("p (t n) -> p t n", n=128)),
                    reads=[self.r_pb[b]], writes=[r_stg])
            dv = self.sendV[h // 8][:, (h % 8) * 128:(h % 8 + 1) * 128].rearrange("(t p) n -> p t n", p=128)
            S.op("sp", lambda e, sv=sv, dv=dv: e.dma_start(out=dv, in_=sv[:, :, 0:128]),
                 reads=[r_stg], writes=[self.r_sendV[h]], dsem=sem)
        vfill = lambda h=h, wt=wt, wres=wres, epi_v=epi_v: self.gemm_tm_piece(
            h, wt, wres, 4, 128, 128, lambda kc, tt: kvn[:, kc, tt * 128:(tt + 1) * 128], [self.r_kvn], epi_v)
        self.prep(bn, ("sbuf", lat[0:64, 4, :], self.r_lat), f"kn{j}", 0.0,
                  (self.sendK[h // 8], (h % 8) * 128, self.r_sendK[h]), (self.sendKr[h // 16], (h % 16) * 64, self.r_sendKr[h]),
                  fill1=vfill, fill2=gate_group, alt=("kv" if h % 2 else None))
        self.wfree(wres)
    self.exchange(True)
    self.phase("q")
    qsel = lambda c: self.qlat[:, c, :]
    self.gemm_fm(4, act, self.r_actT, self.epi_fm_sbuf(
        lambda pi, mc, M, slot: qsel(2 * pi + mc)[0:M, slot * 512:(slot + 1) * 512], self.r_qlat))
    self.latent_norm(qsel, 8, f"qa{j}", self.qn, self.r_qn, self.r_qlat, 1.0 / 1024)
    self.alias([self.r_qlat], self.r_prep_alt)
    qn = self.qn
    for h in range(NH):
        wt, wres, KC, ncols = self.wnext()
        bn, br = [], []
        for (lst, c0, M) in ((bn, 0, 128), (br, 128, 64)):
            for s in range(2):
                b = self.next_bank()

                def fq(e, b=b, s=s, wt=wt, c0=c0, M=M):
                    for kc in range(8):
                        ins = e.matmul(self.pb[b][0:M, :], lhsT=wt[:, kc, c0:c0 + M], rhs=qn[:, kc, s * 512:(s + 1) * 512],
                                       start=(kc == 0), stop=(kc == 7))
                    return ins
                S.op("pe", fq, reads=[wres, self.r_qn], writes=[self.r_pb[b]])
                lst.append(b)
        self.prep(bn, ("psum", br), f"qn{j}", math.log(192 ** -0.5),
                  (self.qT, h * 192, self.r_qT[h]), (self.qT, h * 192 + 128, self.r_qT[h]), fill1=gate_group,
                  alt=("q" if h % 2 else None))
        self.wfree(wres)
    self.phase("attn")
    self.alias(self.r_actT, self.r_og)
    self.load_masks(0)
    self.attention("mla")
    self.phase("outp")
    self.out_proj(x_src, (lambda pi: []) if li == 0 else (lambda pi: [self.r_xcol[pi]]), x_dst, final)


def _sb_layer(self, li, j, x_src, x_res, x_dst, final):
    self.phase("norm")
    self.alias(self.r_og, self.r_actT)
    self.norm_phase(li, x_src, x_res)
    act = lambda kc, slot: self.actT[:, kc, slot * 512:(slot + 1) * 512]
    self.phase("outp")
    self.gemm_fm(16, act, self.r_actT, self.epi_fm_dram(
        lambda pi, mc: self.sendK[(2 * pi + mc) // 8], lambda pi, mc: self.r_sendK[2 * pi + mc],
        lambda pi, mc: ((2 * pi + mc) % 8) * 128))
    self.exchange(False, do_v=False)
    self.gemm_tm(16, lambda kc, tt: self.actT[:, kc, tt * 128:(tt + 1) * 128], self.r_actT,
                 self.epi_tm_dram(lambda pi: self.sendV[pi // 4], lambda pi: self.r_sendV[pi], lambda pi: (pi % 4) * 256))
    self.exchange(False, do_k=False)
    self.gemm_fm(16, act, self.r_actT, self.epi_fm_dram(
        self.qT, lambda pi, mc: self.r_qT[2 * pi + mc], lambda pi, mc: (2 * pi + mc) * 128, scale=128 ** -0.5))
    self.gemm_fm(16, act, self.r_actT, self.epi_fm_dram(
        self.gateT, lambda pi, mc: self.r_gate[2 * pi + mc], lambda pi, mc: (2 * pi + mc) * 128, func=AF.Silu))
    self.phase("attn")
    self.alias(self.r_actT, self.r_og)
    self.load_masks(1)
    self.attention("sb")
    self.phase("outp")
    self.out_proj(x_src, (lambda pi: []) if li == 0 else (lambda pi: [self.r_xcol[pi]]), x_dst, final)


Builder.phase = _phase
Builder.alias = _alias
Builder.latent_norm = _latent_norm
Builder.prep = _prep
Builder.mla_layer = _mla_layer
Builder.sb_layer = _sb_layer


def _rope_tables():
    inv = (1.0 / (10000.0 ** (np.arange(0, 64, 2, dtype=np.float32) / np.float32(64)))).astype(np.float32)
    ang = np.arange(S_FULL, dtype=np.float32)[:, None] * inv[None, :]
    return np.cos(ang).astype(np.float32), np.sin(ang).astype(np.float32)


def _local_tokens(r):
    blks = [0, 3] if r == 0 else [1, 2]
    return np.concatenate([np.arange(b * 512, (b + 1) * 512) for b in blks])


def _make_consts(inputs, kinds, r):
    cf = np.zeros((128, CFW), np.float32)

    def put(name, arr):
        o, w = CF[name]
        cf[:arr.shape[0], o:o + arr.shape[1]] = arr
    for li, (kind, j) in enumerate(kinds):
        g = inputs["mla_norm_g" if kind == "mla" else "sb_norm_g"][j]
        put(f"g{li}", g.reshape(32, 128).T)
    for j in range(inputs["mla_q_a_g"].shape[0]):
        put(f"qa{j}", inputs["mla_q_a_g"][j].reshape(8, 128).T)
        put(f"kva{j}", inputs["mla_kv_a_g"][j].reshape(4, 128).T)
        for nm, key in (("qn", "mla_q_norm_g"), ("kn", "mla_k_norm_g")):
            a = np.zeros((128, 2), np.float32)
            a[:, 0] = inputs[key][j][:128]
            a[:64, 1] = inputs[key][j][128:]
            put(f"{nm}{j}", a)
    jj, ss = np.meshgrid(np.arange(128), np.arange(128), indexing="ij")
    put("tri", (jj >= ss).astype(np.float32))
    put("ones", np.ones((128, 128), np.float32))
    R = np.zeros((64, 64), np.float32)
    for i in range(32):
        R[i, i + 32] = -1.0
        R[i + 32, i] = 1.0
    put("rT", R.T.copy())
    cos, sin = _rope_tables()
    lt = _local_tokens(r)
    put("cos", np.concatenate([cos[lt].T, cos[lt].T], axis=0))
    put("sin", np.concatenate([sin[lt].T, sin[lt].T], axis=0))
    cb = np.zeros((128, CBW), np.float32)
    cb[:, 0:128] = np.eye(128)
    cb[:, 128:256] = -np.eye(128)
    cb[:, 256:384] = 1.0
    masks = np.zeros((2, 128, 16, 512), np.float32)
    qblk = [0, 3] if r == 0 else [1, 2]
    slots = [(0, 0), (0, 1), (1, 2), (1, 3)]
    for gi, (s, kb) in enumerate(slots):
        for i in range(4):
            kpos = kb * 512 + i * 128 + np.arange(128)[:, None]
            qpos = qblk[s] * 512 + np.arange(512)[None, :]
            masks[0, :, gi * 4 + i, :] = np.where(kpos <= qpos, 1.0, 0.0)
            masks[1, :, gi * 4 + i, :] = np.where(kpos < qpos, 1.0, 0.0)
    return cf, cb.astype(ml_dtypes.bfloat16), masks.reshape(2, 128, 16 * 512).astype(ml_dtypes.bfloat16)


_NC_CACHE = {}


def run_layers(x, inputs, kinds):
    key = tuple(kinds)
    if key not in _NC_CACHE:
        _NC_CACHE[key] = Builder(list(kinds)).build()
    nc = _NC_CACHE[key]
    mj = sorted({j for k, j in kinds if k == "mla"})
    sj = sorted({j for k, j in kinds if k == "sb"})
    wmaps = {}
    if mj:
        for nm in ("mla_w_in", "mla_w_qb", "mla_w_kvb", "mla_w_out"):
            wmaps[nm] = np.ascontiguousarray(inputs[nm][mj[0]:mj[-1] + 1])
    if sj:
        for nm in ("sb_w_in", "sb_w_out"):
            wmaps[nm] = np.ascontiguousarray(inputs[nm][sj[0]:sj[-1] + 1])
    kk = [(k, j - (mj[0] if k == "mla" else sj[0])) for k, j in kinds]
    assert kk == list(kinds) or len(kinds) == 1 or True
    in_maps = []
    for c in range(8):
        b, r = divmod(c, 2)
        cf, cb, masks = _make_consts(inputs, kinds, r)
        m = {"x": np.ascontiguousarray(x[b][_local_tokens(r)]), "cf": cf, "cb": cb, "masks": masks}
        m.update(wmaps)
        in_maps.append(m)
    res = run_bass_kernel_spmd(nc, in_maps, core_ids=list(range(8)))
    out = np.empty((4, S_FULL, D), np.float32)
    for c in range(8):
        b, r = divmod(c, 2)
        out[b][_local_tokens(r)] = np.asarray(res.results[c]["out"])
    return out


def kernel(**inputs):
    inputs = {k: np.asarray(v) for k, v in inputs.items()}
    kinds = (("mla", 0), ("sb", 0), ("mla", 1), ("sb", 1))
    return run_layers(inputs["x"].astype(np.float32), inputs, kinds)
```

```python
import math
from contextlib import ExitStack

import numpy as np
import ml_dtypes
import concourse.bass as bass
import concourse.mybir as mybir
from concourse.bass_utils import run_bass_kernel_spmd

F32 = mybir.dt.float32
BF16 = mybir.dt.bfloat16
F32R = mybir.dt.float32r
AF = mybir.ActivationFunctionType
ALU = mybir.AluOpType

D = 4096
T = 1024
S_FULL = 2048
NH = 32
EPS = 1e-6
NEG = -30000.0
RG = [[0, 1], [2, 3], [4, 5], [6, 7]]
GPOS = [0, 2, 3, 1]
WELEMS = 8192
NWBUF = 3

CF = {}
_off = 0
for _name, _w in [("g0", 32), ("g1", 32), ("g2", 32), ("g3", 32),
                  ("qa0", 8), ("qa1", 8), ("kva0", 4), ("kva1", 4),
                  ("qn0", 2), ("qn1", 2), ("kn0", 2), ("kn1", 2),
                  ("tri", 128), ("ones", 128), ("rT", 64),
                  ("cos", 1024), ("sin", 1024)]:
    CF[_name] = (_off, _w)
    _off += _w
CFW = _off
CB = {"ident": (0, 128), "nident": (128, 128), "onesb": (256, 128)}
CBW = 384


class Tok:
    __slots__ = ("sem", "val", "eng", "inc")

    def __init__(self, sem, val, eng, inc):
        self.sem, self.val, self.eng, self.inc = sem, val, eng, inc


class Res:
    __slots__ = ("name", "last_w", "readers")

    def __init__(self, name):
        self.name = name
        self.last_w = None
        self.readers = {}


class Sched:
    ENGS = ("pe", "act", "dve", "pool", "sp")

    def __init__(self, nc, stack):
        self.nc = nc
        self.stack = stack
        self.streams = {e: [] for e in self.ENGS}
        self.esem = {e: stack.enter_context(nc.semaphore("s_" + e)) for e in self.ENGS}
        self.ecount = {e: 0 for e in self.ENGS}
        self.waited = {e: {} for e in self.ENGS}
        self.semcount = {}
        self.final = []
        self.nsem = 0

    def new_sem(self, name):
        self.nsem += 1
        return self.stack.enter_context(self.nc.semaphore(f"{name}_{self.nsem}"))

    def op(self, eng, fn, reads=(), writes=(), dsem=None, inc=16):
        deps = []
        for r in reads:
            if r.last_w is not None:
                deps.append((r.last_w, True))
        for w in writes:
            if w.last_w is not None:
                deps.append((w.last_w, False))
            for t in w.readers.values():
                deps.append((t, False))
        waits = {}
        for t, raw in deps:
            if t.eng == eng and t.inc == 1 and t.sem is self.esem[eng]:
                if not (raw and eng in ("act", "dve", "pool")):
                    continue
            k = id(t.sem)
            if k not in waits or waits[k][1] < t.val:
                waits[k] = (t.sem, t.val)
        wl = []
        for k, (s, v) in waits.items():
            if self.waited[eng].get(k, 0) < v:
                self.waited[eng][k] = v
                wl.append((s, v))
        if dsem is not None:
            k = id(dsem)
            self.semcount[k] = self.semcount.get(k, 0) + inc
            tok = Tok(dsem, self.semcount[k], eng, inc)
        else:
            self.ecount[eng] += 1
            tok = Tok(self.esem[eng], self.ecount[eng], eng, 1)
        self.streams[eng].append((wl, fn, tok))
        for r in reads:
            k = id(tok.sem)
            r.readers[k] = tok
        for w in writes:
            w.last_w = tok
            w.readers = {}
        return tok

    def replay(self):
        nc = self.nc
        with nc.Block() as block:
            def run(e, name):
                for wl, fn, tok in self.streams[name]:
                    for s, v in wl:
                        e.wait_ge(s, v)
                    ins = fn(e)
                    if tok.inc == 1 and tok.sem is not self.esem[name]:
                        ins.then_inc(tok.sem)
                    else:
                        ins.then_inc(tok.sem, tok.inc)
                if name == "sp":
                    done = {}
                    for t in self.final:
                        k = id(t.sem)
                        if k not in done or done[k][1] < t.val:
                            done[k] = (t.sem, t.val)
                    for s, v in done.values():
                        e.wait_ge(s, v)

            @block.tensor
            def _(e):
                run(e, "pe")

            @block.scalar
            def _(e):
                run(e, "act")

            @block.vector
            def _(e):
                run(e, "dve")

            @block.gpsimd
            def _(e):
                run(e, "pool")

            @block.sync
            def _(e):
                run(e, "sp")


class Buf:
    def __init__(self, S, name, aps):
        self.aps = aps
        self.res = [Res(f"{name}{i}") for i in range(len(aps))]
        self.sems = [S.new_sem(name) for _ in aps]
        self.i = -1

    def next(self):
        self.i = (self.i + 1) % len(self.aps)
        return self.aps[self.i], self.res[self.i], self.sems[self.i]


def r32(ap):
    return ap.bitcast(F32R)


class Builder:
    def __init__(self, kinds, first_is_input=True):
        self.kinds = kinds
        nc = self.nc = bass.Bass("TRN2", target_bir_lowering=False)
        self.stack = ExitStack()
        st = self.stack
        S = self.S = Sched(nc, st)
        dt = nc.dram_tensor
        self.x_in = dt("x", [T, D], F32, kind="ExternalInput").ap()
        self.out = dt("out", [T, D], F32, kind="ExternalOutput").ap()
        self.cf_d = dt("cf", [128, CFW], F32, kind="ExternalInput").ap()
        self.cb_d = dt("cb", [128, CBW], BF16, kind="ExternalInput").ap()
        self.mask_d = dt("masks", [2, 128, 16 * 512], BF16, kind="ExternalInput").ap()
        self.w = {}
        nm = sum(1 for k in kinds if k[0] == "mla")
        nsb = sum(1 for k in kinds if k[0] == "sb")
        if nm:
            self.w["mla_w_in"] = dt("mla_w_in", [nm, D, 5696], F32, kind="ExternalInput").ap()
            self.w["mla_w_qb"] = dt("mla_w_qb", [nm, 1024, 6144], F32, kind="ExternalInput").ap()
            self.w["mla_w_kvb"] = dt("mla_w_kvb", [nm, 512, 8192], F32, kind="ExternalInput").ap()
            self.w["mla_w_out"] = dt("mla_w_out", [nm, D, D], F32, kind="ExternalInput").ap()
        if nsb:
            self.w["sb_w_in"] = dt("sb_w_in", [nsb, D, 4 * D], F32, kind="ExternalInput").ap()
            self.w["sb_w_out"] = dt("sb_w_out", [nsb, D, D], F32, kind="ExternalInput").ap()
        self.xres = dt("xres", [T, D], F32).ap()
        self.qT = dt("qT", [6144, T], BF16).ap()
        self.gateT = dt("gateT", [D, T], BF16).ap()
        self.sendK = [dt(f"sendK{c}", [1024, T], BF16).ap() for c in range(4)]
        self.gathK = [dt(f"gathK{c}", [2048, T], BF16).ap() for c in range(4)]
        self.sendKr = [dt(f"sendKr{c}", [1024, T], BF16).ap() for c in range(2)]
        self.gathKr = [dt(f"gathKr{c}", [2048, T], BF16).ap() for c in range(2)]
        self.sendV = [dt(f"sendV{c}", [T, 1024], BF16).ap() for c in range(4)]
        self.gathV = [dt(f"gathV{c}", [2 * T, 1024], BF16).ap() for c in range(4)]
        self.r_xcol = [Res(f"xcol{i}") for i in range(16)]
        self.r_qT = [Res(f"qT{h}") for h in range(NH)]
        self.r_gate = [Res(f"gate{h}") for h in range(NH)]
        self.r_sendK = [Res(f"sendK{h}") for h in range(NH)]
        self.r_sendKr = [Res(f"sendKr{h}") for h in range(NH)]
        self.r_sendV = [Res(f"sendV{i}") for i in range(NH)]
        self.r_gathK = [Res(f"gathK{c}") for c in range(4)]
        self.r_gathKr = [Res(f"gathKr{c}") for c in range(2)]
        self.r_gathV = [Res(f"gathV{c}") for c in range(4)]
        self.cc_sems = [S.new_sem("cc") for _ in range(10)]
        sb = lambda name, shape, dtp: st.enter_context(nc.sbuf_tensor(name, shape, dtp))
        self.actT = sb("actT", [128, 32, T], BF16)
        self.r_actT = [Res(f"actT{t}") for t in range(8)]
        self.r_og = [Res(f"og{h}") for h in range(NH)]
        wts = [sb(f"wbuf{i}", [128, WELEMS], BF16) for i in range(NWBUF)]
        self.wbuf = Buf(S, "wb", [w_[:] for w_ in wts])
        self.cf = sb("cf_sb", [128, CFW], F32)
        self.cb = sb("cb_sb", [128, CBW], BF16)
        self.r_const = Res("const")
        self.cfr = sb("cfr_sb", [128, 320], F32)
        stg = [sb(f"stg{i}", [128, T], BF16) for i in range(3)]
        self.stg = Buf(S, "stg", [s_[:] for s_ in stg])
        self.small = sb("small", [128, 64], F32)
        self.r_small = Res("small")
        AR = 32256
        self.zone = sb("zone", [128, 3584], F32)
        carveZ = lambda off, n: self.zone[:, off:off + n]
        self.arena = sb("arena", [128, AR], BF16)
        A = self.arena
        self.AR = AR

        def carve(off, n, dtp=BF16):
            v = A[:, off:off + n]
            return v.bitcast(F32) if dtp == F32 else v
        self.carve = carve
        self.phase_res = {k: [] for k in ("norm", "outp", "attn", "kv", "q")}

        def reg(ph, buf):
            self.phase_res[ph].extend(buf.res if isinstance(buf, Buf) else [buf])
            return buf
        self.xrowb = reg("norm", Buf(S, "xrow", [carve(0, 8192, F32), carve(8192, 8192, F32)]))
        self.hn = reg("norm", Buf(S, "hn", [carve(16384, 4096), carve(20480, 4096)]))
        self.xold = reg("outp", Buf(S, "xold", [carve(0, 4096, F32), carve(4096, 4096, F32)]))
        self.stgtm = Buf(S, "stgtm", [carve(8192, 2048), carve(10240, 2048)])
        for ph in ("outp", "kv"):
            reg(ph, self.stgtm)
        self.esp = reg("attn", Buf(S, "esp", [carveZ(i * 512, 512) for i in range(3)]))
        self.lsum = reg("attn", Buf(S, "lsum", [carveZ(1536 + i * 512, 512) for i in range(3)]))
        o = 0
        hs = []
        for i in range(2):
            d = {}
            d["q"] = carve(o, 1024); o += 1024
            d["q2"] = carve(o, 1024); o += 1024
            d["k"] = carve(o, 2048); o += 2048
            d["k2"] = carve(o, 2048); o += 2048
            d["v"] = carve(o, 2048); o += 2048
            d["g"] = carve(o, 1024); o += 1024
            hs.append(d)
        self.hset = hs
        self.r_hset = [{k: reg("attn", Res(f"hs{i}{k}")) for k in hs[i]} for i in range(2)]
        self.hsem = [{k: S.new_sem(f"hs{k}") for k in hs[i]} for i in range(2)]
        self.maskt = carve(o, 8192); o += 8192
        self.r_mask = reg("attn", Res("mask"))
        self.mask_sem = S.new_sem("mask")
        self.ebuf = reg("attn", Buf(S, "ebuf", [carve(o + i * 1024, 1024, F32) for i in range(2)])); o += 2048
        self.wt = reg("attn", Buf(S, "wt", [carve(o + i * 512, 512) for i in range(3)])); o += 1536
        self.fin = reg("attn", Buf(S, "fin", [carve(o + i * 1024, 1024, F32) for i in range(2)])); o += 2048
        assert o <= AR, o
        self.ebp = reg("attn", Buf(S, "ebp", [carve(26624, 2048, F32), carve(30208, 2048, F32)]))
        self.spp = reg("attn", Buf(S, "spp", [carveZ(0, 1024), carveZ(1024, 1024)]))
        self.lp = reg("attn", Buf(S, "lp", [carveZ(2048 + i * 512, 512) for i in range(3)]))
        self.wp = reg("attn", Buf(S, "wp", [carve(28672, 1024), hs[0]["k2"][:, 0:1024]]))
        self.prep_f = [carveZ(0, 1024), carve(0, 2048, F32), carveZ(1024, 1024),
                       carve(2048, 2048, F32), carve(4096, 2048, F32)]
        self.r_prep = [Res(f"prep{i}") for i in range(5)]
        self.sqr = carveZ(2048, 1024)
        self.r_sqr = Res("sqr")
        for ph in ("kv", "q"):
            for r in self.r_prep + [self.r_sqr]:
                reg(ph, r)
        self.lat = carve(12288, 10240, F32).rearrange("p (c t) -> p c t", t=T)
        self.r_lat = reg("kv", Res("lat"))
        self.kvn = carve(22528, 4096).rearrange("p (c t) -> p c t", t=T)
        self.r_kvn = reg("kv", Res("kvn"))
        self.qlat = carve(6144, 16384, F32).rearrange("p (c t) -> p c t", t=T)
        self.r_qlat = reg("q", Res("qlat"))
        self.qn = carve(22528, 8192).rearrange("p (c t) -> p c t", t=T)
        self.r_qn = reg("q", Res("qn"))
        assert 22528 + 8192 <= AR
        self.prep_alt = {"kv": [carve(12288, 2048, F32), carve(14336, 2048, F32), carve(16384, 2048, F32)],
                         "q": [carve(6144, 2048, F32), carve(8192, 2048, F32), carve(10240, 2048, F32)]}
        self.r_prep_alt = [Res(f"prepalt{i}") for i in range(3)]
        for ph in ("kv", "q"):
            for r in self.r_prep_alt:
                reg(ph, r)
        self.r_gstg = Res("gstg")
        self.gstg_sem = S.new_sem("gstg")
        for ph in ("kv", "q"):
            reg(ph, self.r_gstg)
        self.cur_phase = None
        self.pbig = [st.enter_context(nc.psum_tensor(f"pb{i}", [128, 1024], F32)) for i in range(4)]
        self.pb = [self.pbig[i // 2][:, (i % 2) * 512:(i % 2 + 1) * 512] for i in range(8)]
        self.r_pb = [Res(f"pb{i}") for i in range(8)]
        self.bank_i = -1
        self.held = set()
        self.plan = []
        self.plan_i = 0
        self.plan_loaded = 0
        self.wbusy = set()
        self.wlast = -1

    def c(self, name, rows=128):
        o, w = CF[name]
        return self.cf[0:rows, o:o + w]

    def cr(self, name, rows=128):
        o = {"tri": 0, "ones": 128, "rT": 256}[name]
        w = CF[name][1]
        return self.cfr[0:rows, o:o + w].bitcast(F32R)

    def cbv(self, name, rows=128):
        o, w = CB[name]
        return self.cb[0:rows, o:o + w]

    def next_bank(self):
        for _ in range(8):
            self.bank_i = (self.bank_i + 1) % 8
            if self.bank_i not in self.held:
                return self.bank_i
        raise RuntimeError("no free PSUM bank")

    def wnext(self):
        self._wprefetch()
        ent = self.plan[self.plan_i]
        assert "view" in ent, ("weight piece not loadable: no free buffer", self.plan_i, ent["lane"], self.wbusy)
        self.plan_i += 1
        return ent["view"], ent["res"], ent["KC"], ent["ncols"]

    def wfree(self, res):
        self.wbusy.discard(self.wv_res.index(res))
        self._wprefetch()

    def _wsetup(self):
        wb = self.wbuf
        h0 = wb.aps[0][:, 0:WELEMS // 2]
        h1 = wb.aps[0][:, WELEMS // 2:WELEMS]
        self.wv_ap = list(wb.aps) + [h0, h1]
        self.wv_res = list(wb.res) + [Res("wb0a"), Res("wb0b")]
        self.wv_sem = list(wb.sems) + [self.S.new_sem("wb0a"), self.S.new_sem("wb0b")]
        self.wv_alias = {0: [3, 4], 3: [0], 4: [0], 1: [], 2: []}

    def _wprefetch(self):
        S = self.S
        if not hasattr(self, "wv_ap"):
            self._wsetup()
        while self.plan_loaded < len(self.plan) and self.plan_loaded < self.plan_i + NWBUF:
            ent = self.plan[self.plan_loaded]
            cand = {"g": [1, 2], "s": [3, 4], "a": [1, 2, 0]}[ent["lane"]]
            free = [c for c in cand if c not in self.wbusy and not any(a_ in self.wbusy for a_ in self.wv_alias[c])]
            if not free:
                return
            if ent["lane"] == "a":
                free.sort(key=lambda c: (c - self.wlast - 1) % 3)
            bi = free[0]
            if bi < 3:
                self.wlast = bi
            self.wbusy.add(bi)
            ap, res, sem = self.wv_ap[bi], self.wv_res[bi], self.wv_sem[bi]
            KC, ncols = ent["KC"], ent["ncols"]
            assert KC * ncols <= (WELEMS if bi < 3 else WELEMS // 2)
            view = ap[:, 0:KC * ncols].rearrange("p (k n) -> p k n", n=ncols)
            src = ent["w"].rearrange("(k p) n -> p k n", p=128)
            S.op("pool", lambda e, view=view, src=src: e.dma_start(out=view, in_=src),
                 writes=[res] + [self.wv_res[a_] for a_ in self.wv_alias[bi]], dsem=sem)
            ent["view"], ent["res"] = view, res
            self.plan_loaded += 1

    def build_plan(self):
        jm = jsb = 0
        for kind, _ in self.kinds:
            if kind == "mla":
                w_in = self.w["mla_w_in"][jm]
                for c0, n in [(1024, 256), (1280, 256), (1536, 64)]:
                    self.plan.append(dict(w=w_in[:, c0:c0 + n], KC=32, ncols=n, lane="a"))
                gi = 0
                for h in range(NH):
                    self.plan.append(dict(w=self.w["mla_w_kvb"][jm][:, h * 256:(h + 1) * 256], KC=4, ncols=256, lane="s"))
                    if h % 4 == 0:
                        self.plan.append(dict(w=w_in[:, 1600 + gi * 256:1600 + (gi + 1) * 256], KC=32, ncols=256, lane="g"))
                        gi += 1
                for c0 in range(0, 1024, 256):
                    self.plan.append(dict(w=w_in[:, c0:c0 + 256], KC=32, ncols=256, lane="a"))
                for h in range(NH):
                    self.plan.append(dict(w=self.w["mla_w_qb"][jm][:, h * 192:(h + 1) * 192], KC=8, ncols=192, lane="s"))
                    if h % 4 == 0:
                        self.plan.append(dict(w=w_in[:, 1600 + gi * 256:1600 + (gi + 1) * 256], KC=32, ncols=256, lane="g"))
                        gi += 1
                for c0 in range(0, D, 256):
                    self.plan.append(dict(w=self.w["mla_w_out"][jm][:, c0:c0 + 256], KC=32, ncols=256, lane="a"))
                jm += 1
            else:
                w_in = self.w["sb_w_in"][jsb]
                for base in (D, 2 * D, 0, 3 * D):
                    for c0 in range(base, base + D, 256):
                        self.plan.append(dict(w=w_in[:, c0:c0 + 256], KC=32, ncols=256, lane="a"))
                for c0 in range(0, D, 256):
                    self.plan.append(dict(w=self.w["sb_w_out"][jsb][:, c0:c0 + 256], KC=32, ncols=256, lane="a"))
                jsb += 1

    def load_consts(self):
        S = self.S
        sem = S.new_sem("const")
        S.op("sp", lambda e: e.dma_start(out=self.cf[:], in_=self.cf_d[:, :]), writes=[self.r_const], dsem=sem)
        S.op("sp", lambda e: e.dma_start(out=self.cb[:], in_=self.cb_d[:, :]), writes=[self.r_const], dsem=sem)
        o0 = CF["tri"][0]
        S.op("dve", lambda e: e.tensor_copy(out=self.cfr[:, :].bitcast(F32R), in_=self.cf[:, o0:o0 + 320]),
             reads=[self.r_const], writes=[self.r_const])

    def norm_phase(self, li, x_src, x_res):
        S = self.S
        g = self.c(f"g{li}")
        ident = self.cbv("ident")
        for t in range(8):
            xrow, r_xrow, xsem = self.xrowb.next()
            S.op("sp", lambda e, t=t, xrow=xrow: e.dma_start(out=xrow, in_=x_src[t * 128:(t + 1) * 128, :]),
                 reads=x_res, writes=[r_xrow], dsem=xsem)
            hn, r_hn, _ = self.hn.next()
            ssq = self.small[:, t:t + 1]
            rstd = self.small[:, 8 + t:9 + t]
            rinv = self.small[:, 16 + t:17 + t]
            S.op("act", lambda e, hn=hn, ssq=ssq, xrow=xrow: e.activation(out=hn, in_=xrow, func=AF.Square, accum_out=ssq),
                 reads=[r_xrow], writes=[r_hn, self.r_small])
            S.op("act", lambda e, ssq=ssq, rstd=rstd: e.activation(out=rstd, in_=ssq, func=AF.Sqrt, scale=1.0 / D, bias=EPS),
                 reads=[self.r_small, self.r_const], writes=[self.r_small])
            S.op("dve", lambda e, rstd=rstd, rinv=rinv: e.reciprocal(out=rinv, in_=rstd),
                 reads=[self.r_small], writes=[self.r_small])
            S.op("dve", lambda e, hn=hn, rinv=rinv, xrow=xrow: e.tensor_scalar(out=hn, in0=xrow, scalar1=rinv, scalar2=None, op0=ALU.mult),
                 reads=[r_xrow, self.r_small], writes=[r_hn])
            for q in range(4):
                b = self.next_bank()
                pv = self.pb[b][:].bitcast(BF16)

                def ftr(e, pv=pv, hn=hn, q=q):
                    for j in range(8):
                        cix = 8 * q + j
                        ins = e.transpose(out=pv[:, j * 128:(j + 1) * 128], in_=hn[:, cix * 128:(cix + 1) * 128], identity=ident)
                    return ins
                S.op("pe", ftr, reads=[r_hn, self.r_const], writes=[self.r_pb[b]])

                def fev(e, pv=pv, q=q, t=t):
                    for j in range(8):
                        cix = 8 * q + j
                        ins = e.tensor_scalar(out=self.actT[:, cix, t * 128:(t + 1) * 128], in0=pv[:, j * 128:(j + 1) * 128],
                                              scalar1=g[:, cix:cix + 1], scalar2=None, op0=ALU.mult)
                    return ins
                S.op("dve", fev, reads=[self.r_pb[b], self.r_const], writes=[self.r_actT[t]])

    def gemm_fm(self, npieces, act_fn, act_res, epi, nslots=2):
        S = self.S
        for pi in range(npieces):
            wt, wres, KC, ncols = self.wnext()
            nm = (ncols + 127) // 128
            banks = [[None] * nslots for _ in range(nm)]
            for slot in range(nslots):
                ares = act_res[4 * slot:4 * slot + 4] if (len(act_res) == 8 and nslots == 2) else act_res
                for mc in range(nm):
                    M = min(128, ncols - mc * 128)
                    b = self.next_bank()

                    def f(e, b=b, wt=wt, KC=KC, mc=mc, M=M, slot=slot):
                        for kc in range(KC):
                            ins = e.matmul(self.pb[b][0:M, :], lhsT=wt[:, kc, mc * 128:mc * 128 + M], rhs=act_fn(kc, slot),
                                           start=(kc == 0), stop=(kc == KC - 1))
                        return ins
                    S.op("pe", f, reads=[wres] + ares, writes=[self.r_pb[b]])
                    banks[mc][slot] = b
            for mc in range(nm):
                epi(pi, mc, min(128, ncols - mc * 128), banks[mc])
            self.wfree(wres)

    def gemm_tm(self, npieces, lhs_fn, lhs_res, epi, col0=0, ncols_use=None, ntt=8):
        S = self.S
        for pi in range(npieces):
            wt, wres, KC, ncols = self.wnext()
            self.gemm_tm_piece(pi, wt, wres, KC, col0, ncols_use or ncols, lhs_fn, lhs_res, epi, ntt)
            self.wfree(wres)

    def gemm_tm_piece(self, pi, wt, wres, KC, col0, n, lhs_fn, lhs_res, epi, ntt=8):
        S = self.S
        per = 512 // n if n <= 256 else 1
        per = min(per, 2) if n == 256 else per
        banks = []
        t = 0
        while t < ntt:
            b = self.next_bank()
            tl = list(range(t, min(ntt, t + per)))

            def f(e, b=b, tl=tl, wt=wt, KC=KC):
                for i, tt in enumerate(tl):
                    for kc in range(KC):
                        ins = e.matmul(self.pb[b][:, i * n:(i + 1) * n], lhsT=lhs_fn(kc, tt), rhs=wt[:, kc, col0:col0 + n],
                                       start=(kc == 0), stop=(kc == KC - 1))
                return ins
            S.op("pe", f, reads=[wres] + lhs_res, writes=[self.r_pb[b]])
            banks.append((b, tl))
            t += per
        epi(pi, banks)

    def epi_fm_dram(self, dst, dst_res_fn, row_fn, func=None, scale=1.0):
        S = self.S

        def epi(pi, mc, M, banks):
            stg, r_stg, sem = self.stg.next()
            for slot, b in enumerate(banks):
                if func is None:
                    S.op("dve", lambda e, b=b, slot=slot, stg=stg, M=M: e.tensor_scalar(
                        out=stg[0:M, slot * 512:(slot + 1) * 512], in0=self.pb[b][0:M, :], scalar1=float(scale), scalar2=None, op0=ALU.mult),
                        reads=[self.r_pb[b]], writes=[r_stg])
                else:
                    S.op("act", lambda e, b=b, slot=slot, stg=stg, M=M: e.activation(
                        out=stg[0:M, slot * 512:(slot + 1) * 512], in_=self.pb[b][0:M, :], func=func),
                        reads=[self.r_pb[b]], writes=[r_stg])
            r0 = row_fn(pi, mc)
            dd = dst(pi, mc) if callable(dst) else dst
            S.op("sp", lambda e, stg=stg, r0=r0, M=M, dd=dd: e.dma_start(out=dd[r0:r0 + M, :], in_=stg[0:M, :]),
                 reads=[r_stg], writes=[dst_res_fn(pi, mc)], dsem=sem)
        return epi

    def epi_fm_sbuf(self, dst_fn, dst_res):
        S = self.S

        def epi(pi, mc, M, banks):
            for slot, b in enumerate(banks):
                S.op("dve", lambda e, b=b, slot=slot, M=M, pi=pi, mc=mc: e.tensor_copy(
                    out=dst_fn(pi, mc, M, slot), in_=self.pb[b][0:M, :]),
                    reads=[self.r_pb[b]], writes=[dst_res])
        return epi

    def epi_tm_dram(self, dst, dst_res_fn, col_fn, n=256):
        S = self.S

        def epi(pi, banks):
            stg, r_stg, sem = self.stgtm.next()
            sv = stg.rearrange("p (t n) -> p t n", n=256)
            for b, tl in banks:
                k = len(tl)
                S.op("dve", lambda e, b=b, tl=tl, k=k, sv=sv: e.tensor_copy(
                    out=sv[:, tl[0]:tl[0] + k, 0:n], in_=self.pb[b][:, 0:k * n].rearrange("p (t n) -> p t n", n=n)),
                    reads=[self.r_pb[b]], writes=[r_stg])
            c0 = col_fn(pi)
            dd = dst(pi) if callable(dst) else dst
            dv = dd[:, c0:c0 + n].rearrange("(t p) n -> p t n", p=128)
            S.op("sp", lambda e, sv=sv, dv=dv: e.dma_start(out=dv, in_=sv[:, :, 0:n]),
                 reads=[r_stg], writes=[dst_res_fn(pi)], dsem=sem)
        return epi

    def out_proj(self, x_src, x_src_res_fn, x_dst, final):
        S = self.S
        pend = {}

        def prefetch(pi):
            xo, r_xo, sem = self.xold.next()
            xv = xo.rearrange("p (t n) -> p t n", n=256)
            sv = x_src[:, pi * 256:(pi + 1) * 256].rearrange("(t p) n -> p t n", p=128)
            S.op("sp", lambda e, xv=xv, sv=sv: e.dma_start(out=xv, in_=sv),
                 reads=x_src_res_fn(pi), writes=[r_xo], dsem=sem)
            pend[pi] = (xv, r_xo, sem)

        def epi(pi, banks):
            xv, r_xo, sem = pend.pop(pi)
            for b, tl in banks:
                k = len(tl)
                S.op("dve", lambda e, b=b, tl=tl, k=k, xv=xv: e.tensor_tensor(
                    out=xv[:, tl[0]:tl[0] + k, :], in0=self.pb[b][:, 0:k * 256].rearrange("p (t n) -> p t n", n=256),
                    in1=xv[:, tl[0]:tl[0] + k, :], op=ALU.add),
                    reads=[self.r_pb[b], r_xo], writes=[r_xo])
            dv = x_dst[:, pi * 256:(pi + 1) * 256].rearrange("(t p) n -> p t n", p=128)
            tok = S.op("sp", lambda e, xv=xv, dv=dv: e.dma_start(out=dv, in_=xv),
                       reads=[r_xo], writes=[self.r_xcol[pi]], dsem=sem)
            if final:
                S.final.append(tok)
            if pi + 2 < 16:
                prefetch(pi + 2)
        prefetch(0)
        prefetch(1)
        self.gemm_tm(16, lambda kc, tt: self.actT[:, kc, tt * 128:(tt + 1) * 128], self.r_og, epi)

    def allgather(self, send, send_res, gath, gath_res, sem):
        S = self.S
        S.op("pool", lambda e: e.collective_compute("AllGather", ALU.bypass, replica_groups=RG,
                                                    ins=[send.opt()], outs=[gath.opt()]),
             reads=send_res, writes=[gath_res], dsem=sem, inc=1)

    def exchange(self, mla, do_k=True, do_v=True):
        if do_k:
            for c in range(4):
                self.allgather(self.sendK[c], self.r_sendK[8 * c:8 * c + 8], self.gathK[c], self.r_gathK[c], self.cc_sems[c])
        if not do_v:
            return
        if mla:
            for c in range(2):
                self.allgather(self.sendKr[c], self.r_sendKr[16 * c:16 * c + 16], self.gathKr[c], self.r_gathKr[c], self.cc_sems[4 + c])
        for c in range(4):
            rs = self.r_sendV[8 * c:8 * c + 8] if mla else self.r_sendV[4 * c:4 * c + 4]
            self.allgather(self.sendV[c], rs, self.gathV[c], self.r_gathV[c], self.cc_sems[6 + c])

    def load_masks(self, which):
        S = self.S
        S.op("sp", lambda e: e.dma_start(out=self.maskt, in_=self.mask_d[which]),
             writes=[self.r_mask], dsem=self.mask_sem)

    def attention(self, kind):
        S = self.S
        sb_mode = kind == "sb"
        krows = 128 if sb_mode else 192
        ident = self.cbv("ident")
        nident = self.cbv("nident")
        onesb = self.cbv("onesb")
        tri = self.cr("tri")
        ones = self.cr("ones")
        items = []
        for h in range(NH):
            for s in range(2):
                nkb = 2 if s == 0 else 4
                tl = []
                for j in reversed(range(nkb)):
                    for i in reversed(range(4)):
                        gt = 4 * GPOS[j] + i
                        if s == 0:
                            mi = j * 4 + i
                        elif j >= 2:
                            mi = 8 + (j - 2) * 4 + i
                        else:
                            mi = None
                        tl.append((gt, mi))
                for n_, (gt, mi) in enumerate(tl):
                    items.append(dict(h=h, s=s, gt=gt, mi=mi, first=(n_ == 0), last=(n_ == len(tl) - 1)))
        n = len(items)

        def load_head(h):
            hs = self.hset[h % 2]
            rs = self.r_hset[h % 2]
            sm = self.hsem[h % 2]
            kv = hs["k"].rearrange("p (r t) -> p r t", r=2)
            ksrc = self.gathK[h // 8].rearrange("(r f) t -> f r t", r=2)[(h % 8) * 128:(h % 8 + 1) * 128]
            S.op("sp", lambda e: e.dma_start(out=kv, in_=ksrc), reads=[self.r_gathK[h // 8]], writes=[rs["k"]], dsem=sm["k"])
            if sb_mode:
                S.op("sp", lambda e: e.dma_start(out=hs["q"], in_=self.qT[h * 128:(h + 1) * 128, :]),
                     reads=[self.r_qT[h]], writes=[rs["q"]], dsem=sm["q"])
                S.op("dve", lambda e: e.tensor_scalar(out=hs["q2"], in0=hs["q"], scalar1=-1.0, scalar2=None, op0=ALU.mult),
                     reads=[rs["q"]], writes=[rs["q2"]])
            else:
                S.op("sp", lambda e: e.dma_start(out=hs["q"], in_=self.qT[h * 192:h * 192 + 128, :]),
                     reads=[self.r_qT[h]], writes=[rs["q"]], dsem=sm["q"])
                S.op("sp", lambda e: e.dma_start(out=hs["q2"][0:64, :], in_=self.qT[h * 192 + 128:h * 192 + 192, :]),
                     reads=[self.r_qT[h]], writes=[rs["q2"]], dsem=sm["q2"])
                kv2 = hs["k2"].rearrange("p (r t) -> p r t", r=2)
                k2src = self.gathKr[h // 16].rearrange("(r f) t -> f r t", r=2)[(h % 16) * 64:(h % 16 + 1) * 64]
                S.op("sp", lambda e: e.dma_start(out=kv2[0:64], in_=k2src),
                     reads=[self.r_gathKr[h // 16]], writes=[rs["k2"]], dsem=sm["k2"])
            vv = hs["v"].rearrange("p (t d) -> p t d", d=128)
            vsrc = self.gathV[h // 8][:, (h % 8) * 128:(h % 8 + 1) * 128].rearrange("(t p) d -> p t d", p=128)
            S.op("sp", lambda e: e.dma_start(out=vv, in_=vsrc), reads=[self.r_gathV[h // 8]], writes=[rs["v"]], dsem=sm["v"])
            S.op("sp", lambda e: e.dma_start(out=hs["g"], in_=self.gateT[h * 128:(h + 1) * 128, :]),
                 reads=[self.r_gate[h]], writes=[rs["g"]], dsem=sm["g"])

        ZB = [0, 1]
        CBK = [2, 3]
        OT = [4, 5]
        DEN = [6, 7]
        st = {}

        def stageA(ix):
            it = items[ix]
            h, s, gt, mi = it["h"], it["s"], it["gt"], it["mi"]
            hs, rs = self.hset[h % 2], self.r_hset[h % 2]
            zb = ZB[ix % 2]
            qs = hs["q"][:, s * 512:(s + 1) * 512]
            kt = hs["k"][:, gt * 128:(gt + 1) * 128]
            mk = None if mi is None else self.maskt[:, mi * 512:(mi + 1) * 512]

            def f1(e):
                if sb_mode:
                    ins = e.matmul(self.pb[zb][:, :], lhsT=kt, rhs=qs, start=True, stop=(mk is None))
                    if mk is not None:
                        ins = e.matmul(self.pb[zb][:, :], lhsT=ident, rhs=mk, start=False, stop=True)
                else:
                    e.matmul(self.pb[zb][:, :], lhsT=kt, rhs=qs, start=True, stop=False)
                    ins = e.matmul(self.pb[zb][:, :], lhsT=hs["k2"][:, gt * 128:(gt + 1) * 128],
                                   rhs=hs["q2"][:, s * 512:(s + 1) * 512], start=False, stop=True)
                return ins
            rd = [rs["q"], rs["k"], self.r_const] + ([] if sb_mode else [rs["q2"], rs["k2"]]) + ([self.r_mask] if (mk is not None and sb_mode) else [])
            S.op("pe", f1, reads=rd, writes=[self.r_pb[zb]])
            if sb_mode:
                esp, r_esp, _ = self.esp.next()
                eb, r_eb, _ = self.ebuf.next()
                S.op("act", lambda e: e.activation(out=eb, in_=self.pb[zb][:, :], func=AF.Exp),
                     reads=[self.r_pb[zb]], writes=[r_eb])
                S.op("act", lambda e: e.activation(out=r32(esp), in_=eb, func=AF.Ln, bias=1.0),
                     reads=[r_eb], writes=[r_esp])
                prev = st.get("lsum")
                if it["first"]:
                    ls, r_ls, _ = self.lsum.next()
                    S.op("dve", lambda e: e.tensor_copy(out=r32(ls), in_=esp), reads=[r_esp], writes=[r_ls])
                else:
                    pls, r_pls = prev
                    ls, r_ls, _ = self.lsum.next()
                    S.op("dve", lambda e: e.tensor_tensor(out=r32(ls), in0=pls, in1=esp, op=ALU.add),
                         reads=[r_esp, r_pls], writes=[r_ls])
                st["lsum"] = (ls, r_ls)
                it["esp"] = (esp, r_esp)
                it["lprev"] = prev if not it["first"] else None
            else:
                w, r_w, _ = self.wt.next()
                S.op("act", lambda e: e.activation(out=w, in_=self.pb[zb][:, :], func=AF.Exp),
                     reads=[self.r_pb[zb]], writes=[r_w])
                if mk is not None:
                    S.op("dve", lambda e: e.tensor_tensor(out=w, in0=w, in1=mk, op=ALU.mult),
                         reads=[r_w, self.r_mask], writes=[r_w])
                it["w"] = (w, r_w)

        def stageB(ix):
            if not sb_mode:
                return
            it = items[ix]
            h, s, gt, mi = it["h"], it["s"], it["gt"], it["mi"]
            hs, rs = self.hset[h % 2], self.r_hset[h % 2]
            cbk = CBK[ix % 2]
            esp, r_esp = it["esp"]
            kt = hs["k"][:, gt * 128:(gt + 1) * 128]
            nq = hs["q2"][:, s * 512:(s + 1) * 512]
            mk = None if mi is None else self.maskt[:, mi * 512:(mi + 1) * 512]
            lp = it["lprev"]

            def f2(e):
                e.matmul(self.pb[cbk][:, :], lhsT=tri, rhs=r32(esp), start=True, stop=False)
                if lp is not None:
                    e.matmul(self.pb[cbk][:, :], lhsT=ones, rhs=r32(lp[0]), start=False, stop=False)
                ins = e.matmul(self.pb[cbk][:, :], lhsT=kt, rhs=nq, start=False, stop=(mk is None))
                if mk is not None:
                    ins = e.matmul(self.pb[cbk][:, :], lhsT=nident, rhs=mk, start=False, stop=True)
                return ins
            rd = [r_esp, rs["k"], rs["q2"], self.r_const] + ([lp[1]] if lp is not None else []) + ([self.r_mask] if mk is not None else [])
            S.op("pe", f2, reads=rd, writes=[self.r_pb[cbk]])
            w, r_w, _ = self.wt.next()
            S.op("act", lambda e: e.activation(out=w, in_=self.pb[cbk][:, :], func=AF.Exp, scale=-1.0),
                 reads=[self.r_pb[cbk]], writes=[r_w])
            it["w"] = (w, r_w)

        def stageC(ix):
            it = items[ix]
            h, s, gt = it["h"], it["s"], it["gt"]
            hs, rs = self.hset[h % 2], self.r_hset[h % 2]
            par = (2 * h + s) % 2
            ot = OT[par]
            w, r_w = it["w"]
            vt = hs["v"][:, gt * 128:(gt + 1) * 128]

            def f3(e):
                ins = e.matmul(self.pb[ot][:, :], lhsT=vt, rhs=w, start=it["first"], stop=it["last"])
                if not sb_mode:
                    ins = e.matmul(self.pb[DEN[par]][:, :], lhsT=onesb, rhs=w, start=it["first"], stop=it["last"])
                return ins
            wr = [self.r_pb[ot]] + ([] if sb_mode else [self.r_pb[DEN[par]]])
            S.op("pe", f3, reads=[r_w, rs["v"], self.r_const], writes=wr)
            if it["last"]:
                dst = self.actT[:, h, s * 512:(s + 1) * 512]
                gs = hs["g"][:, s * 512:(s + 1) * 512]
                if sb_mode:
                    S.op("dve", lambda e: e.tensor_tensor(out=dst, in0=self.pb[ot][:, :], in1=gs, op=ALU.mult),
                         reads=[self.r_pb[ot], rs["g"]], writes=[self.r_og[h]])
                else:
                    fb, r_fb, _ = self.fin.next()
                    S.op("act", lambda e: e.activation(out=fb, in_=self.pb[DEN[par]][:, :], func=AF.Ln),
                         reads=[self.r_pb[DEN[par]]], writes=[r_fb])
                    S.op("act", lambda e: e.activation(out=fb, in_=fb, func=AF.Exp, scale=-1.0),
                         reads=[r_fb], writes=[r_fb])
                    S.op("dve", lambda e: e.tensor_tensor(out=fb, in0=self.pb[ot][:, :], in1=fb, op=ALU.mult),
                         reads=[self.r_pb[ot], r_fb], writes=[r_fb])
                    S.op("dve", lambda e: e.tensor_tensor(out=dst, in0=fb, in1=gs, op=ALU.mult),
                         reads=[r_fb, rs["g"]], writes=[self.r_og[h]])
            if it["last"] and s == 1 and h + 2 < NH:
                load_head(h + 2)

        if not sb_mode:
            for i in range(2):
                for kk in ("q2", "k2"):
                    S.op("pool", lambda e, i=i, kk=kk: e.memset(self.hset[i][kk][64:128, :], 0.0),
                         writes=[self.r_hset[i][kk]])
        if sb_mode:
            pairs = [(items[2 * i], items[2 * i + 1]) for i in range(n // 2)]
            npairs = len(pairs)
            ZP = [self.pbig[0], self.pbig[3]]
            CP = self.pbig[1]
            r_ZP = [[self.r_pb[0], self.r_pb[1]], [self.r_pb[6], self.r_pb[7]]]
            r_CP = [self.r_pb[2], self.r_pb[3]]
            half = lambda ap, x: ap[:, x * 512:(x + 1) * 512]
            pst = {"L": None}

            def mask_of(it):
                return None if it["mi"] is None else self.maskt[:, it["mi"] * 512:(it["mi"] + 1) * 512]

            def pA(pi):
                pr = pairs[pi]
                h, s = pr[0]["h"], pr[0]["s"]
                hs, rs = self.hset[h % 2], self.r_hset[h % 2]
                zp, rz = ZP[pi % 2], r_ZP[pi % 2]
                qs = hs["q"][:, s * 512:(s + 1) * 512]

                def f1(e):
                    for x in range(2):
                        gt = pr[x]["gt"]
                        ins = e.matmul(half(zp, x), lhsT=hs["k"][:, gt * 128:(gt + 1) * 128], rhs=qs, start=True, stop=True)
                    return ins
                S.op("pe", f1, reads=[rs["q"], rs["k"]], writes=rz)
                eb, r_eb, _ = self.ebp.next()
                S.op("act", lambda e: e.activation(out=eb, in_=zp[:, :], func=AF.Exp), reads=rz, writes=[r_eb])
                for x in range(2):
                    mk = mask_of(pr[x])
                    if mk is not None:
                        S.op("dve", lambda e, x=x, mk=mk: e.tensor_tensor(out=half(eb, x), in0=half(eb, x), in1=mk, op=ALU.mult),
                             reads=[r_eb, self.r_mask], writes=[r_eb])
                pr[0]["eb"] = (eb, r_eb)

            def pLn(pi):
                pr = pairs[pi]
                eb, r_eb = pr[0]["eb"]
                sp, r_sp, _ = self.spp.next()
                S.op("act", lambda e: e.activation(out=r32(sp), in_=eb, func=AF.Ln, bias=1.0), reads=[r_eb], writes=[r_sp])
                lprev = None if pr[0]["first"] else pst["L"]
                ln_, r_ln, _ = self.lp.next()
                if lprev is None:
                    S.op("dve", lambda e: e.tensor_tensor(out=r32(ln_), in0=half(sp, 0), in1=half(sp, 1), op=ALU.add),
                         reads=[r_sp], writes=[r_ln])
                else:
                    S.op("dve", lambda e: e.tensor_tensor(out=r32(ln_), in0=lprev[0], in1=half(sp, 0), op=ALU.add),
                         reads=[r_sp, lprev[1]], writes=[r_ln])
                    S.op("dve", lambda e: e.tensor_tensor(out=r32(ln_), in0=ln_, in1=half(sp, 1), op=ALU.add),
                         reads=[r_sp, r_ln], writes=[r_ln])
                pst["L"] = (ln_, r_ln)
                pr[0]["sp"] = (sp, r_sp)
                pr[0]["lprev"] = lprev

            def pB(pi):
                pr = pairs[pi]
                h, s = pr[0]["h"], pr[0]["s"]
                hs, rs = self.hset[h % 2], self.r_hset[h % 2]
                sp, r_sp = pr[0]["sp"]
                lprev = pr[0]["lprev"]
                nq = hs["q2"][:, s * 512:(s + 1) * 512]

                def f2(e):
                    for x in range(2):
                        gt = pr[x]["gt"]
                        e.matmul(half(CP, x), lhsT=tri, rhs=r32(half(sp, x)), start=True, stop=False)
                        if x == 1:
                            e.matmul(half(CP, x), lhsT=ones, rhs=r32(half(sp, 0)), start=False, stop=False)
                        if lprev is not None:
                            e.matmul(half(CP, x), lhsT=ones, rhs=r32(lprev[0]), start=False, stop=False)
                        ins = e.matmul(half(CP, x), lhsT=hs["k"][:, gt * 128:(gt + 1) * 128], rhs=nq, start=False, stop=True)
                    return ins
                rd = [r_sp, rs["k"], rs["q2"], self.r_const] + ([lprev[1]] if lprev is not None else [])
                S.op("pe", f2, reads=rd, writes=r_CP)
                w, r_w, _ = self.wp.next()
                S.op("act", lambda e: e.activation(out=w, in_=CP[:, :], func=AF.Exp, scale=-1.0), reads=r_CP, writes=[r_w])
                for x in range(2):
                    mk = mask_of(pr[x])
                    if mk is not None:
                        S.op("dve", lambda e, x=x, mk=mk: e.tensor_tensor(out=half(w, x), in0=half(w, x), in1=mk, op=ALU.mult),
                             reads=[r_w, self.r_mask], writes=[r_w])
                pr[0]["w"] = (w, r_w)

            def pC(pi):
                pr = pairs[pi]
                h, s = pr[0]["h"], pr[0]["s"]
                hs, rs = self.hset[h % 2], self.r_hset[h % 2]
                ot = OT[(2 * h + s) % 2]
                w, r_w = pr[0]["w"]

                def f3(e):
                    for x in range(2):
                        gt = pr[x]["gt"]
                        ins = e.matmul(self.pb[ot][:, :], lhsT=hs["v"][:, gt * 128:(gt + 1) * 128], rhs=half(w, x),
                                       start=(x == 0 and pr[0]["first"]), stop=(x == 1 and pr[1]["last"]))
                    return ins
                S.op("pe", f3, reads=[r_w, rs["v"]], writes=[self.r_pb[ot]])
                if pr[1]["last"]:
                    dst = self.actT[:, h, s * 512:(s + 1) * 512]
                    gs = hs["g"][:, s * 512:(s + 1) * 512]
                    S.op("dve", lambda e: e.tensor_tensor(out=dst, in0=self.pb[ot][:, :], in1=gs, op=ALU.mult),
                         reads=[self.r_pb[ot], rs["g"]], writes=[self.r_og[h]])
                    if s == 1 and h + 2 < NH:
                        load_head(h + 2)

            load_head(0)
            load_head(1)
            for step in range(npairs + 3):
                if step < npairs:
                    pA(step)
                if 1 <= step <= npairs:
                    pLn(step - 1)
                if 2 <= step <= npairs + 1:
                    pB(step - 2)
                if 3 <= step <= npairs + 2:
                    pC(step - 3)
            return
        load_head(0)
        load_head(1)
        for step in range(n + 2):
            if step < n:
                stageA(step)
            if 1 <= step <= n:
                stageB(step - 1)
            if 2 <= step <= n + 1:
                stageC(step - 2)

    def build(self):
        self.build_plan()
        self.load_consts()
        nl = len(self.kinds)
        for li, (kind, j) in enumerate(self.kinds):
            x_src = self.x_in if li == 0 else self.xres
            x_res = [] if li == 0 else self.r_xcol
            final = li == nl - 1
            x_dst = self.out if final else self.xres
            if kind == "sb":
                self.sb_layer(li, j, x_src, x_res, x_dst, final)
            else:
                self.mla_layer(li, j, x_src, x_res, x_dst, final)
        self.S.replay()
        return self.nc


def _phase(self, new):
    old = self.cur_phase
    self.cur_phase = new
    if old is None or old == new:
        return
    toks = {}
    for r in self.phase_res[old]:
        cand = list(r.readers.values()) + ([r.last_w] if r.last_w is not None else [])
        for t in cand:
            k = id(t.sem)
            if k not in toks or toks[k].val < t.val:
                toks[k] = t
    for r in self.phase_res[new]:
        for k, t in toks.items():
            if k not in r.readers or r.readers[k].val < t.val:
                r.readers[k] = t


def _alias(self, old_list, new_list):
    toks = {}
    for r in old_list:
        cand = list(r.readers.values()) + ([r.last_w] if r.last_w is not None else [])
        for t in cand:
            k = id(t.sem)
            if k not in toks or toks[k].val < t.val:
                toks[k] = t
    for r in new_list:
        for k, t in toks.items():
            if k not in r.readers or r.readers[k].val < t.val:
                r.readers[k] = t


def _latent_norm(self, src_fn, nch, gname, dst, dst_res, src_res, inv_n):
    S = self.S
    ones = self.cr("ones")
    g = self.c(gname)
    P = self.prep_f
    RP = self.r_prep
    bs = [self.next_bank(), self.next_bank()]
    for c in range(nch):
        sq, r_sq = (P[0], RP[0]) if c % 2 == 0 else (P[2], RP[2])
        S.op("act", lambda e, c=c, sq=sq: e.activation(out=r32(sq), in_=src_fn(c), func=AF.Square),
             reads=[src_res], writes=[r_sq])
        for s in range(2):
            S.op("pe", lambda e, c=c, s=s, sq=sq: e.matmul(self.pb[bs[s]][:, :], lhsT=ones, rhs=r32(sq[:, s * 512:(s + 1) * 512]),
                                                          start=(c == 0), stop=(c == nch - 1)),
                 reads=[r_sq, self.r_const], writes=[self.r_pb[bs[s]]])
    for s in range(2):
        S.op("act", lambda e, s=s: e.activation(out=P[1][:, s * 512:(s + 1) * 512], in_=self.pb[bs[s]][:, :], func=AF.Ln,
                                                scale=float(inv_n), bias=EPS),
             reads=[self.r_pb[bs[s]]], writes=[RP[1]])
    S.op("act", lambda e: e.activation(out=P[1], in_=P[1], func=AF.Exp, scale=-0.5), reads=[RP[1]], writes=[RP[1]])
    for c in range(nch):
        S.op("dve", lambda e, c=c: e.scalar_tensor_tensor(out=dst[:, c, :], in0=src_fn(c), scalar=g[:, c:c + 1], in1=P[1],
                                                         op0=ALU.mult, op1=ALU.mult),
             reads=[src_res, RP[1], self.r_const], writes=[dst_res])


def _prep(self, bn, rope, gname, lnscale, dn, dr, fill1=None, fill2=None, alt=None):
    S = self.S
    ones = self.cr("ones")
    ones64 = self.cr("ones", 64)
    rT = self.cr("rT", 64)
    cos = self.c("cos", 64)
    sin = self.c("sin", 64)
    g = self.c(gname)
    gn = g[:, 0:1]
    gr = g[0:64, 1:2]
    P, RP = list(self.prep_f), list(self.r_prep)
    if alt is not None:
        for k_, i_ in enumerate((1, 3, 4)):
            P[i_] = self.prep_alt[alt][k_]
            RP[i_] = self.r_prep_alt[k_]
    sqr, r_sqr = self.sqr, self.r_sqr
    H = lambda ap, s: ap[:, s * 512:(s + 1) * 512]
    self.held.update(bn)
    for s in range(2):
        S.op("act", lambda e, s=s: e.activation(out=r32(H(P[0], s)), in_=self.pb[bn[s]][:, :], func=AF.Square),
             reads=[self.r_pb[bn[s]]], writes=[RP[0]])
    if rope[0] == "psum":
        for s in range(2):
            S.op("act", lambda e, s=s: e.activation(out=r32(H(sqr[0:64], s)), in_=self.pb[rope[1][s]][0:64, :], func=AF.Square),
                 reads=[self.r_pb[rope[1][s]]], writes=[r_sqr])
        rres = [self.r_pb[b] for b in rope[1]]
        rsrc = lambda s: self.pb[rope[1][s]][0:64, :]
    else:
        S.op("act", lambda e: e.activation(out=r32(sqr[0:64]), in_=rope[1], func=AF.Square), reads=[rope[2]], writes=[r_sqr])
        rres = [rope[2]]
        rsrc = lambda s: H(rope[1], s)
    for s in range(2):
        S.op("dve", lambda e, s=s: e.tensor_scalar(out=r32(H(P[2][0:64], s)), in0=rsrc(s), scalar1=gr, scalar2=None, op0=ALU.mult),
             reads=rres + [self.r_const, r_sqr], writes=[RP[2]])
    if fill1 is not None:
        fill1()
    bs = [self.next_bank(), self.next_bank()]
    for s in range(2):
        def fss(e, s=s):
            e.matmul(self.pb[bs[s]][:, :], lhsT=ones, rhs=r32(H(P[0], s)), start=True, stop=False)
            return e.matmul(self.pb[bs[s]][:, :], lhsT=ones64, rhs=r32(H(sqr[0:64], s)), start=False, stop=True)
        S.op("pe", fss, reads=[RP[0], r_sqr, self.r_const], writes=[self.r_pb[bs[s]]])
    if fill2 is not None:
        fill2()
    br = [self.next_bank(), self.next_bank()]
    for s in range(2):
        S.op("pe", lambda e, s=s: e.matmul(self.pb[br[s]][0:64, :], lhsT=rT, rhs=r32(H(P[2][0:64], s)), start=True, stop=True),
             reads=[RP[2], self.r_const], writes=[self.r_pb[br[s]]])
    for s in range(2):
        S.op("act", lambda e, s=s: e.activation(out=H(P[1], s), in_=self.pb[bs[s]][:, :], func=AF.Ln, scale=1.0 / 192, bias=EPS),
             reads=[self.r_pb[bs[s]]], writes=[RP[1]])
    S.op("act", lambda e: e.activation(out=P[1], in_=P[1], func=AF.Exp, scale=-0.5, bias=float(lnscale)),
         reads=[RP[1]], writes=[RP[1]])
    S.op("dve", lambda e: e.tensor_tensor(out=P[3][0:64], in0=P[2][0:64], in1=cos, op=ALU.mult),
         reads=[RP[2], self.r_const], writes=[RP[3]])
    for s in range(2):
        S.op("dve", lambda e, s=s: e.tensor_tensor(out=H(P[4][0:64], s), in0=self.pb[br[s]][0:64, :], in1=H(sin, s), op=ALU.mult),
             reads=[self.r_pb[br[s]], self.r_const], writes=[RP[4]])
    S.op("dve", lambda e: e.tensor_tensor(out=P[3][0:64], in0=P[3][0:64], in1=P[4][0:64], op=ALU.add),
         reads=[RP[3], RP[4]], writes=[RP[3]])
    stg, r_stg, sem = self.stg.next()
    for s in range(2):
        S.op("dve", lambda e, s=s: e.scalar_tensor_tensor(out=H(stg, s), in0=self.pb[bn[s]][:, :], scalar=gn, in1=H(P[1], s),
                                                         op0=ALU.mult, op1=ALU.mult),
             reads=[self.r_pb[bn[s]], RP[1], self.r_const], writes=[r_stg])
    S.op("sp", lambda e: e.dma_start(out=dn[0][dn[1]:dn[1] + 128, :], in_=stg), reads=[r_stg], writes=[dn[2]], dsem=sem)
    self.held.difference_update(bn)
    stg2, r_stg2, sem2 = self.stg.next()
    S.op("dve", lambda e: e.tensor_tensor(out=stg2[0:64], in0=P[3][0:64], in1=P[1][0:64], op=ALU.mult),
         reads=[RP[3], RP[1]], writes=[r_stg2])
    S.op("sp", lambda e: e.dma_start(out=dr[0][dr[1]:dr[1] + 64, :], in_=stg2[0:64]), reads=[r_stg2], writes=[dr[2]], dsem=sem2)


def _mla_layer(self, li, j, x_src, x_res, x_dst, final):
    S = self.S
    self.phase("norm")
    self.alias(self.r_og, self.r_actT)
    self.norm_phase(li, x_src, x_res)
    act = lambda kc, slot: self.actT[:, kc, slot * 512:(slot + 1) * 512]
    self.phase("kv")
    lat = self.lat
    gstate = [0]

    def gate_gen():
        gstg = self.carve(30720, 1024)
        for gi in range(16):
            wt, wres, KC, ncols = self.wnext()
            for mc in range(2):
                for slot in range(2):
                    b = self.next_bank()

                    def f(e, b=b, wt=wt, mc=mc, slot=slot):
                        for kc in range(32):
                            ins = e.matmul(self.pb[b][:, :], lhsT=wt[:, kc, mc * 128:(mc + 1) * 128], rhs=act(kc, slot),
                                           start=(kc == 0), stop=(kc == 31))
                        return ins
                    S.op("pe", f, reads=[wres] + self.r_actT[4 * slot:4 * slot + 4], writes=[self.r_pb[b]])
                    S.op("act", lambda e, b=b, slot=slot: e.activation(
                        out=gstg[:, slot * 512:(slot + 1) * 512], in_=self.pb[b][:, :], func=AF.Silu),
                        reads=[self.r_pb[b]], writes=[self.r_gstg])
                    if slot == 1:
                        r0 = (2 * gi + mc) * 128
                        S.op("sp", lambda e, r0=r0: e.dma_start(out=self.gateT[r0:r0 + 128, :], in_=gstg),
                             reads=[self.r_gstg], writes=[self.r_gate[2 * gi + mc]], dsem=self.gstg_sem)
                    if mc == 1 and slot == 1:
                        self.wfree(wres)
                    yield
    gg = gate_gen()

    def gate_group():
        next(gg, None)
    self.gemm_fm(3, act, self.r_actT, self.epi_fm_sbuf(
        lambda pi, mc, M, slot: lat[0:M, 2 * pi + mc, slot * 512:(slot + 1) * 512], self.r_lat))
    self.latent_norm(lambda c: lat[:, c, :], 4, f"kva{j}", self.kvn, self.r_kvn, self.r_lat, 1.0 / 512)
    self.alias([self.r_lat], self.r_prep_alt)
    kvn = self.kvn
    for h in range(NH):
        wt, wres, KC, ncols = self.wnext()
        bn = []
        for s in range(2):
            b = self.next_bank()

            def fk(e, b=b, s=s, wt=wt):
                for kc in range(4):
                    ins = e.matmul(self.pb[b][:, :], lhsT=wt[:, kc, 0:128], rhs=kvn[:, kc, s * 512:(s + 1) * 512],
                                   start=(kc == 0), stop=(kc == 3))
                return ins
            S.op("pe", fk, reads=[wres, self.r_kvn], writes=[self.r_pb[b]])
            bn.append(b)

        def epi_v(pi, banks, h=h):
            stg, r_stg, sem = self.stgtm.next()
            sv = stg.rearrange("p (t n) -> p t n", n=256)
            for b, tl in banks:
                k = len(tl)
                S.op("dve", lambda e, b=b, tl=tl, k=k, sv=sv: e.tensor_copy(
                    out=sv[:, tl[0]:tl[0] + k, 0:128], in_=self.pb[b][:, 0:k * 128].rearrange("p (t n) -> p t n", n=128)),
                    reads=[self.r_pb[b]], writes=[r_stg])
            dv = self.sendV[h // 8][:, (h % 8) * 128:(h % 8 + 1) * 128].rearrange("(t p) n -> p t n", p=128)
            S.op("sp", lambda e, sv=sv, dv=dv: e.dma_start(out=dv, in_=sv[:, :, 0:128]),
                 reads=[r_stg], writes=[self.r_sendV[h]], dsem=sem)
        vfill = lambda h=h, wt=wt, wres=wres, epi_v=epi_v: self.gemm_tm_piece(
            h, wt, wres, 4, 128, 128, lambda kc, tt: kvn[:, kc, tt * 128:(tt + 1) * 128], [self.r_kvn], epi_v)
        self.prep(bn, ("sbuf", lat[0:64, 4, :], self.r_lat), f"kn{j}", 0.0,
                  (self.sendK[h // 8], (h % 8) * 128, self.r_sendK[h]), (self.sendKr[h // 16], (h % 16) * 64, self.r_sendKr[h]),
                  fill1=vfill, fill2=gate_group, alt=("kv" if h % 2 else None))
        self.wfree(wres)
        if h % 8 == 7:
            c = h // 8
            self.allgather(self.sendK[c], self.r_sendK[8 * c:8 * c + 8], self.gathK[c], self.r_gathK[c], self.cc_sems[c])
            self.allgather(self.sendV[c], self.r_sendV[8 * c:8 * c + 8], self.gathV[c], self.r_gathV[c], self.cc_sems[6 + c])
        if h % 16 == 15:
            c = h // 16
            self.allgather(self.sendKr[c], self.r_sendKr[16 * c:16 * c + 16], self.gathKr[c], self.r_gathKr[c], self.cc_sems[4 + c])
    self.phase("q")
    qsel = lambda c: self.qlat[:, c, :]
    self.gemm_fm(4, act, self.r_actT, self.epi_fm_sbuf(
        lambda pi, mc, M, slot: qsel(2 * pi + mc)[0:M, slot * 512:(slot + 1) * 512], self.r_qlat))
    self.latent_norm(qsel, 8, f"qa{j}", self.qn, self.r_qn, self.r_qlat, 1.0 / 1024)
    self.alias([self.r_qlat], self.r_prep_alt)
    qn = self.qn
    for h in range(NH):
        wt, wres, KC, ncols = self.wnext()
        bn, br = [], []
        for (lst, c0, M) in ((bn, 0, 128), (br, 128, 64)):
            for s in range(2):
                b = self.next_bank()

                def fq(e, b=b, s=s, wt=wt, c0=c0, M=M):
                    for kc in range(8):
                        ins = e.matmul(self.pb[b][0:M, :], lhsT=wt[:, kc, c0:c0 + M], rhs=qn[:, kc, s * 512:(s + 1) * 512],
                                       start=(kc == 0), stop=(kc == 7))
                    return ins
                S.op("pe", fq, reads=[wres, self.r_qn], writes=[self.r_pb[b]])
                lst.append(b)
        self.prep(bn, ("psum", br), f"qn{j}", math.log(192 ** -0.5),
                  (self.qT, h * 192, self.r_qT[h]), (self.qT, h * 192 + 128, self.r_qT[h]), fill1=gate_group,
                  alt=("q" if h % 2 else None))
        self.wfree(wres)
    self.phase("attn")
    self.alias(self.r_actT, self.r_og)
    self.load_masks(0)
    self.attention("mla")
    self.phase("outp")
    self.out_proj(x_src, (lambda pi: []) if li == 0 else (lambda pi: [self.r_xcol[pi]]), x_dst, final)


def _sb_layer(self, li, j, x_src, x_res, x_dst, final):
    self.phase("norm")
    self.alias(self.r_og, self.r_actT)
    self.norm_phase(li, x_src, x_res)
    act = lambda kc, slot: self.actT[:, kc, slot * 512:(slot + 1) * 512]
    self.phase("outp")
    self.gemm_fm(16, act, self.r_actT, self.epi_fm_dram(
        lambda pi, mc: self.sendK[(2 * pi + mc) // 8], lambda pi, mc: self.r_sendK[2 * pi + mc],
        lambda pi, mc: ((2 * pi + mc) % 8) * 128))
    self.exchange(False, do_v=False)
    self.gemm_tm(16, lambda kc, tt: self.actT[:, kc, tt * 128:(tt + 1) * 128], self.r_actT,
                 self.epi_tm_dram(lambda pi: self.sendV[pi // 4], lambda pi: self.r_sendV[pi], lambda pi: (pi % 4) * 256))
    self.exchange(False, do_k=False)
    self.gemm_fm(16, act, self.r_actT, self.epi_fm_dram(
        self.qT, lambda pi, mc: self.r_qT[2 * pi + mc], lambda pi, mc: (2 * pi + mc) * 128, scale=128 ** -0.5))
    self.gemm_fm(16, act, self.r_actT, self.epi_fm_dram(
        self.gateT, lambda pi, mc: self.r_gate[2 * pi + mc], lambda pi, mc: (2 * pi + mc) * 128, func=AF.Silu))
    self.phase("attn")
    self.alias(self.r_actT, self.r_og)
    self.load_masks(1)
    self.attention("sb")
    self.phase("outp")
    self.out_proj(x_src, (lambda pi: []) if li == 0 else (lambda pi: [self.r_xcol[pi]]), x_dst, final)


Builder.phase = _phase
Builder.alias = _alias
Builder.latent_norm = _latent_norm
Builder.prep = _prep
Builder.mla_layer = _mla_layer
Builder.sb_layer = _sb_layer


def _rope_tables():
    inv = (1.0 / (10000.0 ** (np.arange(0, 64, 2, dtype=np.float32) / np.float32(64)))).astype(np.float32)
    ang = np.arange(S_FULL, dtype=np.float32)[:, None] * inv[None, :]
    return np.cos(ang).astype(np.float32), np.sin(ang).astype(np.float32)


def _local_tokens(r):
    blks = [0, 3] if r == 0 else [1, 2]
    return np.concatenate([np.arange(b * 512, (b + 1) * 512) for b in blks])


def _make_consts(inputs, kinds, r):
    cf = np.zeros((128, CFW), np.float32)

    def put(name, arr):
        o, w = CF[name]
        cf[:arr.shape[0], o:o + arr.shape[1]] = arr
    for li, (kind, j) in enumerate(kinds):
        g = inputs["mla_norm_g" if kind == "mla" else "sb_norm_g"][j]
        put(f"g{li}", g.reshape(32, 128).T)
    for j in range(inputs["mla_q_a_g"].shape[0]):
        put(f"qa{j}", inputs["mla_q_a_g"][j].reshape(8, 128).T)
        put(f"kva{j}", inputs["mla_kv_a_g"][j].reshape(4, 128).T)
        for nm, key in (("qn", "mla_q_norm_g"), ("kn", "mla_k_norm_g")):
            a = np.zeros((128, 2), np.float32)
            a[:, 0] = inputs[key][j][:128]
            a[:64, 1] = inputs[key][j][128:]
            put(f"{nm}{j}", a)
    jj, ss = np.meshgrid(np.arange(128), np.arange(128), indexing="ij")
    put("tri", (jj >= ss).astype(np.float32))
    put("ones", np.ones((128, 128), np.float32))
    R = np.zeros((64, 64), np.float32)
    for i in range(32):
        R[i, i + 32] = -1.0
        R[i + 32, i] = 1.0
    put("rT", R.T.copy())
    cos, sin = _rope_tables()
    lt = _local_tokens(r)
    put("cos", np.concatenate([cos[lt].T, cos[lt].T], axis=0))
    put("sin", np.concatenate([sin[lt].T, sin[lt].T], axis=0))
    cb = np.zeros((128, CBW), np.float32)
    cb[:, 0:128] = np.eye(128)
    cb[:, 128:256] = -np.eye(128)
    cb[:, 256:384] = 1.0
    masks = np.zeros((2, 128, 16, 512), np.float32)
    qblk = [0, 3] if r == 0 else [1, 2]
    slots = [(0, 0), (0, 1), (1, 2), (1, 3)]
    for gi, (s, kb) in enumerate(slots):
        for i in range(4):
            kpos = kb * 512 + i * 128 + np.arange(128)[:, None]
            qpos = qblk[s] * 512 + np.arange(512)[None, :]
            masks[0, :, gi * 4 + i, :] = np.where(kpos <= qpos, 1.0, 0.0)
            masks[1, :, gi * 4 + i, :] = np.where(kpos < qpos, 1.0, 0.0)
    return cf, cb.astype(ml_dtypes.bfloat16), masks.reshape(2, 128, 16 * 512).astype(ml_dtypes.bfloat16)


_NC_CACHE = {}


def run_layers(x, inputs, kinds):
    key = tuple(kinds)
    if key not in _NC_CACHE:
        _NC_CACHE[key] = Builder(list(kinds)).build()
    nc = _NC_CACHE[key]
    mj = sorted({j for k, j in kinds if k == "mla"})
    sj = sorted({j for k, j in kinds if k == "sb"})
    wmaps = {}
    if mj:
        for nm in ("mla_w_in", "mla_w_qb", "mla_w_kvb", "mla_w_out"):
            wmaps[nm] = np.ascontiguousarray(inputs[nm][mj[0]:mj[-1] + 1])
    if sj:
        for nm in ("sb_w_in", "sb_w_out"):
            wmaps[nm] = np.ascontiguousarray(inputs[nm][sj[0]:sj[-1] + 1])
    kk = [(k, j - (mj[0] if k == "mla" else sj[0])) for k, j in kinds]
    assert kk == list(kinds) or len(kinds) == 1 or True
    in_maps = []
    for c in range(8):
        b, r = divmod(c, 2)
        cf, cb, masks = _make_consts(inputs, kinds, r)
        m = {"x": np.ascontiguousarray(x[b][_local_tokens(r)]), "cf": cf, "cb": cb, "masks": masks}
        m.update(wmaps)
        in_maps.append(m)
    res = run_bass_kernel_spmd(nc, in_maps, core_ids=list(range(8)))
    out = np.empty((4, S_FULL, D), np.float32)
    for c in range(8):
        b, r = divmod(c, 2)
        out[b][_local_tokens(r)] = np.asarray(res.results[c]["out"])
    return out


def kernel(**inputs):
    inputs = {k: np.asarray(v) for k, v in inputs.items()}
    kinds = (("mla", 0), ("sb", 0), ("mla", 1), ("sb", 1))
    return run_layers(inputs["x"].astype(np.float32), inputs, kinds)
```

```python
import math
from contextlib import ExitStack

import numpy as np
import ml_dtypes
import concourse.bass as bass
import concourse.mybir as mybir
from concourse.bass_utils import run_bass_kernel_spmd

F32 = mybir.dt.float32
BF16 = mybir.dt.bfloat16
F32R = mybir.dt.float32r
AF = mybir.ActivationFunctionType
ALU = mybir.AluOpType

D = 4096
T = 1024
S_FULL = 2048
NH = 32
EPS = 1e-6
NEG = -30000.0
RG = [[0, 1], [2, 3], [4, 5], [6, 7]]
GPOS = [0, 2, 3, 1]
WELEMS = 8192
NWBUF = 3

CF = {}
_off = 0
for _name, _w in [("g0", 32), ("g1", 32), ("g2", 32), ("g3", 32),
                  ("qa0", 8), ("qa1", 8), ("kva0", 4), ("kva1", 4),
                  ("qn0", 2), ("qn1", 2), ("kn0", 2), ("kn1", 2),
                  ("tri", 128), ("ones", 128), ("rT", 64),
                  ("cos", 1024), ("sin", 1024)]:
    CF[_name] = (_off, _w)
    _off += _w
CFW = _off
CB = {"ident": (0, 128), "nident": (128, 128), "onesb": (256, 128)}
CBW = 384


class Tok:
    __slots__ = ("sem", "val", "eng", "inc")

    def __init__(self, sem, val, eng, inc):
        self.sem, self.val, self.eng, self.inc = sem, val, eng, inc


class Res:
    __slots__ = ("name", "last_w", "readers")

    def __init__(self, name):
        self.name = name
        self.last_w = None
        self.readers = {}


class Sched:
    ENGS = ("pe", "act", "dve", "pool", "sp")

    def __init__(self, nc, stack):
        self.nc = nc
        self.stack = stack
        self.streams = {e: [] for e in self.ENGS}
        self.esem = {e: stack.enter_context(nc.semaphore("s_" + e)) for e in self.ENGS}
        self.ecount = {e: 0 for e in self.ENGS}
        self.waited = {e: {} for e in self.ENGS}
        self.semcount = {}
        self.final = []
        self.nsem = 0

    def new_sem(self, name):
        self.nsem += 1
        return self.stack.enter_context(self.nc.semaphore(f"{name}_{self.nsem}"))

    def op(self, eng, fn, reads=(), writes=(), dsem=None, inc=16):
        deps = []
        for r in reads:
            if r.last_w is not None:
                deps.append((r.last_w, True))
        for w in writes:
            if w.last_w is not None:
                deps.append((w.last_w, False))
            for t in w.readers.values():
                deps.append((t, False))
        waits = {}
        for t, raw in deps:
            if t.eng == eng and t.inc == 1 and t.sem is self.esem[eng]:
                if not (raw and eng in ("act", "dve", "pool")):
                    continue
            k = id(t.sem)
            if k not in waits or waits[k][1] < t.val:
                waits[k] = (t.sem, t.val)
        wl = []
        for k, (s, v) in waits.items():
            if self.waited[eng].get(k, 0) < v:
                self.waited[eng][k] = v
                wl.append((s, v))
        if dsem is not None:
            k = id(dsem)
            self.semcount[k] = self.semcount.get(k, 0) + inc
            tok = Tok(dsem, self.semcount[k], eng, inc)
        else:
            self.ecount[eng] += 1
            tok = Tok(self.esem[eng], self.ecount[eng], eng, 1)
        self.streams[eng].append((wl, fn, tok))
        for r in reads:
            k = id(tok.sem)
            r.readers[k] = tok
        for w in writes:
            w.last_w = tok
            w.readers = {}
        return tok

    def replay(self):
        nc = self.nc
        with nc.Block() as block:
            def run(e, name):
                for wl, fn, tok in self.streams[name]:
                    for s, v in wl:
                        e.wait_ge(s, v)
                    ins = fn(e)
                    if tok.inc == 1 and tok.sem is not self.esem[name]:
                        ins.then_inc(tok.sem)
                    else:
                        ins.then_inc(tok.sem, tok.inc)
                if name == "sp":
                    done = {}
                    for t in self.final:
                        k = id(t.sem)
                        if k not in done or done[k][1] < t.val:
                            done[k] = (t.sem, t.val)
                    for s, v in done.values():
                        e.wait_ge(s, v)

            @block.tensor
            def _(e):
                run(e, "pe")

            @block.scalar
            def _(e):
                run(e, "act")

            @block.vector
            def _(e):
                run(e, "dve")

            @block.gpsimd
            def _(e):
                run(e, "pool")

            @block.sync
            def _(e):
                run(e, "sp")


class Buf:
    def __init__(self, S, name, aps):
        self.aps = aps
        self.res = [Res(f"{name}{i}") for i in range(len(aps))]
        self.sems = [S.new_sem(name) for _ in aps]
        self.i = -1

    def next(self):
        self.i = (self.i + 1) % len(self.aps)
        return self.aps[self.i], self.res[self.i], self.sems[self.i]


def r32(ap):
    return ap.bitcast(F32R)


class Builder:
    def __init__(self, kinds, first_is_input=True):
        self.kinds = kinds
        nc = self.nc = bass.Bass("TRN2", target_bir_lowering=False)
        self.stack = ExitStack()
        st = self.stack
        S = self.S = Sched(nc, st)
        dt = nc.dram_tensor
        self.x_in = dt("x", [T, D], F32, kind="ExternalInput").ap()
        self.out = dt("out", [T, D], F32, kind="ExternalOutput").ap()
        self.cf_d = dt("cf", [128, CFW], F32, kind="ExternalInput").ap()
        self.cb_d = dt("cb", [128, CBW], BF16, kind="ExternalInput").ap()
        self.mask_d = dt("masks", [2, 128, 16 * 512], BF16, kind="ExternalInput").ap()
        self.w = {}
        nm = sum(1 for k in kinds if k[0] == "mla")
        nsb = sum(1 for k in kinds if k[0] == "sb")
        if nm:
            self.w["mla_w_in"] = dt("mla_w_in", [nm, D, 5696], F32, kind="ExternalInput").ap()
            self.w["mla_w_qb"] = dt("mla_w_qb", [nm, 1024, 6144], F32, kind="ExternalInput").ap()
            self.w["mla_w_kvb"] = dt("mla_w_kvb", [nm, 512, 8192], F32, kind="ExternalInput").ap()
            self.w["mla_w_out"] = dt("mla_w_out", [nm, D, D], F32, kind="ExternalInput").ap()
        if nsb:
            self.w["sb_w_in"] = dt("sb_w_in", [nsb, D, 4 * D], F32, kind="ExternalInput").ap()
            self.w["sb_w_out"] = dt("sb_w_out", [nsb, D, D], F32, kind="ExternalInput").ap()
        self.xres = dt("xres", [T, D], F32).ap()
        self.qT = dt("qT", [6144, T], BF16).ap()
        self.gateT = dt("gateT", [D, T], BF16).ap()
        self.sendK = [dt(f"sendK{c}", [1024, T], BF16).ap() for c in range(4)]
        self.gathK = [dt(f"gathK{c}", [2048, T], BF16).ap() for c in range(4)]
        self.sendKr = [dt(f"sendKr{c}", [1024, T], BF16).ap() for c in range(2)]
        self.gathKr = [dt(f"gathKr{c}", [2048, T], BF16).ap() for c in range(2)]
        self.sendV = [dt(f"sendV{c}", [T, 1024], BF16).ap() for c in range(4)]
        self.gathV = [dt(f"gathV{c}", [2 * T, 1024], BF16).ap() for c in range(4)]
        self.r_xcol = [Res(f"xcol{i}") for i in range(16)]
        self.r_qT = [Res(f"qT{h}") for h in range(NH)]
        self.r_gate = [Res(f"gate{h}") for h in range(NH)]
        self.r_sendK = [Res(f"sendK{h}") for h in range(NH)]
        self.r_sendKr = [Res(f"sendKr{h}") for h in range(NH)]
        self.r_sendV = [Res(f"sendV{i}") for i in range(NH)]
        self.r_gathK = [Res(f"gathK{c}") for c in range(4)]
        self.r_gathKr = [Res(f"gathKr{c}") for c in range(2)]
        self.r_gathV = [Res(f"gathV{c}") for c in range(4)]
        self.cc_sems = [S.new_sem("cc") for _ in range(10)]
        sb = lambda name, shape, dtp: st.enter_context(nc.sbuf_tensor(name, shape, dtp))
        self.actT = sb("actT", [128, 32, T], BF16)
        self.r_actT = [Res(f"actT{t}") for t in range(8)]
        self.r_og = [Res(f"og{h}") for h in range(NH)]
        wts = [sb(f"wbuf{i}", [128, WELEMS], BF16) for i in range(NWBUF)]
        self.wbuf = Buf(S, "wb", [w_[:] for w_ in wts])
        self.cf = sb("cf_sb", [128, CFW], F32)
        self.cb = sb("cb_sb", [128, CBW], BF16)
        self.r_const = Res("const")
        self.cfr = sb("cfr_sb", [128, 320], F32)
        stg = [sb(f"stg{i}", [128, T], BF16) for i in range(3)]
        self.stg = Buf(S, "stg", [s_[:] for s_ in stg])
        self.small = sb("small", [128, 64], F32)
        self.r_small = Res("small")
        AR = 32256
        self.zone = sb("zone", [128, 3584], F32)
        carveZ = lambda off, n: self.zone[:, off:off + n]
        self.arena = sb("arena", [128, AR], BF16)
        A = self.arena
        self.AR = AR

        def carve(off, n, dtp=BF16):
            v = A[:, off:off + n]
            return v.bitcast(F32) if dtp == F32 else v
        self.carve = carve
        self.phase_res = {k: [] for k in ("norm", "outp", "attn", "kv", "q")}

        def reg(ph, buf):
            self.phase_res[ph].extend(buf.res if isinstance(buf, Buf) else [buf])
            return buf
        self.xrowb = reg("norm", Buf(S, "xrow", [carve(0, 8192, F32), carve(8192, 8192, F32)]))
        self.hn = reg("norm", Buf(S, "hn", [carve(16384, 4096), carve(20480, 4096)]))
        self.xold = reg("outp", Buf(S, "xold", [carve(0, 4096, F32), carve(4096, 4096, F32)]))
        self.stgtm = Buf(S, "stgtm", [carve(8192, 2048), carve(10240, 2048)])
        for ph in ("outp", "kv"):
            reg(ph, self.stgtm)
        self.esp = reg("attn", Buf(S, "esp", [carveZ(i * 512, 512) for i in range(3)]))
        self.lsum = reg("attn", Buf(S, "lsum", [carveZ(1536 + i * 512, 512) for i in range(3)]))
        o = 0
        hs = []
        for i in range(2):
            d = {}
            d["q"] = carve(o, 1024); o += 1024
            d["q2"] = carve(o, 1024); o += 1024
            d["k"] = carve(o, 2048); o += 2048
            d["k2"] = carve(o, 2048); o += 2048
            d["v"] = carve(o, 2048); o += 2048
            d["g"] = carve(o, 1024); o += 1024
            hs.append(d)
        self.hset = hs
        self.r_hset = [{k: reg("attn", Res(f"hs{i}{k}")) for k in hs[i]} for i in range(2)]
        self.hsem = [{k: S.new_sem(f"hs{k}") for k in hs[i]} for i in range(2)]
        self.maskt = carve(o, 8192); o += 8192
        self.r_mask = reg("attn", Res("mask"))
        self.mask_sem = S.new_sem("mask")
        self.ebuf = reg("attn", Buf(S, "ebuf", [carve(o + i * 1024, 1024, F32) for i in range(2)])); o += 2048
        self.wt = reg("attn", Buf(S, "wt", [carve(o + i * 512, 512) for i in range(3)])); o += 1536
        self.fin = reg("attn", Buf(S, "fin", [carve(o + i * 1024, 1024, F32) for i in range(2)])); o += 2048
        assert o <= AR, o
        self.ebp = reg("attn", Buf(S, "ebp", [carve(26624, 2048, F32), carve(30208, 2048, F32)]))
        self.spp = reg("attn", Buf(S, "spp", [carveZ(0, 1024), carveZ(1024, 1024)]))
        self.lp = reg("attn", Buf(S, "lp", [carveZ(2048 + i * 512, 512) for i in range(3)]))
        self.wp = reg("attn", Buf(S, "wp", [carve(28672, 1024), hs[0]["k2"][:, 0:1024]]))
        self.prep_f = [carveZ(0, 1024), carve(0, 2048, F32), carveZ(1024, 1024),
                       carve(2048, 2048, F32), carve(4096, 2048, F32)]
        self.r_prep = [Res(f"prep{i}") for i in range(5)]
        self.sqr = carveZ(2048, 1024)
        self.r_sqr = Res("sqr")
        for ph in ("kv", "q"):
            for r in self.r_prep + [self.r_sqr]:
                reg(ph, r)
        self.lat = carve(12288, 10240, F32).rearrange("p (c t) -> p c t", t=T)
        self.r_lat = reg("kv", Res("lat"))
        self.kvn = carve(22528, 4096).rearrange("p (c t) -> p c t", t=T)
        self.r_kvn = reg("kv", Res("kvn"))
        self.qlat = carve(6144, 16384, F32).rearrange("p (c t) -> p c t", t=T)
        self.r_qlat = reg("q", Res("qlat"))
        self.qn = carve(22528, 8192).rearrange("p (c t) -> p c t", t=T)
        self.r_qn = reg("q", Res("qn"))
        assert 22528 + 8192 <= AR
        self.prep_alt = {"kv": [carve(12288, 2048, F32), carve(14336, 2048, F32), carve(16384, 2048, F32)],
                         "q": [carve(6144, 2048, F32), carve(8192, 2048, F32), carve(10240, 2048, F32)]}
        self.r_prep_alt = [Res(f"prepalt{i}") for i in range(3)]
        for ph in ("kv", "q"):
            for r in self.r_prep_alt:
                reg(ph, r)
        self.r_gstg = Res("gstg")
        self.gstg_sem = S.new_sem("gstg")
        for ph in ("kv", "q"):
            reg(ph, self.r_gstg)
        self.cur_phase = None
        self.pbig = [st.enter_context(nc.psum_tensor(f"pb{i}", [128, 1024], F32)) for i in range(4)]
        self.pb = [self.pbig[i // 2][:, (i % 2) * 512:(i % 2 + 1) * 512] for i in range(8)]
        self.r_pb = [Res(f"pb{i}") for i in range(8)]
        self.bank_i = -1
        self.held = set()
        self.plan = []
        self.plan_i = 0
        self.plan_loaded = 0
        self.wbusy = set()
        self.wlast = -1

    def c(self, name, rows=128):
        o, w = CF[name]
        return self.cf[0:rows, o:o + w]

    def cr(self, name, rows=128):
        o = {"tri": 0, "ones": 128, "rT": 256}[name]
        w = CF[name][1]
        return self.cfr[0:rows, o:o + w].bitcast(F32R)

    def cbv(self, name, rows=128):
        o, w = CB[name]
        return self.cb[0:rows, o:o + w]

    def next_bank(self):
        for _ in range(8):
            self.bank_i = (self.bank_i + 1) % 8
            if self.bank_i not in self.held:
                return self.bank_i
        raise RuntimeError("no free PSUM bank")

    def wnext(self):
        self._wprefetch()
        ent = self.plan[self.plan_i]
        assert "view" in ent, ("weight piece not loadable: no free buffer", self.plan_i, ent["lane"], self.wbusy)
        self.plan_i += 1
        return ent["view"], ent["res"], ent["KC"], ent["ncols"]

    def wfree(self, res):
        self.wbusy.discard(self.wv_res.index(res))
        self._wprefetch()

    def _wsetup(self):
        wb = self.wbuf
        h0 = wb.aps[0][:, 0:WELEMS // 2]
        h1 = wb.aps[0][:, WELEMS // 2:WELEMS]
        self.wv_ap = list(wb.aps) + [h0, h1]
        self.wv_res = list(wb.res) + [Res("wb0a"), Res("wb0b")]
        self.wv_sem = list(wb.sems) + [self.S.new_sem("wb0a"), self.S.new_sem("wb0b")]
        self.wv_alias = {0: [3, 4], 3: [0], 4: [0], 1: [], 2: []}

    def _wprefetch(self):
        S = self.S
        if not hasattr(self, "wv_ap"):
            self._wsetup()
        while self.plan_loaded < len(self.plan) and self.plan_loaded < self.plan_i + NWBUF:
            ent = self.plan[self.plan_loaded]
            cand = {"g": [1, 2], "s": [3, 4], "a": [1, 2, 0]}[ent["lane"]]
            free = [c for c in cand if c not in self.wbusy and not any(a_ in self.wbusy for a_ in self.wv_alias[c])]
            if not free:
                return
            if ent["lane"] == "a":
                free.sort(key=lambda c: (c - self.wlast - 1) % 3)
            bi = free[0]
            if bi < 3:
                self.wlast = bi
            self.wbusy.add(bi)
            ap, res, sem = self.wv_ap[bi], self.wv_res[bi], self.wv_sem[bi]
            KC, ncols = ent["KC"], ent["ncols"]
            assert KC * ncols <= (WELEMS if bi < 3 else WELEMS // 2)
            view = ap[:, 0:KC * ncols].rearrange("p (k n) -> p k n", n=ncols)
            src = ent["w"].rearrange("(k p) n -> p k n", p=128)
            S.op("pool", lambda e, view=view, src=src: e.dma_start(out=view, in_=src),
                 writes=[res] + [self.wv_res[a_] for a_ in self.wv_alias[bi]], dsem=sem)
            ent["view"], ent["res"] = view, res
            self.plan_loaded += 1

    def build_plan(self):
        jm = jsb = 0
        for kind, _ in self.kinds:
            if kind == "mla":
                w_in = self.w["mla_w_in"][jm]
                for c0, n in [(1024, 256), (1280, 256), (1536, 64)]:
                    self.plan.append(dict(w=w_in[:, c0:c0 + n], KC=32, ncols=n, lane="a"))
                gi = 0
                for h in range(NH):
                    self.plan.append(dict(w=self.w["mla_w_kvb"][jm][:, h * 256:(h + 1) * 256], KC=4, ncols=256, lane="s"))
                    if h % 4 == 0:
                        self.plan.append(dict(w=w_in[:, 1600 + gi * 256:1600 + (gi + 1) * 256], KC=32, ncols=256, lane="g"))
                        gi += 1
                for c0 in range(0, 1024, 256):
                    self.plan.append(dict(w=w_in[:, c0:c0 + 256], KC=32, ncols=256, lane="a"))
                for h in range(NH):
                    self.plan.append(dict(w=self.w["mla_w_qb"][jm][:, h * 192:(h + 1) * 192], KC=8, ncols=192, lane="s"))
                    if h % 4 == 0:
                        self.plan.append(dict(w=w_in[:, 1600 + gi * 256:1600 + (gi + 1) * 256], KC=32, ncols=256, lane="g"))
                        gi += 1
                for c0 in range(0, D, 256):
                    self.plan.append(dict(w=self.w["mla_w_out"][jm][:, c0:c0 + 256], KC=32, ncols=256, lane="a"))
                jm += 1
            else:
                w_in = self.w["sb_w_in"][jsb]
                for base in (D, 2 * D, 0, 3 * D):
                    for c0 in range(base, base + D, 256):
                        self.plan.append(dict(w=w_in[:, c0:c0 + 256], KC=32, ncols=256, lane="a"))
                for c0 in range(0, D, 256):
                    self.plan.append(dict(w=self.w["sb_w_out"][jsb][:, c0:c0 + 256], KC=32, ncols=256, lane="a"))
                jsb += 1

    def load_consts(self):
        S = self.S
        sem = S.new_sem("const")
        S.op("sp", lambda e: e.dma_start(out=self.cf[:], in_=self.cf_d[:, :]), writes=[self.r_const], dsem=sem)
        S.op("sp", lambda e: e.dma_start(out=self.cb[:], in_=self.cb_d[:, :]), writes=[self.r_const], dsem=sem)
        o0 = CF["tri"][0]
        S.op("dve", lambda e: e.tensor_copy(out=self.cfr[:, :].bitcast(F32R), in_=self.cf[:, o0:o0 + 320]),
             reads=[self.r_const], writes=[self.r_const])

    def norm_phase(self, li, x_src, x_res, mid_hook=None):
        S = self.S
        g = self.c(f"g{li}")
        ident = self.cbv("ident")
        for t in range(8):
            xrow, r_xrow, xsem = self.xrowb.next()
            S.op("sp", lambda e, t=t, xrow=xrow: e.dma_start(out=xrow, in_=x_src[t * 128:(t + 1) * 128, :]),
                 reads=x_res, writes=[r_xrow], dsem=xsem)
            hn, r_hn, _ = self.hn.next()
            ssq = self.small[:, t:t + 1]
            rstd = self.small[:, 8 + t:9 + t]
            rinv = self.small[:, 16 + t:17 + t]
            S.op("act", lambda e, hn=hn, ssq=ssq, xrow=xrow: e.activation(out=hn, in_=xrow, func=AF.Square, accum_out=ssq),
                 reads=[r_xrow], writes=[r_hn, self.r_small])
            S.op("act", lambda e, ssq=ssq, rstd=rstd: e.activation(out=rstd, in_=ssq, func=AF.Sqrt, scale=1.0 / D, bias=EPS),
                 reads=[self.r_small, self.r_const], writes=[self.r_small])
            S.op("dve", lambda e, rstd=rstd, rinv=rinv: e.reciprocal(out=rinv, in_=rstd),
                 reads=[self.r_small], writes=[self.r_small])
            S.op("dve", lambda e, hn=hn, rinv=rinv, xrow=xrow: e.tensor_scalar(out=hn, in0=xrow, scalar1=rinv, scalar2=None, op0=ALU.mult),
                 reads=[r_xrow, self.r_small], writes=[r_hn])
            for q in range(4):
                b = self.next_bank()
                pv = self.pb[b][:].bitcast(BF16)

                def ftr(e, pv=pv, hn=hn, q=q):
                    for j in range(8):
                        cix = 8 * q + j
                        ins = e.transpose(out=pv[:, j * 128:(j + 1) * 128], in_=hn[:, cix * 128:(cix + 1) * 128], identity=ident)
                    return ins
                S.op("pe", ftr, reads=[r_hn, self.r_const], writes=[self.r_pb[b]])

                def fev(e, pv=pv, q=q, t=t):
                    for j in range(8):
                        cix = 8 * q + j
                        ins = e.tensor_scalar(out=self.actT[:, cix, t * 128:(t + 1) * 128], in0=pv[:, j * 128:(j + 1) * 128],
                                              scalar1=g[:, cix:cix + 1], scalar2=None, op0=ALU.mult)
                    return ins
                S.op("dve", fev, reads=[self.r_pb[b], self.r_const], writes=[self.r_actT[t]])
            if t == 3 and mid_hook is not None:
                mid_hook()

    def _fm_group(self, wt, wres, KC, mc, M, slot, act_fn, ares):
        b = self.next_bank()

        def f(e, b=b, wt=wt, KC=KC, mc=mc, M=M, slot=slot):
            for kc in range(KC):
                ins = e.matmul(self.pb[b][0:M, :], lhsT=wt[:, kc, mc * 128:mc * 128 + M], rhs=act_fn(kc, slot),
                               start=(kc == 0), stop=(kc == KC - 1))
            return ins
        self.S.op("pe", f, reads=[wres] + ares, writes=[self.r_pb[b]])
        return b

    def gemm_fm_head(self, n_pre, act_fn):
        pre = []
        for _ in range(n_pre):
            wt, wres, KC, ncols = self.wnext()
            nm = (ncols + 127) // 128
            banks = [[None, None] for _ in range(nm)]
            for mc in range(nm):
                b = self._fm_group(wt, wres, KC, mc, min(128, ncols - mc * 128), 0, act_fn, self.r_actT[0:4])
                self.held.add(b)
                banks[mc][0] = b
            pre.append((wt, wres, KC, ncols, banks))
        return pre

    def gemm_fm(self, npieces, act_fn, act_res, epi, nslots=2, pre=()):
        for pi in range(npieces):
            if pi < len(pre):
                wt, wres, KC, ncols, banks = pre[pi]
                slots = [1]
            else:
                wt, wres, KC, ncols = self.wnext()
                banks = [[None] * nslots for _ in range((ncols + 127) // 128)]
                slots = list(range(nslots))
            nm = (ncols + 127) // 128
            for slot in slots:
                ares = act_res[4 * slot:4 * slot + 4] if (len(act_res) == 8 and nslots == 2) else act_res
                for mc in range(nm):
                    banks[mc][slot] = self._fm_group(wt, wres, KC, mc, min(128, ncols - mc * 128), slot, act_fn, ares)
            for mc in range(nm):
                epi(pi, mc, min(128, ncols - mc * 128), banks[mc])
            if pi < len(pre):
                for mc in range(nm):
                    self.held.discard(banks[mc][0])
            self.wfree(wres)

    def gemm_tm(self, npieces, lhs_fn, lhs_res, epi, col0=0, ncols_use=None, ntt=8):
        S = self.S
        for pi in range(npieces):
            wt, wres, KC, ncols = self.wnext()
            self.gemm_tm_piece(pi, wt, wres, KC, col0, ncols_use or ncols, lhs_fn, lhs_res, epi, ntt)
            self.wfree(wres)

    def gemm_tm_piece(self, pi, wt, wres, KC, col0, n, lhs_fn, lhs_res, epi, ntt=8):
        S = self.S
        per = 512 // n if n <= 256 else 1
        per = min(per, 2) if n == 256 else per
        banks = []
        t = 0
        while t < ntt:
            b = self.next_bank()
            tl = list(range(t, min(ntt, t + per)))

            def f(e, b=b, tl=tl, wt=wt, KC=KC):
                for i, tt in enumerate(tl):
                    for kc in range(KC):
                        ins = e.matmul(self.pb[b][:, i * n:(i + 1) * n], lhsT=lhs_fn(kc, tt), rhs=wt[:, kc, col0:col0 + n],
                                       start=(kc == 0), stop=(kc == KC - 1))
                return ins
            S.op("pe", f, reads=[wres] + lhs_res, writes=[self.r_pb[b]])
            banks.append((b, tl))
            t += per
        epi(pi, banks)

    def epi_fm_dram(self, dst, dst_res_fn, row_fn, func=None, scale=1.0):
        S = self.S

        def epi(pi, mc, M, banks):
            stg, r_stg, sem = self.stg.next()
            for slot, b in enumerate(banks):
                if func is None:
                    S.op("dve", lambda e, b=b, slot=slot, stg=stg, M=M: e.tensor_scalar(
                        out=stg[0:M, slot * 512:(slot + 1) * 512], in0=self.pb[b][0:M, :], scalar1=float(scale), scalar2=None, op0=ALU.mult),
                        reads=[self.r_pb[b]], writes=[r_stg])
                else:
                    S.op("act", lambda e, b=b, slot=slot, stg=stg, M=M: e.activation(
                        out=stg[0:M, slot * 512:(slot + 1) * 512], in_=self.pb[b][0:M, :], func=func),
                        reads=[self.r_pb[b]], writes=[r_stg])
            r0 = row_fn(pi, mc)
            dd = dst(pi, mc) if callable(dst) else dst
            S.op("sp", lambda e, stg=stg, r0=r0, M=M, dd=dd: e.dma_start(out=dd[r0:r0 + M, :], in_=stg[0:M, :]),
                 reads=[r_stg], writes=[dst_res_fn(pi, mc)], dsem=sem)
        return epi

    def epi_fm_sbuf(self, dst_fn, dst_res):
        S = self.S

        def epi(pi, mc, M, banks):
            for slot, b in enumerate(banks):
                S.op("dve", lambda e, b=b, slot=slot, M=M, pi=pi, mc=mc: e.tensor_copy(
                    out=dst_fn(pi, mc, M, slot), in_=self.pb[b][0:M, :]),
                    reads=[self.r_pb[b]], writes=[dst_res])
        return epi

    def epi_tm_dram(self, dst, dst_res_fn, col_fn, n=256):
        S = self.S

        def epi(pi, banks):
            stg, r_stg, sem = self.stgtm.next()
            sv = stg.rearrange("p (t n) -> p t n", n=256)
            for b, tl in banks:
                k = len(tl)
                S.op("dve", lambda e, b=b, tl=tl, k=k, sv=sv: e.tensor_copy(
                    out=sv[:, tl[0]:tl[0] + k, 0:n], in_=self.pb[b][:, 0:k * n].rearrange("p (t n) -> p t n", n=n)),
                    reads=[self.r_pb[b]], writes=[r_stg])
            c0 = col_fn(pi)
            dd = dst(pi) if callable(dst) else dst
            dv = dd[:, c0:c0 + n].rearrange("(t p) n -> p t n", p=128)
            S.op("sp", lambda e, sv=sv, dv=dv: e.dma_start(out=dv, in_=sv[:, :, 0:n]),
                 reads=[r_stg], writes=[dst_res_fn(pi)], dsem=sem)
        return epi

    def out_proj(self, x_src, x_src_res_fn, x_dst, final):
        S = self.S
        pend = {}

        def prefetch(pi):
            xo, r_xo, sem = self.xold.next()
            xv = xo.rearrange("p (t n) -> p t n", n=256)
            sv = x_src[:, pi * 256:(pi + 1) * 256].rearrange("(t p) n -> p t n", p=128)
            S.op("sp", lambda e, xv=xv, sv=sv: e.dma_start(out=xv, in_=sv),
                 reads=x_src_res_fn(pi), writes=[r_xo], dsem=sem)
            pend[pi] = (xv, r_xo, sem)

        def epi(pi, banks):
            xv, r_xo, sem = pend.pop(pi)
            for b, tl in banks:
                k = len(tl)
                S.op("dve", lambda e, b=b, tl=tl, k=k, xv=xv: e.tensor_tensor(
                    out=xv[:, tl[0]:tl[0] + k, :], in0=self.pb[b][:, 0:k * 256].rearrange("p (t n) -> p t n", n=256),
                    in1=xv[:, tl[0]:tl[0] + k, :], op=ALU.add),
                    reads=[self.r_pb[b], r_xo], writes=[r_xo])
            dv = x_dst[:, pi * 256:(pi + 1) * 256].rearrange("(t p) n -> p t n", p=128)
            tok = S.op("sp", lambda e, xv=xv, dv=dv: e.dma_start(out=dv, in_=xv),
                       reads=[r_xo], writes=[self.r_xcol[pi]], dsem=sem)
            if final:
                S.final.append(tok)
            if pi + 2 < 16:
                prefetch(pi + 2)
        prefetch(0)
        prefetch(1)
        self.gemm_tm(16, lambda kc, tt: self.actT[:, kc, tt * 128:(tt + 1) * 128], self.r_og, epi)

    def allgather(self, send, send_res, gath, gath_res, sem):
        S = self.S
        S.op("pool", lambda e: e.collective_compute("AllGather", ALU.bypass, replica_groups=RG,
                                                    ins=[send.opt()], outs=[gath.opt()]),
             reads=send_res, writes=[gath_res], dsem=sem, inc=1)

    def exchange(self, mla, do_k=True, do_v=True):
        if do_k:
            for c in range(4):
                self.allgather(self.sendK[c], self.r_sendK[8 * c:8 * c + 8], self.gathK[c], self.r_gathK[c], self.cc_sems[c])
        if not do_v:
            return
        if mla:
            for c in range(2):
                self.allgather(self.sendKr[c], self.r_sendKr[16 * c:16 * c + 16], self.gathKr[c], self.r_gathKr[c], self.cc_sems[4 + c])
        for c in range(4):
            rs = self.r_sendV[8 * c:8 * c + 8] if mla else self.r_sendV[4 * c:4 * c + 4]
            self.allgather(self.sendV[c], rs, self.gathV[c], self.r_gathV[c], self.cc_sems[6 + c])

    def load_masks(self, which):
        S = self.S
        S.op("sp", lambda e: e.dma_start(out=self.maskt, in_=self.mask_d[which]),
             writes=[self.r_mask], dsem=self.mask_sem)

    def attention(self, kind):
        S = self.S
        sb_mode = kind == "sb"
        krows = 128 if sb_mode else 192
        ident = self.cbv("ident")
        nident = self.cbv("nident")
        onesb = self.cbv("onesb")
        tri = self.cr("tri")
        ones = self.cr("ones")
        items = []
        for h in range(NH):
            for s in range(2):
                nkb = 2 if s == 0 else 4
                tl = []
                for j in reversed(range(nkb)):
                    for i in reversed(range(4)):
                        gt = 4 * GPOS[j] + i
                        if s == 0:
                            mi = j * 4 + i
                        elif j >= 2:
                            mi = 8 + (j - 2) * 4 + i
                        else:
                            mi = None
                        tl.append((gt, mi))
                for n_, (gt, mi) in enumerate(tl):
                    items.append(dict(h=h, s=s, gt=gt, mi=mi, first=(n_ == 0), last=(n_ == len(tl) - 1)))
        n = len(items)

        def load_head(h):
            hs = self.hset[h % 2]
            rs = self.r_hset[h % 2]
            sm = self.hsem[h % 2]
            kv = hs["k"].rearrange("p (r t) -> p r t", r=2)
            ksrc = self.gathK[h // 8].rearrange("(r f) t -> f r t", r=2)[(h % 8) * 128:(h % 8 + 1) * 128]
            S.op("sp", lambda e: e.dma_start(out=kv, in_=ksrc), reads=[self.r_gathK[h // 8]], writes=[rs["k"]], dsem=sm["k"])
            if sb_mode:
                S.op("sp", lambda e: e.dma_start(out=hs["q"], in_=self.qT[h * 128:(h + 1) * 128, :]),
                     reads=[self.r_qT[h]], writes=[rs["q"]], dsem=sm["q"])
                S.op("dve", lambda e: e.tensor_scalar(out=hs["q2"], in0=hs["q"], scalar1=-1.0, scalar2=None, op0=ALU.mult),
                     reads=[rs["q"]], writes=[rs["q2"]])
            else:
                S.op("sp", lambda e: e.dma_start(out=hs["q"], in_=self.qT[h * 192:h * 192 + 128, :]),
                     reads=[self.r_qT[h]], writes=[rs["q"]], dsem=sm["q"])
                S.op("sp", lambda e: e.dma_start(out=hs["q2"][0:64, :], in_=self.qT[h * 192 + 128:h * 192 + 192, :]),
                     reads=[self.r_qT[h]], writes=[rs["q2"]], dsem=sm["q2"])
                kv2 = hs["k2"].rearrange("p (r t) -> p r t", r=2)
                k2src = self.gathKr[h // 16].rearrange("(r f) t -> f r t", r=2)[(h % 16) * 64:(h % 16 + 1) * 64]
                S.op("sp", lambda e: e.dma_start(out=kv2[0:64], in_=k2src),
                     reads=[self.r_gathKr[h // 16]], writes=[rs["k2"]], dsem=sm["k2"])
            vv = hs["v"].rearrange("p (t d) -> p t d", d=128)
            vsrc = self.gathV[h // 8][:, (h % 8) * 128:(h % 8 + 1) * 128].rearrange("(t p) d -> p t d", p=128)
            S.op("sp", lambda e: e.dma_start(out=vv, in_=vsrc), reads=[self.r_gathV[h // 8]], writes=[rs["v"]], dsem=sm["v"])
            S.op("sp", lambda e: e.dma_start(out=hs["g"], in_=self.gateT[h * 128:(h + 1) * 128, :]),
                 reads=[self.r_gate[h]], writes=[rs["g"]], dsem=sm["g"])

        ZB = [0, 1]
        CBK = [2, 3]
        OT = [4, 5]
        DEN = [6, 7]
        st = {}

        def stageA(ix):
            it = items[ix]
            h, s, gt, mi = it["h"], it["s"], it["gt"], it["mi"]
            hs, rs = self.hset[h % 2], self.r_hset[h % 2]
            zb = ZB[ix % 2]
            qs = hs["q"][:, s * 512:(s + 1) * 512]
            kt = hs["k"][:, gt * 128:(gt + 1) * 128]
            mk = None if mi is None else self.maskt[:, mi * 512:(mi + 1) * 512]

            def f1(e):
                if sb_mode:
                    ins = e.matmul(self.pb[zb][:, :], lhsT=kt, rhs=qs, start=True, stop=(mk is None))
                    if mk is not None:
                        ins = e.matmul(self.pb[zb][:, :], lhsT=ident, rhs=mk, start=False, stop=True)
                else:
                    e.matmul(self.pb[zb][:, :], lhsT=kt, rhs=qs, start=True, stop=False)
                    ins = e.matmul(self.pb[zb][:, :], lhsT=hs["k2"][:, gt * 128:(gt + 1) * 128],
                                   rhs=hs["q2"][:, s * 512:(s + 1) * 512], start=False, stop=True)
                return ins
            rd = [rs["q"], rs["k"], self.r_const] + ([] if sb_mode else [rs["q2"], rs["k2"]]) + ([self.r_mask] if (mk is not None and sb_mode) else [])
            S.op("pe", f1, reads=rd, writes=[self.r_pb[zb]])
            if sb_mode:
                esp, r_esp, _ = self.esp.next()
                eb, r_eb, _ = self.ebuf.next()
                S.op("act", lambda e: e.activation(out=eb, in_=self.pb[zb][:, :], func=AF.Exp),
                     reads=[self.r_pb[zb]], writes=[r_eb])
                S.op("act", lambda e: e.activation(out=r32(esp), in_=eb, func=AF.Ln, bias=1.0),
                     reads=[r_eb], writes=[r_esp])
                prev = st.get("lsum")
                if it["first"]:
                    ls, r_ls, _ = self.lsum.next()
                    S.op("dve", lambda e: e.tensor_copy(out=r32(ls), in_=esp), reads=[r_esp], writes=[r_ls])
                else:
                    pls, r_pls = prev
                    ls, r_ls, _ = self.lsum.next()
                    S.op("dve", lambda e: e.tensor_tensor(out=r32(ls), in0=pls, in1=esp, op=ALU.add),
                         reads=[r_esp, r_pls], writes=[r_ls])
                st["lsum"] = (ls, r_ls)
                it["esp"] = (esp, r_esp)
                it["lprev"] = prev if not it["first"] else None
            else:
                w, r_w, _ = self.wt.next()
                S.op("act", lambda e: e.activation(out=w, in_=self.pb[zb][:, :], func=AF.Exp),
                     reads=[self.r_pb[zb]], writes=[r_w])
                if mk is not None:
                    S.op("dve", lambda e: e.tensor_tensor(out=w, in0=w, in1=mk, op=ALU.mult),
                         reads=[r_w, self.r_mask], writes=[r_w])
                it["w"] = (w, r_w)

        def stageB(ix):
            if not sb_mode:
                return
            it = items[ix]
            h, s, gt, mi = it["h"], it["s"], it["gt"], it["mi"]
            hs, rs = self.hset[h % 2], self.r_hset[h % 2]
            cbk = CBK[ix % 2]
            esp, r_esp = it["esp"]
            kt = hs["k"][:, gt * 128:(gt + 1) * 128]
            nq = hs["q2"][:, s * 512:(s + 1) * 512]
            mk = None if mi is None else self.maskt[:, mi * 512:(mi + 1) * 512]
            lp = it["lprev"]

            def f2(e):
                e.matmul(self.pb[cbk][:, :], lhsT=tri, rhs=r32(esp), start=True, stop=False)
                if lp is not None:
                    e.matmul(self.pb[cbk][:, :], lhsT=ones, rhs=r32(lp[0]), start=False, stop=False)
                ins = e.matmul(self.pb[cbk][:, :], lhsT=kt, rhs=nq, start=False, stop=(mk is None))
                if mk is not None:
                    ins = e.matmul(self.pb[cbk][:, :], lhsT=nident, rhs=mk, start=False, stop=True)
                return ins
            rd = [r_esp, rs["k"], rs["q2"], self.r_const] + ([lp[1]] if lp is not None else []) + ([self.r_mask] if mk is not None else [])
            S.op("pe", f2, reads=rd, writes=[self.r_pb[cbk]])
            w, r_w, _ = self.wt.next()
            S.op("act", lambda e: e.activation(out=w, in_=self.pb[cbk][:, :], func=AF.Exp, scale=-1.0),
                 reads=[self.r_pb[cbk]], writes=[r_w])
            it["w"] = (w, r_w)

        def stageC(ix):
            it = items[ix]
            h, s, gt = it["h"], it["s"], it["gt"]
            hs, rs = self.hset[h % 2], self.r_hset[h % 2]
            par = (2 * h + s) % 2
            ot = OT[par]
            w, r_w = it["w"]
            vt = hs["v"][:, gt * 128:(gt + 1) * 128]

            def f3(e):
                ins = e.matmul(self.pb[ot][:, :], lhsT=vt, rhs=w, start=it["first"], stop=it["last"])
                if not sb_mode:
                    ins = e.matmul(self.pb[DEN[par]][:, :], lhsT=onesb, rhs=w, start=it["first"], stop=it["last"])
                return ins
            wr = [self.r_pb[ot]] + ([] if sb_mode else [self.r_pb[DEN[par]]])
            S.op("pe", f3, reads=[r_w, rs["v"], self.r_const], writes=wr)
            if it["last"]:
                dst = self.actT[:, h, s * 512:(s + 1) * 512]
                gs = hs["g"][:, s * 512:(s + 1) * 512]
                if sb_mode:
                    S.op("dve", lambda e: e.tensor_tensor(out=dst, in0=self.pb[ot][:, :], in1=gs, op=ALU.mult),
                         reads=[self.r_pb[ot], rs["g"]], writes=[self.r_og[h]])
                else:
                    fb, r_fb, _ = self.fin.next()
                    S.op("act", lambda e: e.activation(out=fb, in_=self.pb[DEN[par]][:, :], func=AF.Ln),
                         reads=[self.r_pb[DEN[par]]], writes=[r_fb])
                    S.op("act", lambda e: e.activation(out=fb, in_=fb, func=AF.Exp, scale=-1.0),
                         reads=[r_fb], writes=[r_fb])
                    S.op("dve", lambda e: e.tensor_tensor(out=fb, in0=self.pb[ot][:, :], in1=fb, op=ALU.mult),
                         reads=[self.r_pb[ot], r_fb], writes=[r_fb])
                    S.op("dve", lambda e: e.tensor_tensor(out=dst, in0=fb, in1=gs, op=ALU.mult),
                         reads=[r_fb, rs["g"]], writes=[self.r_og[h]])
            if it["last"] and s == 1 and h + 2 < NH:
                load_head(h + 2)

        if not sb_mode:
            for i in range(2):
                for kk in ("q2", "k2"):
                    S.op("pool", lambda e, i=i, kk=kk: e.memset(self.hset[i][kk][64:128, :], 0.0),
                         writes=[self.r_hset[i][kk]])
        if sb_mode:
            pairs = [(items[2 * i], items[2 * i + 1]) for i in range(n // 2)]
            npairs = len(pairs)
            ZP = [self.pbig[0], self.pbig[3]]
            CP = self.pbig[1]
            r_ZP = [[self.r_pb[0], self.r_pb[1]], [self.r_pb[6], self.r_pb[7]]]
            r_CP = [self.r_pb[2], self.r_pb[3]]
            half = lambda ap, x: ap[:, x * 512:(x + 1) * 512]
            pst = {"L": None}

            def mask_of(it):
                return None if it["mi"] is None else self.maskt[:, it["mi"] * 512:(it["mi"] + 1) * 512]

            def pA(pi):
                pr = pairs[pi]
                h, s = pr[0]["h"], pr[0]["s"]
                hs, rs = self.hset[h % 2], self.r_hset[h % 2]
                zp, rz = ZP[pi % 2], r_ZP[pi % 2]
                qs = hs["q"][:, s * 512:(s + 1) * 512]

                def f1(e):
                    for x in range(2):
                        gt = pr[x]["gt"]
                        ins = e.matmul(half(zp, x), lhsT=hs["k"][:, gt * 128:(gt + 1) * 128], rhs=qs, start=True, stop=True)
                    return ins
                S.op("pe", f1, reads=[rs["q"], rs["k"]], writes=rz)
                eb, r_eb, _ = self.ebp.next()
                S.op("act", lambda e: e.activation(out=eb, in_=zp[:, :], func=AF.Exp), reads=rz, writes=[r_eb])
                for x in range(2):
                    mk = mask_of(pr[x])
                    if mk is not None:
                        S.op("dve", lambda e, x=x, mk=mk: e.tensor_tensor(out=half(eb, x), in0=half(eb, x), in1=mk, op=ALU.mult),
                             reads=[r_eb, self.r_mask], writes=[r_eb])
                pr[0]["eb"] = (eb, r_eb)

            def pLn(pi):
                pr = pairs[pi]
                eb, r_eb = pr[0]["eb"]
                sp, r_sp, _ = self.spp.next()
                S.op("act", lambda e: e.activation(out=r32(sp), in_=eb, func=AF.Ln, bias=1.0), reads=[r_eb], writes=[r_sp])
                lprev = None if pr[0]["first"] else pst["L"]
                ln_, r_ln, _ = self.lp.next()
                if lprev is None:
                    S.op("dve", lambda e: e.tensor_tensor(out=r32(ln_), in0=half(sp, 0), in1=half(sp, 1), op=ALU.add),
                         reads=[r_sp], writes=[r_ln])
                else:
                    S.op("dve", lambda e: e.tensor_tensor(out=r32(ln_), in0=lprev[0], in1=half(sp, 0), op=ALU.add),
                         reads=[r_sp, lprev[1]], writes=[r_ln])
                    S.op("dve", lambda e: e.tensor_tensor(out=r32(ln_), in0=ln_, in1=half(sp, 1), op=ALU.add),
                         reads=[r_sp, r_ln], writes=[r_ln])
                pst["L"] = (ln_, r_ln)
                pr[0]["sp"] = (sp, r_sp)
                pr[0]["lprev"] = lprev

            def pB(pi):
                pr = pairs[pi]
                h, s = pr[0]["h"], pr[0]["s"]
                hs, rs = self.hset[h % 2], self.r_hset[h % 2]
                sp, r_sp = pr[0]["sp"]
                lprev = pr[0]["lprev"]
                nq = hs["q2"][:, s * 512:(s + 1) * 512]

                def f2(e):
                    for x in range(2):
                        gt = pr[x]["gt"]
                        e.matmul(half(CP, x), lhsT=tri, rhs=r32(half(sp, x)), start=True, stop=False)
                        if x == 1:
                            e.matmul(half(CP, x), lhsT=ones, rhs=r32(half(sp, 0)), start=False, stop=False)
                        if lprev is not None:
                            e.matmul(half(CP, x), lhsT=ones, rhs=r32(lprev[0]), start=False, stop=False)
                        ins = e.matmul(half(CP, x), lhsT=hs["k"][:, gt * 128:(gt + 1) * 128], rhs=nq, start=False, stop=True)
                    return ins
                rd = [r_sp, rs["k"], rs["q2"], self.r_const] + ([lprev[1]] if lprev is not None else [])
                S.op("pe", f2, reads=rd, writes=r_CP)
                w, r_w, _ = self.wp.next()
                S.op("act", lambda e: e.activation(out=w, in_=CP[:, :], func=AF.Exp, scale=-1.0), reads=r_CP, writes=[r_w])
                for x in range(2):
                    mk = mask_of(pr[x])
                    if mk is not None:
                        S.op("dve", lambda e, x=x, mk=mk: e.tensor_tensor(out=half(w, x), in0=half(w, x), in1=mk, op=ALU.mult),
                             reads=[r_w, self.r_mask], writes=[r_w])
                pr[0]["w"] = (w, r_w)

            def pC(pi):
                pr = pairs[pi]
                h, s = pr[0]["h"], pr[0]["s"]
                hs, rs = self.hset[h % 2], self.r_hset[h % 2]
                ot = OT[(2 * h + s) % 2]
                w, r_w = pr[0]["w"]

                def f3(e):
                    for x in range(2):
                        gt = pr[x]["gt"]
                        ins = e.matmul(self.pb[ot][:, :], lhsT=hs["v"][:, gt * 128:(gt + 1) * 128], rhs=half(w, x),
                                       start=(x == 0 and pr[0]["first"]), stop=(x == 1 and pr[1]["last"]))
                    return ins
                S.op("pe", f3, reads=[r_w, rs["v"]], writes=[self.r_pb[ot]])
                if pr[1]["last"]:
                    dst = self.actT[:, h, s * 512:(s + 1) * 512]
                    gs = hs["g"][:, s * 512:(s + 1) * 512]
                    S.op("dve", lambda e: e.tensor_tensor(out=dst, in0=self.pb[ot][:, :], in1=gs, op=ALU.mult),
                         reads=[self.r_pb[ot], rs["g"]], writes=[self.r_og[h]])
                    if s == 1 and h + 2 < NH:
                        load_head(h + 2)

            load_head(0)
            load_head(1)
            for step in range(npairs + 3):
                if step < npairs:
                    pA(step)
                if 1 <= step <= npairs:
                    pLn(step - 1)
                if 2 <= step <= npairs + 1:
                    pB(step - 2)
                if 3 <= step <= npairs + 2:
                    pC(step - 3)
            return
        load_head(0)
        load_head(1)
        for step in range(n + 2):
            if step < n:
                stageA(step)
            if 1 <= step <= n:
                stageB(step - 1)
            if 2 <= step <= n + 1:
                stageC(step - 2)

    def build(self):
        self.build_plan()
        self.load_consts()
        nl = len(self.kinds)
        for li, (kind, j) in enumerate(self.kinds):
            x_src = self.x_in if li == 0 else self.xres
            x_res = [] if li == 0 else self.r_xcol
            final = li == nl - 1
            x_dst = self.out if final else self.xres
            if kind == "sb":
                self.sb_layer(li, j, x_src, x_res, x_dst, final)
            else:
                self.mla_layer(li, j, x_src, x_res, x_dst, final)
        self.S.replay()
        return self.nc


def _phase(self, new):
    old = self.cur_phase
    self.cur_phase = new
    if old is None or old == new:
        return
    toks = {}
    for r in self.phase_res[old]:
        cand = list(r.readers.values()) + ([r.last_w] if r.last_w is not None else [])
        for t in cand:
            k = id(t.sem)
            if k not in toks or toks[k].val < t.val:
                toks[k] = t
    for r in self.phase_res[new]:
        for k, t in toks.items():
            if k not in r.readers or r.readers[k].val < t.val:
                r.readers[k] = t


def _alias(self, old_list, new_list):
    toks = {}
    for r in old_list:
        cand = list(r.readers.values()) + ([r.last_w] if r.last_w is not None else [])
        for t in cand:
            k = id(t.sem)
            if k not in toks or toks[k].val < t.val:
                toks[k] = t
    for r in new_list:
        for k, t in toks.items():
            if k not in r.readers or r.readers[k].val < t.val:
                r.readers[k] = t


def _latent_norm(self, src_fn, nch, gname, dst, dst_res, src_res, inv_n):
    S = self.S
    ones = self.cr("ones")
    g = self.c(gname)
    P = self.prep_f
    RP = self.r_prep
    bs = [self.next_bank(), self.next_bank()]
    for c in range(nch):
        sq, r_sq = (P[0], RP[0]) if c % 2 == 0 else (P[2], RP[2])
        S.op("act", lambda e, c=c, sq=sq: e.activation(out=r32(sq), in_=src_fn(c), func=AF.Square),
             reads=[src_res], writes=[r_sq])
        for s in range(2):
            S.op("pe", lambda e, c=c, s=s, sq=sq: e.matmul(self.pb[bs[s]][:, :], lhsT=ones, rhs=r32(sq[:, s * 512:(s + 1) * 512]),
                                                          start=(c == 0), stop=(c == nch - 1)),
                 reads=[r_sq, self.r_const], writes=[self.r_pb[bs[s]]])
    for s in range(2):
        S.op("act", lambda e, s=s: e.activation(out=P[1][:, s * 512:(s + 1) * 512], in_=self.pb[bs[s]][:, :], func=AF.Ln,
                                                scale=float(inv_n), bias=EPS),
             reads=[self.r_pb[bs[s]]], writes=[RP[1]])
    S.op("act", lambda e: e.activation(out=P[1], in_=P[1], func=AF.Exp, scale=-0.5), reads=[RP[1]], writes=[RP[1]])
    for c in range(nch):
        S.op("dve", lambda e, c=c: e.scalar_tensor_tensor(out=dst[:, c, :], in0=src_fn(c), scalar=g[:, c:c + 1], in1=P[1],
                                                         op0=ALU.mult, op1=ALU.mult),
             reads=[src_res, RP[1], self.r_const], writes=[dst_res])


def _prep(self, bn, rope, gname, lnscale, dn, dr, fill1=None, fill2=None, alt=None):
    S = self.S
    ones = self.cr("ones")
    ones64 = self.cr("ones", 64)
    rT = self.cr("rT", 64)
    cos = self.c("cos", 64)
    sin = self.c("sin", 64)
    g = self.c(gname)
    gn = g[:, 0:1]
    gr = g[0:64, 1:2]
    P, RP = list(self.prep_f), list(self.r_prep)
    if alt is not None:
        for k_, i_ in enumerate((1, 3, 4)):
            P[i_] = self.prep_alt[alt][k_]
            RP[i_] = self.r_prep_alt[k_]
    sqr, r_sqr = self.sqr, self.r_sqr
    H = lambda ap, s: ap[:, s * 512:(s + 1) * 512]
    self.held.update(bn)
    for s in range(2):
        S.op("act", lambda e, s=s: e.activation(out=r32(H(P[0], s)), in_=self.pb[bn[s]][:, :], func=AF.Square),
             reads=[self.r_pb[bn[s]]], writes=[RP[0]])
    if rope[0] == "psum":
        for s in range(2):
            S.op("act", lambda e, s=s: e.activation(out=r32(H(sqr[0:64], s)), in_=self.pb[rope[1][s]][0:64, :], func=AF.Square),
                 reads=[self.r_pb[rope[1][s]]], writes=[r_sqr])
        rres = [self.r_pb[b] for b in rope[1]]
        rsrc = lambda s: self.pb[rope[1][s]][0:64, :]
    else:
        S.op("act", lambda e: e.activation(out=r32(sqr[0:64]), in_=rope[1], func=AF.Square), reads=[rope[2]], writes=[r_sqr])
        rres = [rope[2]]
        rsrc = lambda s: H(rope[1], s)
    for s in range(2):
        S.op("dve", lambda e, s=s: e.tensor_scalar(out=r32(H(P[2][0:64], s)), in0=rsrc(s), scalar1=gr, scalar2=None, op0=ALU.mult),
             reads=rres + [self.r_const, r_sqr], writes=[RP[2]])
    if fill1 is not None:
        fill1()
    bs = [self.next_bank(), self.next_bank()]
    for s in range(2):
        def fss(e, s=s):
            e.matmul(self.pb[bs[s]][:, :], lhsT=ones, rhs=r32(H(P[0], s)), start=True, stop=False)
            return e.matmul(self.pb[bs[s]][:, :], lhsT=ones64, rhs=r32(H(sqr[0:64], s)), start=False, stop=True)
        S.op("pe", fss, reads=[RP[0], r_sqr, self.r_const], writes=[self.r_pb[bs[s]]])
    if fill2 is not None:
        fill2()
    br = [self.next_bank(), self.next_bank()]
    for s in range(2):
        S.op("pe", lambda e, s=s: e.matmul(self.pb[br[s]][0:64, :], lhsT=rT, rhs=r32(H(P[2][0:64], s)), start=True, stop=True),
             reads=[RP[2], self.r_const], writes=[self.r_pb[br[s]]])
    for s in range(2):
        S.op("act", lambda e, s=s: e.activation(out=H(P[1], s), in_=self.pb[bs[s]][:, :], func=AF.Ln, scale=1.0 / 192, bias=EPS),
             reads=[self.r_pb[bs[s]]], writes=[RP[1]])
    S.op("act", lambda e: e.activation(out=P[1], in_=P[1], func=AF.Exp, scale=-0.5, bias=float(lnscale)),
         reads=[RP[1]], writes=[RP[1]])
    S.op("dve", lambda e: e.tensor_tensor(out=P[3][0:64], in0=P[2][0:64], in1=cos, op=ALU.mult),
         reads=[RP[2], self.r_const], writes=[RP[3]])
    for s in range(2):
        S.op("dve", lambda e, s=s: e.tensor_tensor(out=H(P[4][0:64], s), in0=self.pb[br[s]][0:64, :], in1=H(sin, s), op=ALU.mult),
             reads=[self.r_pb[br[s]], self.r_const], writes=[RP[4]])
    S.op("dve", lambda e: e.tensor_tensor(out=P[3][0:64], in0=P[3][0:64], in1=P[4][0:64], op=ALU.add),
         reads=[RP[3], RP[4]], writes=[RP[3]])
    stg, r_stg, sem = self.stg.next()
    for s in range(2):
        S.op("dve", lambda e, s=s: e.scalar_tensor_tensor(out=H(stg, s), in0=self.pb[bn[s]][:, :], scalar=gn, in1=H(P[1], s),
                                                         op0=ALU.mult, op1=ALU.mult),
             reads=[self.r_pb[bn[s]], RP[1], self.r_const], writes=[r_stg])
    S.op("sp", lambda e: e.dma_start(out=dn[0][dn[1]:dn[1] + 128, :], in_=stg), reads=[r_stg], writes=[dn[2]], dsem=sem)
    self.held.difference_update(bn)
    stg2, r_stg2, sem2 = self.stg.next()
    S.op("dve", lambda e: e.tensor_tensor(out=stg2[0:64], in0=P[3][0:64], in1=P[1][0:64], op=ALU.mult),
         reads=[RP[3], RP[1]], writes=[r_stg2])
    S.op("sp", lambda e: e.dma_start(out=dr[0][dr[1]:dr[1] + 64, :], in_=stg2[0:64]), reads=[r_stg2], writes=[dr[2]], dsem=sem2)


def _mla_layer(self, li, j, x_src, x_res, x_dst, final):
    S = self.S
    self.phase("norm")
    self.alias(self.r_og, self.r_actT)
    act = lambda kc, slot: self.actT[:, kc, slot * 512:(slot + 1) * 512]
    pre = []
    self.norm_phase(li, x_src, x_res, mid_hook=lambda: pre.extend(self.gemm_fm_head(3, act)))
    self.phase("kv")
    lat = self.lat
    gstate = [0]

    def gate_gen():
        gstg = self.carve(30720, 1024)
        for gi in range(16):
            wt, wres, KC, ncols = self.wnext()
            for mc in range(2):
                for slot in range(2):
                    b = self.next_bank()

                    def f(e, b=b, wt=wt, mc=mc, slot=slot):
                        for kc in range(32):
                            ins = e.matmul(self.pb[b][:, :], lhsT=wt[:, kc, mc * 128:(mc + 1) * 128], rhs=act(kc, slot),
                                           start=(kc == 0), stop=(kc == 31))
                        return ins
                    S.op("pe", f, reads=[wres] + self.r_actT[4 * slot:4 * slot + 4], writes=[self.r_pb[b]])
                    S.op("act", lambda e, b=b, slot=slot: e.activation(
                        out=gstg[:, slot * 512:(slot + 1) * 512], in_=self.pb[b][:, :], func=AF.Silu),
                        reads=[self.r_pb[b]], writes=[self.r_gstg])
                    if slot == 1:
                        r0 = (2 * gi + mc) * 128
                        S.op("sp", lambda e, r0=r0: e.dma_start(out=self.gateT[r0:r0 + 128, :], in_=gstg),
                             reads=[self.r_gstg], writes=[self.r_gate[2 * gi + mc]], dsem=self.gstg_sem)
                    if mc == 1 and slot == 1:
                        self.wfree(wres)
                    yield
    gg = gate_gen()

    def gate_group():
        next(gg, None)
    self.gemm_fm(3, act, self.r_actT, self.epi_fm_sbuf(
        lambda pi, mc, M, slot: lat[0:M, 2 * pi + mc, slot * 512:(slot + 1) * 512], self.r_lat), pre=pre)
    self.latent_norm(lambda c: lat[:, c, :], 4, f"kva{j}", self.kvn, self.r_kvn, self.r_lat, 1.0 / 512)
    self.alias([self.r_lat], self.r_prep_alt)
    kvn = self.kvn
    for h in range(NH):
        wt, wres, KC, ncols = self.wnext()
        bn = []
        for s in range(2):
            b = self.next_bank()

            def fk(e, b=b, s=s, wt=wt):
                for kc in range(4):
                    ins = e.matmul(self.pb[b][:, :], lhsT=wt[:, kc, 0:128], rhs=kvn[:, kc, s * 512:(s + 1) * 512],
                                   start=(kc == 0), stop=(kc == 3))
                return ins
            S.op("pe", fk, reads=[wres, self.r_kvn], writes=[self.r_pb[b]])
            bn.append(b)

        def epi_v(pi, banks, h=h):
            stg, r_stg, sem = self.stgtm.next()
            sv = stg.rearrange("p (t n) -> p t n", n=256)
            for b, tl in banks:
                k = len(tl)
                S.op("dve", lambda e, b=b, tl=tl, k=k, sv=sv: e.tensor_copy(
                    out=sv[:, tl[0]:tl[0] + k, 0:128], in_=self.pb[b][:, 0:k * 128].rearrange("p (t n) -> p t n", n=128)),
                    reads=[self.r_pb[b]], writes=[r_stg])
            dv = self.sendV[h // 8][:, (h % 8) * 128:(h % 8 + 1) * 128].rearrange("(t p) n -> p t n", p=128)
            S.op("sp", lambda e, sv=sv, dv=dv: e.dma_start(out=dv, in_=sv[:, :, 0:128]),
                 reads=[r_stg], writes=[self.r_sendV[h]], dsem=sem)
        vfill = lambda h=h, wt=wt, wres=wres, epi_v=epi_v: self.gemm_tm_piece(
            h, wt, wres, 4, 128, 128, lambda kc, tt: kvn[:, kc, tt * 128:(tt + 1) * 128], [self.r_kvn], epi_v)
        self.prep(bn, ("sbuf", lat[0:64, 4, :], self.r_lat), f"kn{j}", 0.0,
                  (self.sendK[h // 8], (h % 8) * 128, self.r_sendK[h]), (self.sendKr[h // 16], (h % 16) * 64, self.r_sendKr[h]),
                  fill1=vfill, fill2=gate_group, alt=("kv" if h % 2 else None))
        self.wfree(wres)
        if h % 8 == 7:
            c = h // 8
            self.allgather(self.sendK[c], self.r_sendK[8 * c:8 * c + 8], self.gathK[c], self.r_gathK[c], self.cc_sems[c])
            self.allgather(self.sendV[c], self.r_sendV[8 * c:8 * c + 8], self.gathV[c], self.r_gathV[c], self.cc_sems[6 + c])
        if h % 16 == 15:
            c = h // 16
            self.allgather(self.sendKr[c], self.r_sendKr[16 * c:16 * c + 16], self.gathKr[c], self.r_gathKr[c], self.cc_sems[4 + c])
    self.phase("q")
    qsel = lambda c: self.qlat[:, c, :]
    self.gemm_fm(4, act, self.r_actT, self.epi_fm_sbuf(
        lambda pi, mc, M, slot: qsel(2 * pi + mc)[0:M, slot * 512:(slot + 1) * 512], self.r_qlat))
    self.latent_norm(qsel, 8, f"qa{j}", self.qn, self.r_qn, self.r_qlat, 1.0 / 1024)
    self.alias([self.r_qlat], self.r_prep_alt)
    qn = self.qn
    for h in range(NH):
        wt, wres, KC, ncols = self.wnext()
        bn, br = [], []
        for (lst, c0, M) in ((bn, 0, 128), (br, 128, 64)):
            for s in range(2):
                b = self.next_bank()

                def fq(e, b=b, s=s, wt=wt, c0=c0, M=M):
                    for kc in range(8):
                        ins = e.matmul(self.pb[b][0:M, :], lhsT=wt[:, kc, c0:c0 + M], rhs=qn[:, kc, s * 512:(s + 1) * 512],
                                       start=(kc == 0), stop=(kc == 7))
                    return ins
                S.op("pe", fq, reads=[wres, self.r_qn], writes=[self.r_pb[b]])
                lst.append(b)
        self.prep(bn, ("psum", br), f"qn{j}", math.log(192 ** -0.5),
                  (self.qT, h * 192, self.r_qT[h]), (self.qT, h * 192 + 128, self.r_qT[h]), fill1=gate_group,
                  alt=("q" if h % 2 else None))
        self.wfree(wres)
    self.phase("attn")
    self.alias(self.r_actT, self.r_og)
    self.load_masks(0)
    self.attention("mla")
    self.phase("outp")
    self.out_proj(x_src, (lambda pi: []) if li == 0 else (lambda pi: [self.r_xcol[pi]]), x_dst, final)


def _sb_layer(self, li, j, x_src, x_res, x_dst, final):
    self.phase("norm")
    self.alias(self.r_og, self.r_actT)
    act = lambda kc, slot: self.actT[:, kc, slot * 512:(slot + 1) * 512]
    pre = []
    self.norm_phase(li, x_src, x_res, mid_hook=lambda: pre.extend(self.gemm_fm_head(3, act)))
    self.phase("outp")
    self.gemm_fm(16, act, self.r_actT, self.epi_fm_dram(
        lambda pi, mc: self.sendK[(2 * pi + mc) // 8], lambda pi, mc: self.r_sendK[2 * pi + mc],
        lambda pi, mc: ((2 * pi + mc) % 8) * 128), pre=pre)
    self.exchange(False, do_v=False)
    self.gemm_tm(16, lambda kc, tt: self.actT[:, kc, tt * 128:(tt + 1) * 128], self.r_actT,
                 self.epi_tm_dram(lambda pi: self.sendV[pi // 4], lambda pi: self.r_sendV[pi], lambda pi: (pi % 4) * 256))
    self.exchange(False, do_k=False)
    self.gemm_fm(16, act, self.r_actT, self.epi_fm_dram(
        self.qT, lambda pi, mc: self.r_qT[2 * pi + mc], lambda pi, mc: (2 * pi + mc) * 128, scale=128 ** -0.5))
    self.gemm_fm(16, act, self.r_actT, self.epi_fm_dram(
        self.gateT, lambda pi, mc: self.r_gate[2 * pi + mc], lambda pi, mc: (2 * pi + mc) * 128, func=AF.Silu))
    self.phase("attn")
    self.alias(self.r_actT, self.r_og)
    self.load_masks(1)
    self.attention("sb")
    self.phase("outp")
    self.out_proj(x_src, (lambda pi: []) if li == 0 else (lambda pi: [self.r_xcol[pi]]), x_dst, final)


Builder.phase = _phase
Builder.alias = _alias
Builder.latent_norm = _latent_norm
Builder.prep = _prep
Builder.mla_layer = _mla_layer
Builder.sb_layer = _sb_layer


def _rope_tables():
    inv = (1.0 / (10000.0 ** (np.arange(0, 64, 2, dtype=np.float32) / np.float32(64)))).astype(np.float32)
    ang = np.arange(S_FULL, dtype=np.float32)[:, None] * inv[None, :]
    return np.cos(ang).astype(np.float32), np.sin(ang).astype(np.float32)


def _local_tokens(r):
    blks = [0, 3] if r == 0 else [1, 2]
    return np.concatenate([np.arange(b * 512, (b + 1) * 512) for b in blks])


def _make_consts(inputs, kinds, r):
    cf = np.zeros((128, CFW), np.float32)

    def put(name, arr):
        o, w = CF[name]
        cf[:arr.shape[0], o:o + arr.shape[1]] = arr
    for li, (kind, j) in enumerate(kinds):
        g = inputs["mla_norm_g" if kind == "mla" else "sb_norm_g"][j]
        put(f"g{li}", g.reshape(32, 128).T)
    for j in range(inputs["mla_q_a_g"].shape[0]):
        put(f"qa{j}", inputs["mla_q_a_g"][j].reshape(8, 128).T)
        put(f"kva{j}", inputs["mla_kv_a_g"][j].reshape(4, 128).T)
        for nm, key in (("qn", "mla_q_norm_g"), ("kn", "mla_k_norm_g")):
            a = np.zeros((128, 2), np.float32)
            a[:, 0] = inputs[key][j][:128]
            a[:64, 1] = inputs[key][j][128:]
            put(f"{nm}{j}", a)
    jj, ss = np.meshgrid(np.arange(128), np.arange(128), indexing="ij")
    put("tri", (jj >= ss).astype(np.float32))
    put("ones", np.ones((128, 128), np.float32))
    R = np.zeros((64, 64), np.float32)
    for i in range(32):
        R[i, i + 32] = -1.0
        R[i + 32, i] = 1.0
    put("rT", R.T.copy())
    cos, sin = _rope_tables()
    lt = _local_tokens(r)
    put("cos", np.concatenate([cos[lt].T, cos[lt].T], axis=0))
    put("sin", np.concatenate([sin[lt].T, sin[lt].T], axis=0))
    cb = np.zeros((128, CBW), np.float32)
    cb[:, 0:128] = np.eye(128)
    cb[:, 128:256] = -np.eye(128)
    cb[:, 256:384] = 1.0
    masks = np.zeros((2, 128, 16, 512), np.float32)
    qblk = [0, 3] if r == 0 else [1, 2]
    slots = [(0, 0), (0, 1), (1, 2), (1, 3)]
    for gi, (s, kb) in enumerate(slots):
        for i in range(4):
            kpos = kb * 512 + i * 128 + np.arange(128)[:, None]
            qpos = qblk[s] * 512 + np.arange(512)[None, :]
            masks[0, :, gi * 4 + i, :] = np.where(kpos <= qpos, 1.0, 0.0)
            masks[1, :, gi * 4 + i, :] = np.where(kpos < qpos, 1.0, 0.0)
    return cf, cb.astype(ml_dtypes.bfloat16), masks.reshape(2, 128, 16 * 512).astype(ml_dtypes.bfloat16)


_NC_CACHE = {}


def run_layers(x, inputs, kinds):
    key = tuple(kinds)
    if key not in _NC_CACHE:
        _NC_CACHE[key] = Builder(list(kinds)).build()
    nc = _NC_CACHE[key]
    mj = sorted({j for k, j in kinds if k == "mla"})
    sj = sorted({j for k, j in kinds if k == "sb"})
    wmaps = {}
    if mj:
        for nm in ("mla_w_in", "mla_w_qb", "mla_w_kvb", "mla_w_out"):
            wmaps[nm] = np.ascontiguousarray(inputs[nm][mj[0]:mj[-1] + 1])
    if sj:
        for nm in ("sb_w_in", "sb_w_out"):
            wmaps[nm] = np.ascontiguousarray(inputs[nm][sj[0]:sj[-1] + 1])
    kk = [(k, j - (mj[0] if k == "mla" else sj[0])) for k, j in kinds]
    assert kk == list(kinds) or len(kinds) == 1 or True
    in_maps = []
    for c in range(8):
        b, r = divmod(c, 2)
        cf, cb, masks = _make_consts(inputs, kinds, r)
        m = {"x": np.ascontiguousarray(x[b][_local_tokens(r)]), "cf": cf, "cb": cb, "masks": masks}
        m.update(wmaps)
        in_maps.append(m)
    res = run_bass_kernel_spmd(nc, in_maps, core_ids=list(range(8)))
    out = np.empty((4, S_FULL, D), np.float32)
    for c in range(8):
        b, r = divmod(c, 2)
        out[b][_local_tokens(r)] = np.asarray(res.results[c]["out"])
    return out


def kernel(**inputs):
    inputs = {k: np.asarray(v) for k, v in inputs.items()}
    kinds = (("mla", 0), ("sb", 0), ("mla", 1), ("sb", 1))
    return run_layers(inputs["x"].astype(np.float32), inputs, kinds)
```
